# Optimizing a Trainium2 kernel written in Bass

```python
import numpy as np
import jax, jax.numpy as jnp
from jax import lax

D_MODEL = 1024
BATCH = 8
SEQ = 2048
DEPTH = 2

GRID_W = 64
CTX_LEN = 256
N_MIXERS = 2
N_HEADS = 16
HEAD_DIM = D_MODEL // N_HEADS
NA_KH = 8
NA_KW = 16
NA_QCB = NA_KW
NA_KCB = 2 * NA_KW
WA_KV_HEADS = 4
WA_GROUPS = N_HEADS // WA_KV_HEADS
WA_WINDOW = 128
WA_BLOCK = 128
D_FF = 2816
FFN_RES = 0.5
ROPE_BASE = 10000.0
N_MOD = 9
ALPHA = (2 * DEPTH) ** 0.25
BETA = (8 * DEPTH) ** -0.25
LN_EPS = 1e-5
NEG_INF = -1e30
N_NA_LAYERS = (DEPTH + 1) // 2
N_WA_LAYERS = DEPTH // 2

kernel_name = "hybrid_natten_swa_macaron_deepnorm"


def layer_norm(x, g, b):
    xf = x.astype(jnp.float32)
    mu = jnp.mean(xf, axis=-1, keepdims=True)
    var = jnp.mean(jnp.square(xf - mu), axis=-1, keepdims=True)
    return ((xf - mu) * lax.rsqrt(var + LN_EPS)).astype(x.dtype) * g + b


def modulate(h, shift, scale):
    return h * (1.0 + scale) + shift


def swiglu(u, w_in, w_out):
    a, v = jnp.split(u @ w_in, 2, axis=-1)
    return (jax.nn.silu(a) * v) @ w_out


def ffn_sublayer(h, shift, scale, gate, w_in, w_out, g, b):
    y = swiglu(modulate(h, shift, scale), w_in, w_out)
    return layer_norm(ALPHA * h + FFN_RES * gate * y, g, b)


def axial_rope(x):
    S = x.shape[1]
    t = jnp.arange(S, dtype=jnp.int32)
    rows = (t // GRID_W).astype(jnp.float32)
    cols = (t % GRID_W).astype(jnp.float32)
    n = HEAD_DIM // 4
    inv_freq = ROPE_BASE ** (-jnp.arange(n, dtype=jnp.float32) / n)
    half = HEAD_DIM // 2

    def rot(xa, pos):
        ang = pos[:, None] * inv_freq[None, :]
        cos = jnp.cos(ang)[None, :, None, :].astype(x.dtype)
        sin = jnp.sin(ang)[None, :, None, :].astype(x.dtype)
        x1, x2 = xa[..., :n], xa[..., n:]
        return jnp.concatenate([x1 * cos - x2 * sin, x2 * cos + x1 * sin], axis=-1)

    return jnp.concatenate([rot(x[..., :half], rows), rot(x[..., half:], cols)], axis=-1)


def ctx_attention(qc, kc, vc, sink):
    B, C, H, d = qc.shape
    hk = kc.shape[2]
    g = H // hk
    qg = qc.reshape(B, C, hk, g, d)
    s = jnp.einsum('bqhgd,bkhd->bhgqk', qg, kc).astype(jnp.float32)
    if sink is not None:
        sk = jnp.broadcast_to(sink.astype(jnp.float32).reshape(1, hk, g, 1, 1), s.shape[:-1] + (1,))
        s = jnp.concatenate([s, sk], axis=-1)
    p = jax.nn.softmax(s, axis=-1)[..., :C].astype(vc.dtype)
    o = jnp.einsum('bhgqk,bkhd->bqhgd', p, vc)
    return o.reshape(B, C, H * d)


def neighborhood_attention(u_lat, u_ctx, w_qkv, w_o, rpb, with_ctx_out):
    B, S, _ = u_lat.shape
    C = u_ctx.shape[1]
    rows = S // GRID_W
    kh = min(NA_KH, rows)
    scale = HEAD_DIM ** -0.5
    qkv = (u_lat @ w_qkv).reshape(B, rows, GRID_W, 3, N_HEADS, HEAD_DIM)
    q = qkv[:, :, :, 0] * scale
    k = qkv[:, :, :, 1]
    v = qkv[:, :, :, 2]
    kvc = (u_ctx @ w_qkv[:, D_MODEL:]).reshape(B, C, 2, N_HEADS, HEAD_DIM)
    kc, vc = kvc[:, :, 0], kvc[:, :, 1]

    ncb = GRID_W // NA_QCB
    qcol = np.arange(GRID_W).reshape(ncb, NA_QCB)
    blk_start = np.clip(np.arange(ncb) * NA_QCB - NA_KW // 2, 0, GRID_W - NA_KCB)
    kcol = blk_start[:, None] + np.arange(NA_KCB)[None, :]
    win_start = np.clip(qcol - NA_KW // 2, 0, GRID_W - NA_KW)
    col_valid = (kcol[:, None, :] >= win_start[..., None]) & (kcol[:, None, :] < win_start[..., None] + NA_KW)
    col_off = np.clip(kcol[:, None, :] - qcol[..., None], -(NA_KW - 1), NA_KW - 1) + NA_KW - 1
    bias_cols = jnp.where(col_valid, rpb[:, :, col_off].astype(jnp.float32), NEG_INF)
    n_win = kh * NA_KCB

    def row_block(r):
        rs = jnp.clip(r - kh // 2, 0, rows - kh)
        q_r = lax.dynamic_index_in_dim(q, r, axis=1, keepdims=False).reshape(B, ncb, NA_QCB, N_HEADS, HEAD_DIM)
        k_r = lax.dynamic_slice_in_dim(k, rs, kh, axis=1)[:, :, kcol]
        v_r = lax.dynamic_slice_in_dim(v, rs, kh, axis=1)[:, :, kcol]
        row_idx = rs + jnp.arange(kh) - r + NA_KH - 1
        bias = jnp.take(bias_cols, row_idx, axis=1).transpose(0, 2, 3, 1, 4)[None]
        s_win = jnp.einsum('bcqhd,bicjhd->bhcqij', q_r, k_r).astype(jnp.float32) + bias
        s_win = s_win.reshape(B, N_HEADS, ncb, NA_QCB, n_win)
        s_ctx = jnp.einsum('bcqhd,bkhd->bhcqk', q_r, kc).astype(jnp.float32)
        p = jax.nn.softmax(jnp.concatenate([s_win, s_ctx], axis=-1), axis=-1).astype(v.dtype)
        p_win = p[..., :n_win].reshape(B, N_HEADS, ncb, NA_QCB, kh, NA_KCB)
        p_ctx = p[..., n_win:]
        o = (jnp.einsum('bhcqij,bicjhd->bcqhd', p_win, v_r)
             + jnp.einsum('bhcqk,bkhd->bcqhd', p_ctx, vc))
        return o.reshape(B, GRID_W, D_MODEL)

    out = lax.map(row_block, jnp.arange(rows, dtype=jnp.int32))
    y_lat = out.transpose(1, 0, 2, 3).reshape(B, S, D_MODEL) @ w_o
    y_ctx = None
    if with_ctx_out:
        qc = (u_ctx @ w_qkv[:, :D_MODEL]).reshape(B, C, N_HEADS, HEAD_DIM) * scale
        y_ctx = ctx_attention(qc, kc, vc, None) @ w_o
    return y_lat, y_ctx


def window_gqa(u_lat, u_ctx, w_qkv, w_o, sinks, with_ctx_out):
    B, S, _ = u_lat.shape
    C = u_ctx.shape[1]
    dq = N_HEADS * HEAD_DIM
    dkv = WA_KV_HEADS * HEAD_DIM
    scale = HEAD_DIM ** -0.5
    qkv = u_lat @ w_qkv
    q = axial_rope(qkv[..., :dq].reshape(B, S, N_HEADS, HEAD_DIM)) * scale
    k = axial_rope(qkv[..., dq:dq + dkv].reshape(B, S, WA_KV_HEADS, HEAD_DIM))
    v = qkv[..., dq + dkv:].reshape(B, S, WA_KV_HEADS, HEAD_DIM)
    kvc = u_ctx @ w_qkv[:, dq:]
    kc = kvc[..., :dkv].reshape(B, C, WA_KV_HEADS, HEAD_DIM)
    vc = kvc[..., dkv:].reshape(B, C, WA_KV_HEADS, HEAD_DIM)

    nb = S // WA_BLOCK
    pad = ((0, 0), (WA_BLOCK, WA_BLOCK), (0, 0), (0, 0))
    k_pad = jnp.pad(k, pad)
    v_pad = jnp.pad(v, pad)
    q_blk = q.reshape(B, nb, WA_BLOCK, WA_KV_HEADS, WA_GROUPS, HEAD_DIM)
    sink = sinks.astype(jnp.float32).reshape(1, WA_KV_HEADS, WA_GROUPS, 1, 1)
    qi = jnp.arange(WA_BLOCK, dtype=jnp.int32)
    kj = jnp.arange(3 * WA_BLOCK, dtype=jnp.int32)
    n_win = 3 * WA_BLOCK

    def band_block(b):
        qb = lax.dynamic_index_in_dim(q_blk, b, axis=1, keepdims=False)
        kb = lax.dynamic_slice_in_dim(k_pad, b * WA_BLOCK, n_win, axis=1)
        vb = lax.dynamic_slice_in_dim(v_pad, b * WA_BLOCK, n_win, axis=1)
        pos_q = b * WA_BLOCK + qi
        pos_k = (b - 1) * WA_BLOCK + kj
        valid = ((jnp.abs(pos_q[:, None] - pos_k[None, :]) <= WA_WINDOW)
                 & (pos_k >= 0)[None, :] & (pos_k < S)[None, :])
        s_win = jnp.where(valid, jnp.einsum('bqhgd,bkhd->bhgqk', qb, kb).astype(jnp.float32), NEG_INF)
        s_ctx = jnp.einsum('bqhgd,bkhd->bhgqk', qb, kc).astype(jnp.float32)
        s_sink = jnp.broadcast_to(sink, s_win.shape[:-1] + (1,))
        p = jax.nn.softmax(jnp.concatenate([s_win, s_ctx, s_sink], axis=-1), axis=-1).astype(v.dtype)
        o = (jnp.einsum('bhgqk,bkhd->bqhgd', p[..., :n_win], vb)
             + jnp.einsum('bhgqk,bkhd->bqhgd', p[..., n_win:n_win + C], vc))
        return o.reshape(B, WA_BLOCK, D_MODEL)

    out = lax.map(band_block, jnp.arange(nb, dtype=jnp.int32))
    y_lat = out.transpose(1, 0, 2, 3).reshape(B, S, D_MODEL) @ w_o
    y_ctx = None
    if with_ctx_out:
        qc = (u_ctx @ w_qkv[:, :dq]).reshape(B, C, N_HEADS, HEAD_DIM) * scale
        y_ctx = ctx_attention(qc, kc, vc, sinks) @ w_o
    return y_lat, y_ctx


def setup_inputs(seed: int = 0) -> dict:
    key = jax.random.key(seed)
    ks = jax.random.split(key, 16)
    f32 = jnp.float32
    dq = N_HEADS * HEAD_DIM
    dkv = WA_KV_HEADS * HEAD_DIM

    def nrm(k, shape, s):
        return jax.random.normal(k, shape, f32) * s

    return {
        "x": nrm(ks[0], (BATCH, SEQ, D_MODEL), 1.0),
        "c": nrm(ks[1], (BATCH, D_MODEL), 1.0),
        "ctx": nrm(ks[2], (BATCH, CTX_LEN, D_MODEL), 1.0),
        "c_ctx": nrm(ks[3], (D_MODEL,), 1.0),
        "w_mod": nrm(ks[4], (DEPTH, D_MODEL, N_MOD * D_MODEL), 0.5 * D_MODEL ** -0.5),
        "b_mod": nrm(ks[5], (DEPTH, N_MOD * D_MODEL), 0.02),
        "ln_g": 1.0 + nrm(ks[6], (DEPTH, 3, D_MODEL), 0.02),
        "ln_b": nrm(ks[7], (DEPTH, 3, D_MODEL), 0.02),
        "ffn_w_in": nrm(ks[8], (DEPTH, 2, D_MODEL, 2 * D_FF), D_MODEL ** -0.5),
        "ffn_w_out": nrm(ks[9], (DEPTH, 2, D_FF, D_MODEL), BETA * D_FF ** -0.5),
        "na_w_qkv": nrm(ks[10], (N_NA_LAYERS, D_MODEL, 3 * dq), D_MODEL ** -0.5),
        "na_w_o": nrm(ks[11], (N_NA_LAYERS, dq, D_MODEL), BETA * dq ** -0.5),
        "na_rpb": nrm(ks[12], (N_NA_LAYERS, N_HEADS, 2 * NA_KH - 1, 2 * NA_KW - 1), 0.1),
        "wa_w_qkv": nrm(ks[13], (N_WA_LAYERS, D_MODEL, dq + 2 * dkv), D_MODEL ** -0.5),
        "wa_w_o": nrm(ks[14], (N_WA_LAYERS, dq, D_MODEL), BETA * dq ** -0.5),
        "wa_sinks": nrm(ks[15], (N_WA_LAYERS, N_HEADS), 0.5),
    }


def reference(x, c, ctx, c_ctx, w_mod, b_mod, ln_g, ln_b, ffn_w_in, ffn_w_out,
              na_w_qkv, na_w_o, na_rpb, wa_w_qkv, wa_w_o, wa_sinks):
    h_lat, h_ctx = x, ctx
    silu_c = jax.nn.silu(c)
    silu_cc = jax.nn.silu(c_ctx)
    for i in range(DEPTH):
        ctx_needed = i < DEPTH - 1
        m_lat = jnp.split((silu_c @ w_mod[i] + b_mod[i])[:, None, :], N_MOD, axis=-1)
        m_ctx = jnp.split((silu_cc @ w_mod[i] + b_mod[i])[None, None, :], N_MOD, axis=-1)

        h_lat = ffn_sublayer(h_lat, m_lat[0], m_lat[1], m_lat[2], ffn_w_in[i, 0], ffn_w_out[i, 0], ln_g[i, 0], ln_b[i, 0])
        h_ctx = ffn_sublayer(h_ctx, m_ctx[0], m_ctx[1], m_ctx[2], ffn_w_in[i, 0], ffn_w_out[i, 0], ln_g[i, 0], ln_b[i, 0])

        u_lat = modulate(h_lat, m_lat[3], m_lat[4])
        u_ctx = modulate(h_ctx, m_ctx[3], m_ctx[4])
        j = i // N_MIXERS
        if i % N_MIXERS == 0:
            y_lat, y_ctx = neighborhood_attention(u_lat, u_ctx, na_w_qkv[j], na_w_o[j], na_rpb[j], ctx_needed)
        else:
            y_lat, y_ctx = window_gqa(u_lat, u_ctx, wa_w_qkv[j], wa_w_o[j], wa_sinks[j], ctx_needed)
        h_lat = layer_norm(ALPHA * h_lat + m_lat[5] * y_lat, ln_g[i, 1], ln_b[i, 1])

        h_lat = ffn_sublayer(h_lat, m_lat[6], m_lat[7], m_lat[8], ffn_w_in[i, 1], ffn_w_out[i, 1], ln_g[i, 2], ln_b[i, 2])
        if ctx_needed:
            h_ctx = layer_norm(ALPHA * h_ctx + m_ctx[5] * y_ctx, ln_g[i, 1], ln_b[i, 1])
            h_ctx = ffn_sublayer(h_ctx, m_ctx[6], m_ctx[7], m_ctx[8], ffn_w_in[i, 1], ffn_w_out[i, 1], ln_g[i, 2], ln_b[i, 2])
    return h_lat
```

```python
import numpy as np
from contextlib import ExitStack
import concourse.bass as bass
import concourse.mybir as mybir
from concourse.bass_utils import run_bass_kernel_spmd

F32 = mybir.dt.float32
BF16 = mybir.dt.bfloat16
AF = mybir.ActivationFunctionType
ALU = mybir.AluOpType

D = 1024
KC = 8
S = 2048
CT = 256
T = S + CT
FF = 2816
FC = 22
NH = 16
HD = 64
GW = 64
ROWS = 32
DEPTH = 2
ALPHA = (2 * DEPTH) ** 0.25
LN_EPS = 1e-5
NEG = -1e30
SLOT = 4096
NSLOT = 3
LOOKAHEAD = 2
PARTS = [(0, 4), (4, 4), (8, 4), (12, 4), (16, 3), (19, 3)]
MODW = 512
NMOD = 9 * D // MODW
TILES = [(0, 512), (512, 512), (1024, 512), (1536, 512), (2048, 256)]


class Res:
    __slots__ = ("w", "r")

    def __init__(self):
        self.w = None
        self.r = {}


class Sched:
    def __init__(self):
        self.ops = {k: [] for k in ("pe", "act", "dve", "pool", "sp")}
        self.cnt = {k: 0 for k in self.ops}
        self.seen = {k: {} for k in self.ops}
        self.sems = {}
        self.step = {k: 1 for k in self.ops}

    def add_dma_sem(self, name):
        self.cnt[name] = 0
        self.step[name] = 16

    def _deps(self, e, rd, wr):
        deps = {}

        def need(x):
            if x is None:
                return
            en, c = x
            if c > deps.get(en, 0):
                deps[en] = c

        for r in rd:
            need(r.w)
        for w in wr:
            need(w.w)
            for en, c in w.r.items():
                need((en, c))
        waits = []
        seen = self.seen[e]
        for en, c in deps.items():
            if en == e and e == "pe":
                continue
            if seen.get(en, 0) >= c:
                continue
            seen[en] = c
            waits.append((en, c))
        return waits

    def op(self, e, fn, rd=(), wr=()):
        waits = self._deps(e, rd, wr)
        self.cnt[e] += 1
        c = self.cnt[e]
        self.ops[e].append((waits, fn, e, c))
        for r in rd:
            if r.r.get(e, 0) < c:
                r.r[e] = c
        for w in wr:
            w.w = (e, c)
            w.r = {}

    def dma(self, q, fn, dsem, rd=(), wr=()):
        waits = self._deps(q, rd, wr)
        self.cnt[dsem] += 16
        c = self.cnt[dsem]
        self.ops[q].append((waits, fn, dsem, c))
        for r in rd:
            if r.r.get(dsem, 0) < c:
                r.r[dsem] = c
        for w in wr:
            w.w = (dsem, c)
            w.r = {}

    def replay(self, name, eng, final_waits=()):
        for waits, fn, sname, c in self.ops[name]:
            for en, v in waits:
                eng.wait_ge(self.sems[en], v)
            inst = fn(eng)
            inst.then_inc(self.sems[sname], self.step[sname])
        for en, v in final_waits:
            eng.wait_ge(self.sems[en], v)


def build_program(stop=None):
    nc = bass.Bass("TRN2", target_bir_lowering=False)

    def din(name, shape):
        return nc.dram_tensor(name, list(shape), F32, kind="ExternalInput").ap()

    xT = din("xT", [D, S])
    ctxT = din("ctxT", [D, CT])
    cc = din("cc", [128, KC, 2])
    w_mod = din("w_mod", [DEPTH, D, 9 * D])
    bmod = din("bmod", [128, DEPTH, 72])
    lng = din("lng", [128, DEPTH, 3, KC])
    lnb = din("lnb", [128, DEPTH, 3, KC])
    w_in = din("ffn_w_in", [DEPTH, 2, D, 2 * FF])
    w_out = din("ffn_w_out", [DEPTH, 2, FF, D])
    na_qkv = din("na_w_qkv", [D, 3 * D])
    na_o = din("na_w_o", [D, D])
    na_tb = din("na_tb", [128, 8, 15 * 64])
    wa_qkv = din("wa_w_qkv", [D, D + 512])
    wa_perm = din("wa_w_perm", [D, D + 256])
    wa_o = din("wa_w_o", [D, D])
    wa_sk = din("wa_sk", [128, 8])
    wa_mask = din("wa_mask", [128, 320])
    rope_cs = din("rope_cs", [128, 2, S])
    consts = din("consts", [128, 3, 128])
    n_out_tok = T if stop is not None else S
    outT = nc.dram_tensor("outT", [D, n_out_tok], F32, kind="ExternalOutput").ap()

    es = ExitStack()
    with es:
        def sb(name, shape, dt):
            return es.enter_context(nc.sbuf_tensor(name, list(shape), dt))

        hT = sb("hT", [128, KC, T], F32)
        uT = sb("uT", [128, KC, T], BF16)
        ring = sb("ring", [128, NSLOT, SLOT], BF16)
        ARENA = 4 * T + 2 * 1024
        ARENA_N = 25152
        arena = sb("arena", [128, ARENA_N], BF16)
        zb = sb("zb", [128, 2, 512], BF16)
        zq = sb("zq", [128, 2, 512], BF16)
        mean_sb2 = sb("mean_sb", [128, 2, 512], F32)
        tmpA2 = sb("tmpA", [128, 2, 512], F32)
        cst = sb("cst", [128, 3, 128], BF16)
        ccs = sb("ccs", [128, KC, 2], F32)
        scb = sb("scb", [128, KC, 2], BF16)
        bmod_sb = sb("bmod_sb", [128, DEPTH, 72], F32)
        raw = sb("raw", [128, DEPTH, 72, 2], F32)
        Acol = sb("Acol", [128, DEPTH, 3, KC, 2], F32)
        Gcol = sb("Gcol", [128, DEPTH, 3, KC, 2], F32)
        lng_sb = sb("lng_sb", [128, DEPTH, 3, KC], F32)
        lnb_sb = sb("lnb_sb", [128, DEPTH, 3, KC], F32)
        sk_sb = sb("sk_sb", [128, 8], F32)
        es_sb = sb("es_sb", [128, 8], F32)
        zero_col = sb("zero_col", [128, 1], F32)
        banks = [es.enter_context(nc.psum_tensor(f"bank{i}", [128, 512], F32)) for i in range(8)]

        ident = cst[:, 0, :]
        onesbd = cst[:, 1, :]
        onesd = cst[:, 2, :]

        def shaped(flat, shape):
            if len(shape) == 1:
                return flat
            names = " ".join(f"a{i}" for i in range(len(shape)))
            kw = {f"a{i}": int(s_) for i, s_ in enumerate(shape)}
            return flat.rearrange(f"p ({names}) -> p {names}", **kw)

        def aview(off, shape, dt=BF16):
            n = int(np.prod(shape))
            if dt == F32:
                assert off % 2 == 0
                flat = arena[:, off:off + 2 * n].bitcast(F32)
            else:
                flat = arena[:, off:off + n]
            return shaped(flat, shape)

        def ring_view(si, off, shape):
            n = int(np.prod(shape))
            return shaped(ring[:, si, off:off + n], shape)

        gT = aview(0, [4, T])
        sa = aview(4 * T, [2, 512], F32)
        o = 0
        qT = aview(o, [T]); o += T
        OT = aview(o, [2, T]); o += 2 * T
        PT = aview(o, [3, 512]); o += 3 * 512
        rsb = aview(o, [1, 512], F32); o += 1024
        t1 = aview(o, [1, 512], F32); o += 1024
        t2 = aview(o, [1, 512], F32); o += 1024
        ropeC = aview(o, [S]); o += S
        ropeS = aview(o, [S]); o += S
        maskw = aview(o, [320]); o += 320
        assert o >= ARENA
        KBD = aview(o, [36, 128]); o += 36 * 128
        VBD = aview(o, [36, 128]); o += 36 * 128
        assert o <= ARENA_N, o
        assert ARENA <= ARENA_N

        SEMS = {}
        for k in ("pe", "act", "dve", "pool", "sp"):
            SEMS[k] = es.enter_context(nc.semaphore(f"s_{k}"))
        dma_names = [f"slot{i}" for i in range(NSLOT)] + ["init_sp", "init_pool", "aux", "outd"] + [f"init_x{t}" for t in range(5)]
        for k in dma_names:
            SEMS[k] = es.enter_context(nc.semaphore(f"s_{k}"))

        def wrows(ap2d):
            return ap2d.rearrange("(c p) f -> p c f", p=128)

        def in_groups(part):
            j0, n = part
            gs = []
            j = j0
            while j < j0 + n:
                g = min(2, j0 + n - j)
                gs.append((j, g))
                j += g
            return gs

        def emit(sc, plan):
            record = plan is None
            rec = []
            for k in dma_names:
                sc.add_dma_sem(k)

            R_h = [[Res() for _ in TILES] for _ in range(KC)]
            R_u = [Res() for _ in TILES]
            R_g = [[Res() for _ in TILES] for _ in range(4)]
            R_slot = [Res() for _ in range(NSLOT)]
            R_bank = [Res() for _ in range(8)]
            R_sa = [Res(), Res()]
            R_zb = [Res(), Res()]
            R_zq = [Res(), Res()]
            R_mean2 = [Res(), Res()]
            R_tmpA2 = [Res(), Res()]
            R_cst = Res()
            R_small = Res()
            R_scb = Res()
            R_raw = [Res() for _ in range(DEPTH)]
            R_cols = [[Res() for _ in range(3)] for _ in range(DEPTH)]
            R_colA = [[Res() for _ in range(3)] for _ in range(DEPTH)]
            R_ln = Res()
            R_es = Res()
            R_KBD = [Res() for _ in range(5)]
            R_VBD = [Res() for _ in range(9)]
            R_q = [Res() for _ in TILES]
            R_OT = [[Res() for _ in TILES] for _ in range(2)]
            R_PT = [Res() for _ in range(3)]
            R_rsb = [Res()]
            R_t1 = [Res()]
            R_t2 = [Res()]
            R_rope = Res()
            R_rope2 = Res()
            R_mask = Res()

            def h_res(tiles, cs=range(KC)):
                return [R_h[c][t] for c in cs for t in tiles]

            def ffn_arena():
                return [r for row in R_g for r in row] + R_sa

            def att_res():
                return (R_q + R_OT[0] + R_OT[1] + R_PT + R_rsb + R_t1 + R_t2 + [R_rope, R_rope2, R_mask])

            wstate = {"issued": 0, "next": 0}

            def w_issue_upto(n):
                while wstate["issued"] < min(n, len(plan)):
                    i = wstate["issued"]
                    si = i % NSLOT
                    key, dmas_fn = plan[i]
                    for dst, src in dmas_fn(si):
                        sc.dma("pool", (lambda e, dst=dst, src=src: e.dma_start(out=dst, in_=src)),
                               f"slot{si}", wr=[R_slot[si]])
                    wstate["issued"] += 1

            def w_get(key, dmas_fn):
                i = wstate["next"]
                wstate["next"] += 1
                if record:
                    rec.append((key, dmas_fn))
                    return i % NSLOT
                assert plan[i][0] == key, (plan[i][0], key)
                w_issue_upto(i + 1 + LOOKAHEAD)
                return i % NSLOT

            def mod_dmas(l, i):
                src = wrows(w_mod[l][:, i * MODW:(i + 1) * MODW])
                return lambda si: [(ring_view(si, 0, [KC, MODW]), src)]

            def win_dmas(l, w, j, g):
                sa_ = wrows(w_in[l, w][:, j * 128:(j + g) * 128])
                sv_ = wrows(w_in[l, w][:, FF + j * 128:FF + (j + g) * 128])
                return lambda si: [(ring_view(si, 0, [KC, 2, g * 128])[:, :, 0, :], sa_),
                                   (ring_view(si, 0, [KC, 2, g * 128])[:, :, 1, :], sv_)]

            def wout_dmas(l, w, j0, n):
                so = w_out[l, w][j0 * 128:(j0 + n) * 128, :].rearrange("(j p) d -> p j d", p=128)
                return lambda si: [(ring_view(si, 0, [n, D]), so)]

            def qkv_dmas(l, pr):
                if l == 0:
                    def f(si):
                        dm = []
                        for i in range(3):
                            src = wrows(na_qkv[:, i * D + pr * 128:i * D + (pr + 1) * 128])
                            dm.append((ring_view(si, 0, [KC, 3, 128])[:, :, i, :], src))
                        dm.append((ring_view(si, 3072, [960]), na_tb[:, pr, :]))
                        return dm
                    return f
                def f(si):
                    rv = ring_view(si, 0, [KC, 256])
                    return [(rv[:, :, 0:128], wrows(wa_qkv[:, pr * 128:(pr + 1) * 128])),
                            (rv[:, :, 128:256], wrows(wa_perm[:, pr * 128:(pr + 1) * 128]))]
                return f

            def kv_dmas(pr):
                g = pr // 2

                def f(si):
                    rv = ring_view(si, 0, [KC, 384])
                    kq = wrows(wa_qkv[:, D + g * 64:D + (g + 1) * 64])
                    kp = wrows(wa_perm[:, D + g * 64:D + (g + 1) * 64])
                    vq = wrows(wa_qkv[:, D + 256 + g * 64:D + 256 + (g + 1) * 64])
                    return [(rv[:, :, 0:64], kq), (rv[:, :, 64:128], kq),
                            (rv[:, :, 128:192], kp), (rv[:, :, 192:256], kp),
                            (rv[:, :, 256:320], vq), (rv[:, :, 320:384], vq)]
                return f

            def wo_dmas(l, q):
                wo = na_o if l == 0 else wa_o
                src = wo[2 * q * 128:(2 * q + 2) * 128, :].rearrange("(j p) d -> p j d", p=128)
                return lambda si: [(ring_view(si, 0, [2, D]), src)]

            sc.dma("pool", lambda e: e.dma_start(out=cst[:], in_=consts), "init_pool", wr=[R_cst])
            sc.dma("sp", lambda e: e.dma_start(out=ccs[:], in_=cc), "init_sp", wr=[Res()])
            sc.dma("sp", lambda e: e.dma_start(out=bmod_sb[:], in_=bmod), "init_sp", wr=[Res()])
            sc.dma("sp", lambda e: e.dma_start(out=lng_sb[:], in_=lng), "init_sp", wr=[Res()])
            sc.dma("sp", lambda e: e.dma_start(out=lnb_sb[:], in_=lnb), "init_sp", wr=[Res()])
            sc.dma("sp", lambda e: e.dma_start(out=sk_sb[:], in_=wa_sk), "init_sp", wr=[Res()])
            R_small.w = ("init_sp", sc.cnt["init_sp"])
            for t in range(4):
                off_, n_ = TILES[t]
                sc.dma("sp", lambda e, off_=off_, n_=n_: e.dma_start(
                    out=hT[:, :, off_:off_ + n_], in_=xT.rearrange("(c p) t -> p c t", p=128)[:, :, off_:off_ + n_]),
                    f"init_x{t}", wr=[R_h[c][t] for c in range(KC)])
            sc.dma("sp", lambda e: e.dma_start(out=hT[:, :, S:T], in_=ctxT.rearrange("(c p) t -> p c t", p=128)),
                   "init_x4", wr=[R_h[c][4] for c in range(KC)])
            if not record:
                w_issue_upto(LOOKAHEAD)

            sc.op("act", lambda e: e.activation(out=scb[:], in_=ccs[:], func=AF.Silu), rd=[R_small], wr=[R_scb])
            sc.op("dve", lambda e: e.memset(zero_col[:], 0.0), wr=[R_es])
            sc.op("act", lambda e: e.activation(out=es_sb[:], in_=sk_sb[:], func=AF.Exp), rd=[R_small, R_es], wr=[R_es])
            for l in range(DEPTH):
                for i in range(3):
                    a = 1.0 if (l == DEPTH - 1 and i == 2) else ALPHA
                    sc.op("dve", lambda e, l=l, i=i, a=a: e.tensor_scalar(
                        out=lng_sb[:, l, i, :], in0=lng_sb[:, l, i, :], scalar1=a, scalar2=None, op0=ALU.mult),
                        rd=[R_small], wr=[R_ln])
                    sc.op("dve", lambda e, l=l, i=i, a=a: e.tensor_scalar(
                        out=lnb_sb[:, l, i, :], in0=lnb_sb[:, l, i, :], scalar1=a, scalar2=None, op0=ALU.mult),
                        rd=[R_small], wr=[R_ln])
            for t in range(5):
                off_, n_ = TILES[t]
                if t % 2 == 0:
                    sc.op("act", lambda e, off_=off_, n_=n_: e.activation(out=hT[:, :, off_:off_ + n_], in_=hT[:, :, off_:off_ + n_],
                                                                          func=AF.Copy, scale=ALPHA),
                          rd=[], wr=h_res([t]))
                else:
                    sc.op("dve", lambda e, off_=off_, n_=n_: e.tensor_scalar(out=hT[:, :, off_:off_ + n_], in0=hT[:, :, off_:off_ + n_],
                                                                             scalar1=ALPHA, scalar2=None, op0=ALU.mult),
                          rd=[], wr=h_res([t]))

            modq = [(l, i) for l in range(DEPTH) for i in range(NMOD)]

            def mod_step():
                if not modq:
                    return
                l, i = modq.pop(0)
                bk = 6
                si = w_get(("mod", l, i), mod_dmas(l, i))
                wv = ring_view(si, 0, [KC, MODW])
                NJ = MODW // 128

                def f(e):
                    last = None
                    for jj in range(NJ):
                        for kc in range(KC):
                            last = e.matmul(banks[bk][:, 2 * jj:2 * jj + 2], lhsT=wv[:, kc, jj * 128:(jj + 1) * 128],
                                            rhs=scb[:, kc, :], start=(kc == 0), stop=(kc == KC - 1))
                    return last
                sc.op("pe", f, rd=[R_slot[si], R_scb], wr=[R_bank[bk]])
                sc.op("dve", lambda e: e.tensor_tensor(
                    out=raw[:, l, NJ * i:NJ * i + NJ, :], in0=banks[bk][:, 0:2 * NJ].rearrange("p (j s) -> p j s", s=2),
                    in1=bmod_sb[:, l, NJ * i:NJ * i + NJ].unsqueeze(2).broadcast_to([128, NJ, 2]), op=ALU.add),
                    rd=[R_bank[bk], R_small], wr=[R_raw[l]])
                LPS = NMOD // 3
                ii = i // LPS
                n0 = 3 * ii
                if i % LPS == 3:
                    sc.op("dve", lambda e: e.tensor_scalar(
                        out=Acol[:, l, ii, :, :], in0=raw[:, l, (n0 + 1) * 8:(n0 + 2) * 8, :], scalar1=1.0,
                        scalar2=1.0 / ALPHA, op0=ALU.add, op1=ALU.mult), rd=[R_raw[l]], wr=[R_colA[l][ii]])
                if i % LPS == LPS - 1:
                    wres = 1.0 if ii == 1 else 0.5
                    sc.op("dve", lambda e: e.tensor_scalar(
                        out=Gcol[:, l, ii, :, :], in0=raw[:, l, (n0 + 2) * 8:(n0 + 3) * 8, :], scalar1=wres,
                        scalar2=None, op0=ALU.mult), rd=[R_raw[l]], wr=[R_cols[l][ii]])

            def mod_ensure(l, ii):
                while modq and (modq[0][0] < l or (modq[0][0] == l and modq[0][1] < (NMOD // 3) * (ii + 1))):
                    mod_step()

            pend = {}

            def advance(t):
                lst = pend.get(t)
                if lst:
                    lst.pop(0)()
                    if not lst:
                        del pend[t]

            def flush(t):
                while t in pend:
                    advance(t)

            def flush_all():
                for t in sorted(list(pend.keys())):
                    flush(t)

            def modulate_tile(l, i, t):
                n0 = 3 * i
                off, n = TILES[t]
                s_ = 1 if t == 4 else 0
                for c in range(KC):
                    if c < 2 and t != 4:
                        sc.op("dve", lambda e, c=c: e.tensor_scalar(
                            out=uT[:, c, off:off + n], in0=hT[:, c, off:off + n], scalar1=Acol[:, l, i, c, s_:s_ + 1],
                            scalar2=raw[:, l, n0 * 8 + c, s_:s_ + 1], op0=ALU.mult, op1=ALU.add),
                            rd=[R_h[c][t], R_colA[l][i], R_raw[l]], wr=[R_u[t]])
                    elif t == 4:
                        sc.op("act", lambda e, c=c: e.activation(
                            out=uT[:, c, off:off + n], in_=hT[:, c, off:off + n], func=AF.Identity,
                            scale=Acol[:, l, i, c, s_:s_ + 1], bias=raw[:, l, n0 * 8 + c, s_:s_ + 1]),
                            rd=[R_h[c][t], R_colA[l][i], R_raw[l]], wr=[R_u[t]])
                    else:
                        sc.op("pool", lambda e, c=c: e.tensor_scalar(
                            out=uT[:, c, off:off + n], in0=hT[:, c, off:off + n], scalar1=Acol[:, l, i, c, s_:s_ + 1],
                            scalar2=raw[:, l, n0 * 8 + c, s_:s_ + 1], op0=ALU.mult, op1=ALU.add),
                            rd=[R_h[c][t], R_colA[l][i], R_raw[l]], wr=[R_u[t]])

            def ln_stats(l, i, t, k):
                off, n = TILES[t]
                mean_sb = mean_sb2[:, k, :]
                tmpA = tmpA2[:, k, :]
                R_mean, R_tmpA = R_mean2[k], R_tmpA2[k]
                for c in range(KC):
                    b = c % 2
                    sc.op("act", lambda e, c=c, b=b: e.activation(out=zb[:, b, 0:n], in_=hT[:, c, off:off + n], func=AF.Copy),
                          rd=[R_h[c][t]], wr=[R_zb[b]])
                    sc.op("act", lambda e, c=c, b=b: e.activation(out=zq[:, b, 0:n], in_=hT[:, c, off:off + n], func=AF.Square),
                          rd=[R_h[c][t]], wr=[R_zq[b]])

                    def f(e, c=c, b=b):
                        e.matmul(banks[6][:, 0:n], lhsT=onesd, rhs=zb[:, b, 0:n], start=(c == 0), stop=(c == KC - 1))
                        return e.matmul(banks[7][:, 0:n], lhsT=onesd, rhs=zq[:, b, 0:n], start=(c == 0), stop=(c == KC - 1))
                    sc.op("pe", f, rd=[R_zb[b], R_zq[b], R_cst], wr=[R_bank[6], R_bank[7]])

            def ln_stats2(l, i, t, k):
                off, n = TILES[t]
                mean_sb = mean_sb2[:, k, :]
                tmpA = tmpA2[:, k, :]
                R_mean, R_tmpA = R_mean2[k], R_tmpA2[k]
                sc.op("act", lambda e: e.activation(out=mean_sb[:, 0:n], in_=banks[6][:, 0:n], func=AF.Copy),
                      rd=[R_bank[6]], wr=[R_mean])
                sc.op("dve", lambda e: e.tensor_tensor(out=tmpA[:, 0:n], in0=mean_sb[:, 0:n], in1=mean_sb[:, 0:n], op=ALU.mult),
                      rd=[R_mean], wr=[R_tmpA])
                sc.op("dve", lambda e: e.tensor_tensor(out=tmpA[:, 0:n], in0=banks[7][:, 0:n], in1=tmpA[:, 0:n], op=ALU.subtract),
                      rd=[R_bank[7], R_tmpA], wr=[R_tmpA])
                sc.op("act", lambda e: e.activation(out=tmpA[:, 0:n], in_=tmpA[:, 0:n], func=AF.Ln, bias=eps_col[:, 0:1]),
                      rd=[R_tmpA, R_es], wr=[R_tmpA])
                sc.op("act", lambda e: e.activation(out=tmpA[:, 0:n], in_=tmpA[:, 0:n], func=AF.Exp, scale=-0.5),
                      rd=[R_tmpA], wr=[R_tmpA])

            def ln_apply(l, i, t, k, nxt):
                off, n = TILES[t]
                mean_sb = mean_sb2[:, k, :]
                tmpA = tmpA2[:, k, :]
                R_mean, R_tmpA = R_mean2[k], R_tmpA2[k]
                for c in range(KC):
                    hv = hT[:, c, off:off + n]
                    sc.op("pool", lambda e, hv=hv: e.tensor_tensor(out=hv, in0=hv, in1=mean_sb[:, 0:n], op=ALU.subtract),
                          rd=[R_mean, R_h[c][t]], wr=[R_h[c][t]])
                    sc.op("dve", lambda e, hv=hv: e.tensor_tensor(out=hv, in0=hv, in1=tmpA[:, 0:n], op=ALU.mult),
                          rd=[R_tmpA, R_h[c][t]], wr=[R_h[c][t]])
                    sc.op("act", lambda e, hv=hv, c=c: e.activation(out=hv, in_=hv, func=AF.Identity,
                                                                    scale=lng_sb[:, l, i, c:c + 1], bias=lnb_sb[:, l, i, c:c + 1]),
                          rd=[R_ln, R_h[c][t]], wr=[R_h[c][t]])
                if nxt is not None and t in nxt[2]:
                    assert not (modq and (modq[0][0] < nxt[0] or (modq[0][0] == nxt[0] and modq[0][1] < (NMOD // 3) * (nxt[1] + 1))))
                    modulate_tile(nxt[0], nxt[1], t)
                if nxt is None and stop is None:
                    sc.dma("sp", lambda e: e.dma_start(out=outT.rearrange("(c p) t -> p c t", p=128)[:, :, off:off + n],
                                                       in_=hT[:, :, off:off + n]),
                           "outd", rd=[R_h[c][t] for c in range(KC)])

            cnt = {"sa": 0, "av": 0, "y": 0}

            def z_update(l, i, c, t, bank):
                off, n = TILES[t]
                s_ = 1 if t == 4 else 0
                hv = hT[:, c, off:off + n]
                sc.op("dve", lambda e: e.scalar_tensor_tensor(out=hv, in0=banks[bank][:, 0:n], scalar=Gcol[:, l, i, c, s_:s_ + 1],
                                                              in1=hv, op0=ALU.mult, op1=ALU.add),
                      rd=[R_bank[bank], R_cols[l][i], R_h[c][t]], wr=[R_h[c][t]])

            def tail_pipeline(l, i, tiles, nxt, emit_outproj):
                tl = list(tiles)
                for k, t in enumerate(tl):
                    emit_outproj(t)
                    pend[t] = [(lambda t=t, k=k: ln_stats(l, i, t, k % 2)),
                               (lambda t=t, k=k: ln_stats2(l, i, t, k % 2)),
                               (lambda t=t, k=k: ln_apply(l, i, t, k % 2, nxt))]
                    if k >= 1:
                        advance(tl[k - 1])
                    if k >= 2:
                        advance(tl[k - 2])
                    if k >= 1:
                        advance(tl[k - 1])
                advance(tl[-1])
                if len(tl) >= 2:
                    advance(tl[-2])
                advance(tl[-1])

            def ffn(l, w, tiles, nxt):
                i = 0 if w == 0 else 2
                tiles = list(tiles)
                mod_ensure(l, i)
                first = True
                for pi, part in enumerate(PARTS):
                    j0, npart = part
                    for (j, g) in in_groups(part):
                        si = w_get(("win", l, w, j), win_dmas(l, w, j, g))
                        wv = ring_view(si, 0, [KC, 2, g * 128])
                        for jj in range(g):
                            jl = j + jj - j0
                            for t in tiles:
                                flush(t)
                                off, n = TILES[t]
                                k = cnt["av"]; cnt["av"] += 1
                                ba, bv = (k % 2), 2 + (k % 2)

                                def f(e, jj=jj, off=off, n=n, ba=ba, bv=bv, wv=wv):
                                    for kc in range(KC):
                                        e.matmul(banks[ba][:, 0:n], lhsT=wv[:, kc, 0, jj * 128:(jj + 1) * 128],
                                                 rhs=uT[:, kc, off:off + n], start=(kc == 0), stop=(kc == KC - 1))
                                    last = None
                                    for kc in range(KC):
                                        last = e.matmul(banks[bv][:, 0:n], lhsT=wv[:, kc, 1, jj * 128:(jj + 1) * 128],
                                                        rhs=uT[:, kc, off:off + n], start=(kc == 0), stop=(kc == KC - 1))
                                    return last
                                rdx = ffn_arena() if first else []
                                sc.op("pe", f, rd=[R_slot[si], R_u[t]], wr=[R_bank[ba], R_bank[bv]])
                                b = cnt["sa"] % 2; cnt["sa"] += 1
                                sc.op("act", lambda e, b=b, ba=ba, n=n: e.activation(out=sa[:, b, 0:n], in_=banks[ba][:, 0:n], func=AF.Silu),
                                      rd=[R_bank[ba]], wr=[R_sa[b]])
                                sc.op("dve", lambda e, b=b, bv=bv, n=n, jl=jl, off=off: e.tensor_tensor(
                                    out=gT[:, jl, off:off + n], in0=banks[bv][:, 0:n], in1=sa[:, b, 0:n], op=ALU.mult),
                                    rd=[R_bank[bv], R_sa[b]], wr=[R_g[jl][t]])
                                first = False
                        mod_step()
                    si = w_get(("wout", l, w, pi), wout_dmas(l, w, j0, npart))
                    wo = ring_view(si, 0, [npart, D])

                    last_part = (pi == len(PARTS) - 1)

                    def outproj(c, t, si=si, wo=wo, npart=npart, last_part=last_part):
                        off, n = TILES[t]
                        bank = (cnt["y"] % 6) if last_part else (4 + cnt["y"] % 4)
                        cnt["y"] += 1

                        def f(e):
                            last = None
                            for jl in range(npart):
                                last = e.matmul(banks[bank][:, 0:n], lhsT=wo[:, jl, c * 128:(c + 1) * 128],
                                                rhs=gT[:, jl, off:off + n], start=(jl == 0), stop=(jl == npart - 1))
                            return last
                        sc.op("pe", f, rd=[R_slot[si]] + [R_g[jl][t] for jl in range(npart)], wr=[R_bank[bank]])
                        z_update(l, i, c, t, bank)
                    if pi < len(PARTS) - 1:
                        for c in range(KC):
                            for t in tiles:
                                outproj(c, t)
                    else:
                        tail_pipeline(l, i, tiles, nxt, lambda t: [outproj(c, t) for c in range(KC)])
                    mod_step()

            def rs_na(r):
                return min(max(r - 4, 0), ROWS - 8)

            def na_items(qb):
                if qb == 4:
                    return [(32 + k, 0, 256, None) for k in range(4)]
                items = [(32 + k, 0, 512, None) for k in range(4)]
                for kr in range(ROWS):
                    rr = [r for r in range(8 * qb, 8 * qb + 8) if rs_na(r) <= kr <= rs_na(r) + 7]
                    if not rr:
                        continue
                    ra, rb = rr[0], rr[-1]
                    assert rr == list(range(ra, rb + 1))
                    e0 = ra - kr + 7
                    assert 0 <= e0 and e0 + (rb - ra) <= 14
                    items.append((kr, (ra - 8 * qb) * 64, (rb - 8 * qb + 1) * 64, ("tb", e0 * 64, (e0 + rb - ra + 1) * 64)))
                return items

            def wa_items(qb):
                items = [(32 + k, 0, 512, None) for k in range(4)]
                for kt in range(32):
                    lo = 64 * kt - 128
                    g0 = max(lo, 512 * qb, 0)
                    g1 = min(lo + 320, 512 * qb + 512, S)
                    if g1 <= g0:
                        continue
                    items.append((kt, g0 - 512 * qb, g1 - 512 * qb, ("mask", g0 - lo, g1 - lo)))
                return items

            def attention_pair(l, pr, pl, si, qbanks):
                exp_scale = 1.0 if l == 0 else 0.125
                flat = []
                for qi, qb in enumerate(qbanks):
                    its = na_items(qb) if l == 0 else wa_items(qb)
                    for ii, it in enumerate(its):
                        flat.append((qi, qb, ii, len(its), it))
                nI = len(flat)
                sbank = [None] * nI

                def emit_qk(x):
                    qi, qb, ii, nits, (kt, q0, q1, bias) = flat[x]
                    bk = x % 4
                    sbank[x] = bk
                    qoff = TILES[qb][0]
                    n = q1 - q0

                    extra = []
                    if bias is not None:
                        if bias[0] == "tb":
                            extra.append((0, n, ring_view(si, 3072, [960])[:, bias[1]:bias[2]]))
                        else:
                            for (lo, hi) in ((0, 63), (257, 320)):
                                a_, b_ = max(bias[1], lo), min(bias[2], hi)
                                if a_ < b_:
                                    extra.append((a_ - bias[1], b_ - bias[1], maskw[:, a_:b_]))

                    def f(e):
                        last = e.matmul(banks[bk][:, 0:n], lhsT=KBD[:, kt, :], rhs=qT[:, qoff + q0:qoff + q1],
                                        start=True, stop=(len(extra) == 0))
                        for xi, (c0, c1, bap) in enumerate(extra):
                            last = e.matmul(banks[bk][:, c0:c1], lhsT=ident, rhs=bap, start=False,
                                            stop=(xi == len(extra) - 1))
                        return last
                    rd = [R_KBD[kt // 8], R_q[qb], R_cst]
                    if bias is not None:
                        rd.append(R_slot[si] if bias[0] == "tb" else R_mask)
                    sc.op("pe", f, rd=rd, wr=[R_bank[bk]])

                LA = 2
                for x in range(min(LA, nI)):
                    emit_qk(x)
                pending = []
                for x in range(nI):
                    qi, qb, ii, nits, (kt, q0, q1, bias) = flat[x]
                    n = q1 - q0
                    bk = sbank[x]
                    pb = x % 3
                    while pending and pending[0][0] <= x:
                        pending.pop(0)[1]()
                    sc.op("act", lambda e, bk=bk, pb=pb, n=n: e.activation(out=PT[:, pb, 0:n], in_=banks[bk][:, 0:n],
                                                                           func=AF.Exp, scale=exp_scale),
                          rd=[R_bank[bk]], wr=[R_PT[pb]])
                    if x + LA < nI:
                        emit_qk(x + LA)
                    ob = 4 + 2 * (qi % 2)

                    def f(e, kt=kt, q0=q0, q1=q1, n=n, pb=pb, ob=ob, ii=ii, nits=nits):
                        e.matmul(banks[ob][:, q0:q1], lhsT=VBD[:, kt, :], rhs=PT[:, pb, 0:n],
                                 start=(ii == 0), stop=(ii == nits - 1))
                        return e.matmul(banks[ob + 1][:, q0:q1], lhsT=onesbd, rhs=PT[:, pb, 0:n],
                                        start=(ii == 0), stop=(ii == nits - 1))
                    sc.op("pe", f, rd=[R_VBD[kt // 4], R_PT[pb], R_cst], wr=[R_bank[ob], R_bank[ob + 1]])
                    if ii == nits - 1:
                        qoff, nq = TILES[qb]
                        escol = es_sb[:, pr:pr + 1] if l == 1 else zero_col[:, 0:1]

                        def norm(ob=ob, nq=nq, escol=escol, qoff=qoff, qb=qb):
                            sc.op("act", lambda e: e.activation(
                                out=rsb[:, 0, 0:nq], in_=banks[ob + 1][:, 0:nq], func=AF.Ln, bias=escol),
                                rd=[R_bank[ob + 1], R_es], wr=[R_rsb[0]])
                            sc.op("act", lambda e: e.activation(out=rsb[:, 0, 0:nq], in_=rsb[:, 0, 0:nq], func=AF.Exp, scale=-1.0),
                                  rd=[R_rsb[0]], wr=[R_rsb[0]])
                            sc.op("dve", lambda e: e.tensor_tensor(
                                out=OT[:, pl, qoff:qoff + nq], in0=banks[ob][:, 0:nq], in1=rsb[:, 0, 0:nq], op=ALU.mult),
                                rd=[R_bank[ob], R_rsb[0]], wr=[R_OT[pl][qb]])
                        pending.append((x + 3, norm))
                for _, fn in pending:
                    fn()

            def qkv_pair(l, pr, si, qtiles, part):
                pc = {"k": 0}

                def nb():
                    b = pc["k"] % 4
                    pc["k"] += 1
                    return b
                if l == 0:
                    wv = ring_view(si, 0, [KC, 3, 128])
                    wq = lambda kc: wv[:, kc, 0, :]
                    wk = lambda kc: wv[:, kc, 1, :]
                    wvv = lambda kc: wv[:, kc, 2, :]
                elif part == "q":
                    wv = ring_view(si, 0, [KC, 256])
                    wq = lambda kc: wv[:, kc, 0:128]
                    wqp = lambda kc: wv[:, kc, 128:256]
                else:
                    wv = ring_view(si, 0, [KC, 384])
                    wk = lambda kc: wv[:, kc, 0:128]
                    wkp = lambda kc: wv[:, kc, 128:256]
                    wvv = lambda kc: wv[:, kc, 256:384]

                def proj(bank, wf, off, n):
                    def f(e):
                        last = None
                        for kc in range(KC):
                            last = e.matmul(banks[bank][:, 0:n], lhsT=wf(kc), rhs=uT[:, kc, off:off + n],
                                            start=(kc == 0), stop=(kc == KC - 1))
                        return last
                    return f

                def rope(bq, bp, off, n, dst_fn):
                    sc.op("dve", lambda e: e.tensor_tensor(out=t1[:, 0, 0:n], in0=banks[bq][:, 0:n], in1=ropeC[:, off:off + n], op=ALU.mult),
                          rd=[R_bank[bq], R_rope], wr=[R_t1[0]])
                    sc.op("dve", lambda e: e.tensor_tensor(out=t2[:, 0, 0:n], in0=banks[bp][:, 0:n], in1=ropeS[:, off:off + n], op=ALU.mult),
                          rd=[R_bank[bp], R_rope2], wr=[R_t2[0]])
                    dst_fn()

                for t in (qtiles if part == "q" else []):
                    flush(t)
                    off, n = TILES[t]
                    b = nb()
                    sc.op("pe", proj(b, wq, off, n), rd=[R_slot[si], R_u[t]], wr=[R_bank[b]])
                    if l == 0:
                        sc.op("act", lambda e, b=b, off=off, n=n: e.activation(out=qT[:, off:off + n], in_=banks[b][:, 0:n],
                                                                               func=AF.Copy, scale=0.125),
                              rd=[R_bank[b]], wr=[R_q[t]])
                    else:
                        b2 = nb()
                        sc.op("pe", proj(b2, wqp, off, n), rd=[R_slot[si], R_u[t]], wr=[R_bank[b2]])

                        def dst(off=off, n=n, t=t):
                            sc.op("dve", lambda e: e.tensor_tensor(out=qT[:, off:off + n], in0=t1[:, 0, 0:n], in1=t2[:, 0, 0:n], op=ALU.add),
                                  rd=[R_t1[0], R_t2[0]], wr=[R_q[t]])
                        rope(b, b2, off, n, dst)
                if part != "kv":
                    return
                for t in range(5):
                    flush(t)
                    off, n = TILES[t]
                    nk = n // 64
                    kt0 = off // 64
                    b = nb()
                    sc.op("pe", proj(b, wk, off, n), rd=[R_slot[si], R_u[t]], wr=[R_bank[b]])
                    dA = KBD[0:64, kt0:kt0 + nk, 0:64]
                    dB = KBD[64:128, kt0:kt0 + nk, 64:128]
                    if l == 0 or t == 4:
                        sA = banks[b][0:64, 0:n].rearrange("p (k c) -> p k c", c=64)
                        sB = banks[b][64:128, 0:n].rearrange("p (k c) -> p k c", c=64)
                        sc.op("act", lambda e, dA=dA, sA=sA: e.activation(out=dA, in_=sA, func=AF.Copy),
                              rd=[R_bank[b]], wr=[R_KBD[t]])
                        sc.op("dve", lambda e, dB=dB, sB=sB: e.tensor_copy(out=dB, in_=sB), rd=[R_bank[b]], wr=[R_KBD[t]])
                    else:
                        b2 = nb()
                        sc.op("pe", proj(b2, wkp, off, n), rd=[R_slot[si], R_u[t]], wr=[R_bank[b2]])

                        def dst(dA=dA, dB=dB, n=n, t=t):
                            sc.op("dve", lambda e: e.tensor_tensor(
                                out=dA, in0=t1[0:64, 0, 0:n].rearrange("p (k c) -> p k c", c=64),
                                in1=t2[0:64, 0, 0:n].rearrange("p (k c) -> p k c", c=64), op=ALU.add),
                                rd=[R_t1[0], R_t2[0]], wr=[R_KBD[t]])
                            sc.op("dve", lambda e: e.tensor_tensor(
                                out=dB, in0=t1[64:128, 0, 0:n].rearrange("p (k c) -> p k c", c=64),
                                in1=t2[64:128, 0, 0:n].rearrange("p (k c) -> p k c", c=64), op=ALU.add),
                                rd=[R_t1[0], R_t2[0]], wr=[R_KBD[t]])
                        rope(b, b2, off, n, dst)
                for gi in range(9):
                    b = nb()

                    def f(e, gi=gi, b=b):
                        last = None
                        for k4 in range(4):
                            m = gi * 4 + k4
                            if m < 35:
                                for kc in range(KC):
                                    last = e.matmul(banks[b][:, k4 * 128:(k4 + 1) * 128], lhsT=uT[:, kc, 64 * m:64 * m + 128],
                                                    rhs=wvv(kc), start=(kc == 0), stop=(kc == KC - 1))
                            else:
                                for kc in range(KC):
                                    e.matmul(banks[b][0:64, k4 * 128:k4 * 128 + 64], lhsT=uT[:, kc, 64 * m:64 * m + 64],
                                             rhs=wvv(kc)[:, 0:64], start=(kc == 0), stop=(kc == KC - 1))
                                for kc in range(KC):
                                    last = e.matmul(banks[b][64:128, k4 * 128 + 64:k4 * 128 + 128], lhsT=uT[:, kc, 0:64],
                                                    rhs=wvv(kc)[:, 64:128], start=(kc == 0), stop=(kc == KC - 1))
                        return last
                    sc.op("pe", f, rd=[R_slot[si]] + R_u, wr=[R_bank[b]])
                    bv = banks[b][:, :].rearrange("p (k c) -> p k c", c=128)
                    sA = bv[0:64, :, 0:64]
                    dA = VBD[0:64, gi * 4:gi * 4 + 4, 0:64]
                    sc.op("act", lambda e, dA=dA, sA=sA: e.activation(out=dA, in_=sA, func=AF.Copy), rd=[R_bank[b]], wr=[R_VBD[gi]])
                    if gi < 8:
                        sB = bv[64:128, :, 64:128]
                        dB = VBD[64:128, gi * 4 + 1:gi * 4 + 5, 64:128]
                        sc.op("dve", lambda e, dB=dB, sB=sB: e.tensor_copy(out=dB, in_=sB), rd=[R_bank[b]],
                              wr=[R_VBD[gi], R_VBD[gi + 1]])
                    else:
                        sB = bv[64:128, 0:3, 64:128]
                        dB = VBD[64:128, 33:36, 64:128]
                        sc.op("dve", lambda e, dB=dB, sB=sB: e.tensor_copy(out=dB, in_=sB), rd=[R_bank[b]], wr=[R_VBD[8]])
                        sB2 = bv[64:128, 3, 64:128]
                        dB2 = VBD[64:128, 0, 64:128]
                        sc.op("dve", lambda e, dB2=dB2, sB2=sB2: e.tensor_copy(out=dB2, in_=sB2), rd=[R_bank[b]], wr=[R_VBD[0]])

            def mixer(l, ctx_out, nxt):
                i = 1
                mod_ensure(l, i)
                qtiles = list(range(5)) if ctx_out else list(range(4))
                if l == 1:
                    sc.op("dve", lambda e: e.memset(rsb[:, 0, 0:2], 0.0), rd=[], wr=[R_rope, R_rope2, R_mask] + R_rsb + ffn_arena())
                    sc.dma("pool", lambda e: e.dma_start(out=ropeC[:], in_=rope_cs[:, 0, :]), "aux", wr=[R_rope])
                    sc.dma("pool", lambda e: e.dma_start(out=ropeS[:], in_=rope_cs[:, 1, :]), "aux", wr=[R_rope2])
                    sc.dma("pool", lambda e: e.dma_start(out=maskw[:], in_=wa_mask), "aux", wr=[R_mask])
                for pr in range(8):
                    if l == 0:
                        si = w_get(("qkv", l, pr), qkv_dmas(l, pr))
                        qkv_pair(l, pr, si, qtiles, "kv")
                        qkv_pair(l, pr, si, qtiles, "q")
                    else:
                        if pr % 2 == 0:
                            skv = w_get(("kv", l, pr), kv_dmas(pr))
                            qkv_pair(l, pr, skv, qtiles, "kv")
                            mod_step()
                        si = w_get(("qkv", l, pr), qkv_dmas(l, pr))
                        qkv_pair(l, pr, si, qtiles, "q")
                    pl = pr % 2
                    attention_pair(l, pr, pl, si, qtiles)
                    mod_step()
                    if pl == 0:
                        continue
                    so = w_get(("wo", l, pr // 2), wo_dmas(l, pr // 2))
                    wo = ring_view(so, 0, [2, D])
                    lastp = (pr == 7)

                    def outproj(c, t, so=so, wo=wo, lastp=lastp):
                        off, n = TILES[t]
                        bank = (cnt["y"] % 6) if lastp else (4 + cnt["y"] % 4)
                        cnt["y"] += 1

                        def f(e):
                            e.matmul(banks[bank][:, 0:n], lhsT=wo[:, 0, c * 128:(c + 1) * 128], rhs=OT[:, 0, off:off + n],
                                     start=True, stop=False)
                            return e.matmul(banks[bank][:, 0:n], lhsT=wo[:, 1, c * 128:(c + 1) * 128], rhs=OT[:, 1, off:off + n],
                                            start=False, stop=True)
                        sc.op("pe", f, rd=[R_slot[so], R_OT[0][t], R_OT[1][t]], wr=[R_bank[bank]])
                        z_update(l, i, c, t, bank)
                    if not lastp:
                        for c in range(KC):
                            for t in qtiles:
                                outproj(c, t)
                    else:
                        tail_pipeline(l, i, qtiles, nxt, lambda t: [outproj(c, t) for c in range(KC)])
                    mod_step()

            def arena_to_ffn():
                sc.op("dve", lambda e: e.memset(sa[:, 0, 0:2], 0.0), rd=[], wr=att_res() + ffn_arena())

            sc.op("dve", lambda e: e.memset(eps_col[:], LN_EPS), wr=[R_es])
            sc.op("dve", lambda e: e.memset(KBD[:, :, :], 0.0), rd=[], wr=R_KBD)
            sc.op("dve", lambda e: e.memset(VBD[:, :, :], 0.0), rd=[], wr=R_VBD)
            for _ in range(4):
                mod_step()
            for t in range(5):
                modulate_tile(0, 0, t)
            mod_ensure(0, 0)
            seq = [("ffn", 0, 0), ("mix", 0), ("ffn", 0, 1), ("ffn", 1, 0), ("mix", 1), ("ffn", 1, 1)]
            stops = {"ffn00": 1, "mix0": 2, "ffn01": 3, "ffn10": 4, "mix1": 5, None: 6}
            seq = seq[:stops[stop]]
            for k, sub in enumerate(seq):
                nxt = None
                if k + 1 < len(seq):
                    ns = seq[k + 1]
                    if ns[0] == "mix":
                        nxt = (ns[1], 1, list(range(5)))
                    else:
                        ni = 0 if ns[2] == 0 else 2
                        ntl = list(range(4)) if (ns[1] == 1 and ns[2] == 1) else list(range(5))
                        nxt = (ns[1], ni, ntl)
                if sub[0] == "ffn":
                    if k > 0 and seq[k - 1][0] == "mix":
                        arena_to_ffn()
                    tl = range(4) if (sub[1] == 1 and sub[2] == 1) else range(5)
                    ffn(sub[1], sub[2], tl, nxt)
                else:
                    mixer(sub[1], sub[1] == 0, nxt)
            flush_all()

            ntile_out = 5 if stop is not None else 4
            for c in (range(KC) if stop is not None else []):
                sc.dma("sp", lambda e, c=c: e.dma_start(out=outT[c * 128:(c + 1) * 128, :], in_=hT[:, c, 0:n_out_tok]),
                       "outd", rd=[R_h[c][t] for t in range(ntile_out)])

            for coarse in ("init_sp", "aux"):
                tot = sc.cnt[coarse]
                for k in sc.ops:
                    sc.ops[k] = [([(en, (tot if en == coarse else v)) for en, v in waits], fn, sn, c)
                                 for waits, fn, sn, c in sc.ops[k]]
            return rec

        eps_col = sb("eps_col", [128, 1], F32)
        plan = emit(Sched(), None)
        sc = Sched()
        sc.sems = SEMS
        emit(sc, plan)

        block = es.enter_context(nc.Block())

        @block.tensor
        def _(e):
            sc.replay("pe", e)

        @block.scalar
        def _(e):
            sc.replay("act", e)

        @block.vector
        def _(e):
            sc.replay("dve", e)

        @block.gpsimd
        def _(e):
            sc.replay("pool", e)

        @block.sync
        def _(e):
            sc.replay("sp", e, final_waits=[("outd", sc.cnt["outd"])])

    return nc


def _host_tables(na_rpb, wa_sinks):
    kc = np.arange(64)[:, None]
    qc = np.arange(64)[None, :]
    ws = np.clip(qc - 8, 0, 48)
    valid = (kc >= ws) & (kc < ws + 16)
    coff = np.clip(kc - qc, -15, 15) + 15
    rpb = np.asarray(na_rpb[0], dtype=np.float32)
    tb = np.empty((2, 64, 8, 15, 64), np.float32)
    for hh in range(2):
        for e in range(15):
            g = rpb[hh::2][:, 14 - e][:, coff]
            g = np.where(valid[None], g, np.float32(NEG))
            tb[hh, :, :, e, :] = g.transpose(1, 0, 2)
    tb = tb.reshape(128, 8, 15 * 64)
    sk = np.asarray(wa_sinks[0], np.float32)
    wa_sk = np.empty((128, 8), np.float32)
    for p in range(128):
        wa_sk[p] = sk[(p // 64)::2]
    kl = np.arange(64)[:, None]
    j = np.arange(320)[None, :]
    m = np.where((j >= kl) & (j <= kl + 256), np.float32(0), np.float32(NEG)).astype(np.float32)
    wa_mask = np.concatenate([m, m], 0)
    t = np.arange(S)
    rows = (t // GW).astype(np.float32)
    cols = (t % GW).astype(np.float32)
    inv = (np.float32(10000.0) ** (-np.arange(16, dtype=np.float32) / np.float32(16))).astype(np.float32)
    cs = np.empty((128, 2, S), np.float32)
    for p in range(128):
        d = p % 64
        pos = rows if d < 32 else cols
        ang = (pos * inv[d % 16]).astype(np.float32)
        cs[p, 0] = np.cos(ang)
        sn = np.sin(ang)
        cs[p, 1] = -sn if (d % 32) < 16 else sn
    perm = np.empty(64, np.int64)
    for d in range(64):
        perm[d] = d + 16 if (d % 32) < 16 else d - 16
    consts = np.zeros((128, 3, 128), np.float32)
    consts[:, 0, :] = np.eye(128, dtype=np.float32)
    consts[0:64, 1, 0:64] = 1.0
    consts[64:128, 1, 64:128] = 1.0
    consts[:, 2, :] = 1.0 / D
    return tb, wa_sk, wa_mask, cs, perm, consts


_CACHE = {}


def _prep_shared(c_ctx, w_mod, b_mod, ln_g, ln_b, ffn_w_in, ffn_w_out, na_w_qkv, na_w_o, na_rpb,
                 wa_w_qkv, wa_w_o, wa_sinks):
    f = lambda a: np.ascontiguousarray(np.asarray(a, dtype=np.float32))
    tb, wa_sk, wa_mask, cs, perm, consts = _host_tables(np.asarray(na_rpb), np.asarray(wa_sinks))
    wq = f(wa_w_qkv)[0]
    cols = np.concatenate([(h * 64 + perm) for h in range(NH)] + [D + g * 64 + perm for g in range(4)])
    sh = {
        "w_mod": f(w_mod),
        "bmod": f(np.asarray(b_mod).reshape(DEPTH, 72, 128).transpose(2, 0, 1)),
        "lng": f(np.asarray(ln_g).reshape(DEPTH, 3, KC, 128).transpose(3, 0, 1, 2)),
        "lnb": f(np.asarray(ln_b).reshape(DEPTH, 3, KC, 128).transpose(3, 0, 1, 2)),
        "ffn_w_in": f(ffn_w_in),
        "ffn_w_out": f(ffn_w_out),
        "na_w_qkv": f(na_w_qkv)[0],
        "na_w_o": f(na_w_o)[0],
        "na_tb": f(tb),
        "wa_w_qkv": wq,
        "wa_w_perm": f(wq[:, cols]),
        "wa_w_o": f(wa_w_o)[0],
        "wa_sk": f(wa_sk),
        "wa_mask": f(wa_mask),
        "rope_cs": f(cs),
        "consts": f(consts),
    }
    return sh


def run(inputs, stop=None):
    x = np.asarray(inputs["x"], np.float32)
    c = np.asarray(inputs["c"], np.float32)
    ctx = np.asarray(inputs["ctx"], np.float32)
    c_ctx = np.asarray(inputs["c_ctx"], np.float32)
    sh = _prep_shared(c_ctx, *[inputs[k] for k in ("w_mod", "b_mod", "ln_g", "ln_b", "ffn_w_in", "ffn_w_out",
                                                     "na_w_qkv", "na_w_o", "na_rpb", "wa_w_qkv", "wa_w_o", "wa_sinks")])
    key = ("nc", stop)
    if key not in _CACHE:
        _CACHE[key] = build_program(stop)
    nc = _CACHE[key]
    B = x.shape[0]
    in_maps = []
    for b in range(B):
        m = dict(sh)
        m["xT"] = np.ascontiguousarray(x[b].T)
        m["ctxT"] = np.ascontiguousarray(ctx[b].T)
        ccb = np.stack([c[b], c_ctx], -1).reshape(KC, 128, 2).transpose(1, 0, 2)
        m["cc"] = np.ascontiguousarray(ccb)
        in_maps.append(m)
    res = run_bass_kernel_spmd(nc, in_maps, core_ids=list(range(B)))
    outs = [np.asarray(r["outT"]) for r in res.results]
    return np.stack([o.T for o in outs], 0)


def kernel(**inputs):
    out = run(inputs, None)
    return np.ascontiguousarray(out.astype(np.float32))
```

```python
import numpy as np
from contextlib import ExitStack
import concourse.bass as bass
import concourse.mybir as mybir
from concourse.bass_utils import run_bass_kernel_spmd

F32 = mybir.dt.float32
BF16 = mybir.dt.bfloat16
AF = mybir.ActivationFunctionType
ALU = mybir.AluOpType

D = 1024
KC = 8
S = 2048
CT = 256
T = S + CT
FF = 2816
FC = 22
NH = 16
HD = 64
GW = 64
ROWS = 32
DEPTH = 2
ALPHA = (2 * DEPTH) ** 0.25
LN_EPS = 1e-5
NEG = -1e30
SLOT = 4096
NSLOT = 3
LOOKAHEAD = 2
PARTS = [(0, 4), (4, 4), (8, 4), (12, 4), (16, 3), (19, 3)]
MODW = 512
NMOD = 9 * D // MODW
TILES = [(0, 512), (512, 512), (1024, 512), (1536, 512), (2048, 256)]


class Res:
    __slots__ = ("w", "r")

    def __init__(self):
        self.w = None
        self.r = {}


class Sched:
    def __init__(self):
        self.ops = {k: [] for k in ("pe", "act", "dve", "pool", "sp")}
        self.cnt = {k: 0 for k in self.ops}
        self.seen = {k: {} for k in self.ops}
        self.sems = {}
        self.step = {k: 1 for k in self.ops}

    def add_dma_sem(self, name):
        self.cnt[name] = 0
        self.step[name] = 16

    def _deps(self, e, rd, wr):
        deps = {}

        def need(x):
            if x is None:
                return
            en, c = x
            if c > deps.get(en, 0):
                deps[en] = c

        for r in rd:
            need(r.w)
        for w in wr:
            need(w.w)
            for en, c in w.r.items():
                need((en, c))
        waits = []
        seen = self.seen[e]
        for en, c in deps.items():
            if en == e and e == "pe":
                continue
            if seen.get(en, 0) >= c:
                continue
            seen[en] = c
            waits.append((en, c))
        return waits

    def op(self, e, fn, rd=(), wr=()):
        waits = self._deps(e, rd, wr)
        self.cnt[e] += 1
        c = self.cnt[e]
        self.ops[e].append((waits, fn, e, c))
        for r in rd:
            if r.r.get(e, 0) < c:
                r.r[e] = c
        for w in wr:
            w.w = (e, c)
            w.r = {}

    def dma(self, q, fn, dsem, rd=(), wr=()):
        waits = self._deps(q, rd, wr)
        self.cnt[dsem] += 16
        c = self.cnt[dsem]
        self.ops[q].append((waits, fn, dsem, c))
        for r in rd:
            if r.r.get(dsem, 0) < c:
                r.r[dsem] = c
        for w in wr:
            w.w = (dsem, c)
            w.r = {}

    def replay(self, name, eng, final_waits=()):
        for waits, fn, sname, c in self.ops[name]:
            for en, v in waits:
                eng.wait_ge(self.sems[en], v)
            inst = fn(eng)
            inst.then_inc(self.sems[sname], self.step[sname])
        for en, v in final_waits:
            eng.wait_ge(self.sems[en], v)


def build_program(stop=None):
    nc = bass.Bass("TRN2", target_bir_lowering=False)

    def din(name, shape):
        return nc.dram_tensor(name, list(shape), F32, kind="ExternalInput").ap()

    xT = din("xT", [D, S])
    ctxT = din("ctxT", [D, CT])
    cc = din("cc", [128, KC, 2])
    w_mod = din("w_mod", [DEPTH, D, 9 * D])
    bmod = din("bmod", [128, DEPTH, 72])
    lng = din("lng", [128, DEPTH, 3, KC])
    lnb = din("lnb", [128, DEPTH, 3, KC])
    w_in = din("ffn_w_in", [DEPTH, 2, D, 2 * FF])
    w_out = din("ffn_w_out", [DEPTH, 2, FF, D])
    na_qkv = din("na_w_qkv", [D, 3 * D])
    na_o = din("na_w_o", [D, D])
    na_tb = din("na_tb", [128, 8, 15 * 64])
    wa_qkv = din("wa_w_qkv", [D, D + 512])
    wa_perm = din("wa_w_perm", [D, D + 256])
    wa_o = din("wa_w_o", [D, D])
    wa_sk = din("wa_sk", [128, 8])
    wa_mask = din("wa_mask", [128, 320])
    rope_cs = din("rope_cs", [128, 2, S])
    consts = din("consts", [128, 3, 128])
    n_out_tok = T if stop is not None else S
    outT = nc.dram_tensor("outT", [D, n_out_tok], F32, kind="ExternalOutput").ap()

    es = ExitStack()
    with es:
        def sb(name, shape, dt):
            return es.enter_context(nc.sbuf_tensor(name, list(shape), dt))

        hT = sb("hT", [128, KC, T], F32)
        uT = sb("uT", [128, KC, T], BF16)
        ring = sb("ring", [128, NSLOT, SLOT], BF16)
        ARENA = 4 * T + 2 * 1024
        ARENA_N = 25152
        arena = sb("arena", [128, ARENA_N], BF16)
        zb = sb("zb", [128, 2, 512], BF16)
        zq = sb("zq", [128, 2, 512], BF16)
        mean_sb2 = sb("mean_sb", [128, 2, 512], F32)
        tmpA2 = sb("tmpA", [128, 2, 512], F32)
        cst = sb("cst", [128, 3, 128], BF16)
        ccs = sb("ccs", [128, KC, 2], F32)
        scb = sb("scb", [128, KC, 2], BF16)
        bmod_sb = sb("bmod_sb", [128, DEPTH, 72], F32)
        raw = sb("raw", [128, DEPTH, 72, 2], F32)
        Acol = sb("Acol", [128, DEPTH, 3, KC, 2], F32)
        Gcol = sb("Gcol", [128, DEPTH, 3, KC, 2], F32)
        lng_sb = sb("lng_sb", [128, DEPTH, 3, KC], F32)
        lnb_sb = sb("lnb_sb", [128, DEPTH, 3, KC], F32)
        sk_sb = sb("sk_sb", [128, 8], F32)
        es_sb = sb("es_sb", [128, 8], F32)
        zero_col = sb("zero_col", [128, 1], F32)
        banks = [es.enter_context(nc.psum_tensor(f"bank{i}", [128, 512], F32)) for i in range(8)]

        ident = cst[:, 0, :]
        onesbd = cst[:, 1, :]
        onesd = cst[:, 2, :]

        def shaped(flat, shape):
            if len(shape) == 1:
                return flat
            names = " ".join(f"a{i}" for i in range(len(shape)))
            kw = {f"a{i}": int(s_) for i, s_ in enumerate(shape)}
            return flat.rearrange(f"p ({names}) -> p {names}", **kw)

        def aview(off, shape, dt=BF16):
            n = int(np.prod(shape))
            if dt == F32:
                assert off % 2 == 0
                flat = arena[:, off:off + 2 * n].bitcast(F32)
            else:
                flat = arena[:, off:off + n]
            return shaped(flat, shape)

        def ring_view(si, off, shape):
            n = int(np.prod(shape))
            return shaped(ring[:, si, off:off + n], shape)

        gT = aview(0, [4, T])
        sa = aview(4 * T, [2, 512], F32)
        o = 0
        qT = aview(o, [T]); o += T
        OT = aview(o, [2, T]); o += 2 * T
        PT = aview(o, [3, 512]); o += 3 * 512
        rsb = aview(o, [1, 512], F32); o += 1024
        t1 = aview(o, [1, 512], F32); o += 1024
        t2 = aview(o, [1, 512], F32); o += 1024
        ropeC = aview(o, [S]); o += S
        ropeS = aview(o, [S]); o += S
        maskw = aview(o, [320]); o += 320
        assert o >= ARENA
        KBD = aview(o, [36, 128]); o += 36 * 128
        VBD = aview(o, [36, 128]); o += 36 * 128
        assert o <= ARENA_N, o
        assert ARENA <= ARENA_N

        SEMS = {}
        for k in ("pe", "act", "dve", "pool", "sp"):
            SEMS[k] = es.enter_context(nc.semaphore(f"s_{k}"))
        dma_names = [f"slot{i}" for i in range(NSLOT)] + ["init_sp", "init_pool", "aux", "outd"] + [f"init_x{t}" for t in range(5)]
        for k in dma_names:
            SEMS[k] = es.enter_context(nc.semaphore(f"s_{k}"))

        def wrows(ap2d):
            return ap2d.rearrange("(c p) f -> p c f", p=128)

        def in_groups(part):
            j0, n = part
            gs = []
            j = j0
            while j < j0 + n:
                g = min(2, j0 + n - j)
                gs.append((j, g))
                j += g
            return gs

        def emit(sc, plan):
            record = plan is None
            rec = []
            for k in dma_names:
                sc.add_dma_sem(k)

            R_h = [[Res() for _ in TILES] for _ in range(KC)]
            R_u = [Res() for _ in TILES]
            R_g = [[Res() for _ in TILES] for _ in range(4)]
            R_slot = [Res() for _ in range(NSLOT)]
            R_bank = [Res() for _ in range(8)]
            R_sa = [Res(), Res()]
            R_zb = [Res(), Res()]
            R_zq = [Res(), Res()]
            R_mean2 = [Res(), Res()]
            R_tmpA2 = [Res(), Res()]
            R_cst = Res()
            R_small = Res()
            R_scb = Res()
            R_raw = [Res() for _ in range(DEPTH)]
            R_cols = [[Res() for _ in range(3)] for _ in range(DEPTH)]
            R_colA = [[Res() for _ in range(3)] for _ in range(DEPTH)]
            R_ln = Res()
            R_es = Res()
            R_KBD = [Res() for _ in range(5)]
            R_VBD = [Res() for _ in range(9)]
            R_q = [Res() for _ in TILES]
            R_OT = [[Res() for _ in TILES] for _ in range(2)]
            R_PT = [Res() for _ in range(3)]
            R_rsb = [Res()]
            R_t1 = [Res()]
            R_t2 = [Res()]
            R_rope = Res()
            R_rope2 = Res()
            R_mask = Res()

            def h_res(tiles, cs=range(KC)):
                return [R_h[c][t] for c in cs for t in tiles]

            def ffn_arena():
                return [r for row in R_g for r in row] + R_sa

            def att_res():
                return (R_q + R_OT[0] + R_OT[1] + R_PT + R_rsb + R_t1 + R_t2 + [R_rope, R_rope2, R_mask])

            wstate = {"issued": 0, "next": 0}

            def w_issue_upto(n):
                while wstate["issued"] < min(n, len(plan)):
                    i = wstate["issued"]
                    si = i % NSLOT
                    key, dmas_fn = plan[i]
                    for dst, src in dmas_fn(si):
                        sc.dma("pool", (lambda e, dst=dst, src=src: e.dma_start(out=dst, in_=src)),
                               f"slot{si}", wr=[R_slot[si]])
                    wstate["issued"] += 1

            def w_get(key, dmas_fn):
                i = wstate["next"]
                wstate["next"] += 1
                if record:
                    rec.append((key, dmas_fn))
                    return i % NSLOT
                assert plan[i][0] == key, (plan[i][0], key)
                w_issue_upto(i + 1 + LOOKAHEAD)
                return i % NSLOT

            def mod_dmas(l, i):
                src = wrows(w_mod[l][:, i * MODW:(i + 1) * MODW])
                return lambda si: [(ring_view(si, 0, [KC, MODW]), src)]

            def win_dmas(l, w, j, g):
                sa_ = wrows(w_in[l, w][:, j * 128:(j + g) * 128])
                sv_ = wrows(w_in[l, w][:, FF + j * 128:FF + (j + g) * 128])
                return lambda si: [(ring_view(si, 0, [KC, 2, g * 128])[:, :, 0, :], sa_),
                                   (ring_view(si, 0, [KC, 2, g * 128])[:, :, 1, :], sv_)]

            def wout_dmas(l, w, j0, n):
                so = w_out[l, w][j0 * 128:(j0 + n) * 128, :].rearrange("(j p) d -> p j d", p=128)
                return lambda si: [(ring_view(si, 0, [n, D]), so)]

            def qkv_dmas(l, pr):
                if l == 0:
                    def f(si):
                        dm = []
                        for i in range(3):
                            src = wrows(na_qkv[:, i * D + pr * 128:i * D + (pr + 1) * 128])
                            dm.append((ring_view(si, 0, [KC, 3, 128])[:, :, i, :], src))
                        dm.append((ring_view(si, 3072, [960]), na_tb[:, pr, :]))
                        return dm
                    return f
                def f(si):
                    rv = ring_view(si, 0, [KC, 256])
                    return [(rv[:, :, 0:128], wrows(wa_qkv[:, pr * 128:(pr + 1) * 128])),
                            (rv[:, :, 128:256], wrows(wa_perm[:, pr * 128:(pr + 1) * 128]))]
                return f

            def kv_dmas(pr):
                g = pr // 2

                def f(si):
                    rv = ring_view(si, 0, [KC, 384])
                    kq = wrows(wa_qkv[:, D + g * 64:D + (g + 1) * 64])
                    kp = wrows(wa_perm[:, D + g * 64:D + (g + 1) * 64])
                    vq = wrows(wa_qkv[:, D + 256 + g * 64:D + 256 + (g + 1) * 64])
                    return [(rv[:, :, 0:64], kq), (rv[:, :, 64:128], kq),
                            (rv[:, :, 128:192], kp), (rv[:, :, 192:256], kp),
                            (rv[:, :, 256:320], vq), (rv[:, :, 320:384], vq)]
                return f

            def wo_dmas(l, q):
                wo = na_o if l == 0 else wa_o
                src = wo[2 * q * 128:(2 * q + 2) * 128, :].rearrange("(j p) d -> p j d", p=128)
                return lambda si: [(ring_view(si, 0, [2, D]), src)]

            sc.dma("pool", lambda e: e.dma_start(out=cst[:], in_=consts), "init_pool", wr=[R_cst])
            sc.dma("sp", lambda e: e.dma_start(out=ccs[:], in_=cc), "init_sp", wr=[Res()])
            sc.dma("sp", lambda e: e.dma_start(out=bmod_sb[:], in_=bmod), "init_sp", wr=[Res()])
            sc.dma("sp", lambda e: e.dma_start(out=lng_sb[:], in_=lng), "init_sp", wr=[Res()])
            sc.dma("sp", lambda e: e.dma_start(out=lnb_sb[:], in_=lnb), "init_sp", wr=[Res()])
            sc.dma("sp", lambda e: e.dma_start(out=sk_sb[:], in_=wa_sk), "init_sp", wr=[Res()])
            R_small.w = ("init_sp", sc.cnt["init_sp"])
            for t in range(4):
                off_, n_ = TILES[t]
                sc.dma("sp", lambda e, off_=off_, n_=n_: e.dma_start(
                    out=hT[:, :, off_:off_ + n_], in_=xT.rearrange("(c p) t -> p c t", p=128)[:, :, off_:off_ + n_]),
                    f"init_x{t}", wr=[R_h[c][t] for c in range(KC)])
            sc.dma("sp", lambda e: e.dma_start(out=hT[:, :, S:T], in_=ctxT.rearrange("(c p) t -> p c t", p=128)),
                   "init_x4", wr=[R_h[c][4] for c in range(KC)])
            if not record:
                w_issue_upto(LOOKAHEAD)

            sc.op("act", lambda e: e.activation(out=scb[:], in_=ccs[:], func=AF.Silu), rd=[R_small], wr=[R_scb])
            sc.op("dve", lambda e: e.memset(zero_col[:], 0.0), wr=[R_es])
            sc.op("act", lambda e: e.activation(out=es_sb[:], in_=sk_sb[:], func=AF.Exp), rd=[R_small, R_es], wr=[R_es])
            for l in range(DEPTH):
                for i in range(3):
                    a = 1.0 if (l == DEPTH - 1 and i == 2) else ALPHA
                    sc.op("dve", lambda e, l=l, i=i, a=a: e.tensor_scalar(
                        out=lng_sb[:, l, i, :], in0=lng_sb[:, l, i, :], scalar1=a, scalar2=None, op0=ALU.mult),
                        rd=[R_small], wr=[R_ln])
                    sc.op("dve", lambda e, l=l, i=i, a=a: e.tensor_scalar(
                        out=lnb_sb[:, l, i, :], in0=lnb_sb[:, l, i, :], scalar1=a, scalar2=None, op0=ALU.mult),
                        rd=[R_small], wr=[R_ln])
            for t in range(5):
                off_, n_ = TILES[t]
                if t % 2 == 0:
                    sc.op("act", lambda e, off_=off_, n_=n_: e.activation(out=hT[:, :, off_:off_ + n_], in_=hT[:, :, off_:off_ + n_],
                                                                          func=AF.Copy, scale=ALPHA),
                          rd=[], wr=h_res([t]))
                else:
                    sc.op("dve", lambda e, off_=off_, n_=n_: e.tensor_scalar(out=hT[:, :, off_:off_ + n_], in0=hT[:, :, off_:off_ + n_],
                                                                             scalar1=ALPHA, scalar2=None, op0=ALU.mult),
                          rd=[], wr=h_res([t]))

            modq = [(l, i) for l in range(DEPTH) for i in range(NMOD)]

            def mod_step():
                if not modq:
                    return
                l, i = modq.pop(0)
                bk = 6
                si = w_get(("mod", l, i), mod_dmas(l, i))
                wv = ring_view(si, 0, [KC, MODW])
                NJ = MODW // 128

                def f(e):
                    last = None
                    for jj in range(NJ):
                        for kc in range(KC):
                            last = e.matmul(banks[bk][:, 2 * jj:2 * jj + 2], lhsT=wv[:, kc, jj * 128:(jj + 1) * 128],
                                            rhs=scb[:, kc, :], start=(kc == 0), stop=(kc == KC - 1))
                    return last
                sc.op("pe", f, rd=[R_slot[si], R_scb], wr=[R_bank[bk]])
                sc.op("dve", lambda e: e.tensor_tensor(
                    out=raw[:, l, NJ * i:NJ * i + NJ, :], in0=banks[bk][:, 0:2 * NJ].rearrange("p (j s) -> p j s", s=2),
                    in1=bmod_sb[:, l, NJ * i:NJ * i + NJ].unsqueeze(2).broadcast_to([128, NJ, 2]), op=ALU.add),
                    rd=[R_bank[bk], R_small], wr=[R_raw[l]])
                LPS = NMOD // 3
                ii = i // LPS
                n0 = 3 * ii
                if i % LPS == 3:
                    sc.op("dve", lambda e: e.tensor_scalar(
                        out=Acol[:, l, ii, :, :], in0=raw[:, l, (n0 + 1) * 8:(n0 + 2) * 8, :], scalar1=1.0,
                        scalar2=1.0 / ALPHA, op0=ALU.add, op1=ALU.mult), rd=[R_raw[l]], wr=[R_colA[l][ii]])
                if i % LPS == LPS - 1:
                    wres = 1.0 if ii == 1 else 0.5
                    sc.op("dve", lambda e: e.tensor_scalar(
                        out=Gcol[:, l, ii, :, :], in0=raw[:, l, (n0 + 2) * 8:(n0 + 3) * 8, :], scalar1=wres,
                        scalar2=None, op0=ALU.mult), rd=[R_raw[l]], wr=[R_cols[l][ii]])

            def mod_ensure(l, ii):
                while modq and (modq[0][0] < l or (modq[0][0] == l and modq[0][1] < (NMOD // 3) * (ii + 1))):
                    mod_step()

            pend = {}

            def advance(t):
                lst = pend.get(t)
                if lst:
                    lst.pop(0)()
                    if not lst:
                        del pend[t]

            def flush(t):
                while t in pend:
                    advance(t)

            def flush_all():
                for t in sorted(list(pend.keys())):
                    flush(t)

            def modulate_tile(l, i, t, pool_ok=True):
                n0 = 3 * i
                off, n = TILES[t]
                s_ = 1 if t == 4 else 0
                for c in range(KC):
                    if (c < 2 or (not pool_ok and c < 5)) and t != 4:
                        sc.op("dve", lambda e, c=c: e.tensor_scalar(
                            out=uT[:, c, off:off + n], in0=hT[:, c, off:off + n], scalar1=Acol[:, l, i, c, s_:s_ + 1],
                            scalar2=raw[:, l, n0 * 8 + c, s_:s_ + 1], op0=ALU.mult, op1=ALU.add),
                            rd=[R_h[c][t], R_colA[l][i], R_raw[l]], wr=[R_u[t]])
                    elif t == 4 or not pool_ok:
                        sc.op("act", lambda e, c=c: e.activation(
                            out=uT[:, c, off:off + n], in_=hT[:, c, off:off + n], func=AF.Identity,
                            scale=Acol[:, l, i, c, s_:s_ + 1], bias=raw[:, l, n0 * 8 + c, s_:s_ + 1]),
                            rd=[R_h[c][t], R_colA[l][i], R_raw[l]], wr=[R_u[t]])
                    else:
                        sc.op("pool", lambda e, c=c: e.tensor_scalar(
                            out=uT[:, c, off:off + n], in0=hT[:, c, off:off + n], scalar1=Acol[:, l, i, c, s_:s_ + 1],
                            scalar2=raw[:, l, n0 * 8 + c, s_:s_ + 1], op0=ALU.mult, op1=ALU.add),
                            rd=[R_h[c][t], R_colA[l][i], R_raw[l]], wr=[R_u[t]])

            def ln_stats(l, i, t, k):
                off, n = TILES[t]
                mean_sb = mean_sb2[:, k, :]
                tmpA = tmpA2[:, k, :]
                R_mean, R_tmpA = R_mean2[k], R_tmpA2[k]
                for c in range(KC):
                    b = c % 2
                    sc.op("act", lambda e, c=c, b=b: e.activation(out=zb[:, b, 0:n], in_=hT[:, c, off:off + n], func=AF.Copy),
                          rd=[R_h[c][t]], wr=[R_zb[b]])
                    sc.op("act", lambda e, c=c, b=b: e.activation(out=zq[:, b, 0:n], in_=hT[:, c, off:off + n], func=AF.Square),
                          rd=[R_h[c][t]], wr=[R_zq[b]])

                    def f(e, c=c, b=b):
                        e.matmul(banks[6][:, 0:n], lhsT=onesd, rhs=zb[:, b, 0:n], start=(c == 0), stop=(c == KC - 1))
                        return e.matmul(banks[7][:, 0:n], lhsT=onesd, rhs=zq[:, b, 0:n], start=(c == 0), stop=(c == KC - 1))
                    sc.op("pe", f, rd=[R_zb[b], R_zq[b], R_cst], wr=[R_bank[6], R_bank[7]])

            def ln_stats2(l, i, t, k):
                off, n = TILES[t]
                mean_sb = mean_sb2[:, k, :]
                tmpA = tmpA2[:, k, :]
                R_mean, R_tmpA = R_mean2[k], R_tmpA2[k]
                sc.op("act", lambda e: e.activation(out=mean_sb[:, 0:n], in_=banks[6][:, 0:n], func=AF.Copy),
                      rd=[R_bank[6]], wr=[R_mean])
                sc.op("dve", lambda e: e.tensor_tensor(out=tmpA[:, 0:n], in0=mean_sb[:, 0:n], in1=mean_sb[:, 0:n], op=ALU.mult),
                      rd=[R_mean], wr=[R_tmpA])
                sc.op("dve", lambda e: e.tensor_tensor(out=tmpA[:, 0:n], in0=banks[7][:, 0:n], in1=tmpA[:, 0:n], op=ALU.subtract),
                      rd=[R_bank[7], R_tmpA], wr=[R_tmpA])
                sc.op("act", lambda e: e.activation(out=tmpA[:, 0:n], in_=tmpA[:, 0:n], func=AF.Ln, bias=eps_col[:, 0:1]),
                      rd=[R_tmpA, R_es], wr=[R_tmpA])
                sc.op("act", lambda e: e.activation(out=tmpA[:, 0:n], in_=tmpA[:, 0:n], func=AF.Exp, scale=-0.5),
                      rd=[R_tmpA], wr=[R_tmpA])

            def ln_apply(l, i, t, k, nxt, late=False):
                off, n = TILES[t]
                mean_sb = mean_sb2[:, k, :]
                tmpA = tmpA2[:, k, :]
                R_mean, R_tmpA = R_mean2[k], R_tmpA2[k]
                for c in range(KC):
                    hv = hT[:, c, off:off + n]
                    sc.op("dve" if late else "pool",
                          lambda e, hv=hv: e.tensor_tensor(out=hv, in0=hv, in1=mean_sb[:, 0:n], op=ALU.subtract),
                          rd=[R_mean, R_h[c][t]], wr=[R_h[c][t]])
                    sc.op("dve", lambda e, hv=hv: e.tensor_tensor(out=hv, in0=hv, in1=tmpA[:, 0:n], op=ALU.mult),
                          rd=[R_tmpA, R_h[c][t]], wr=[R_h[c][t]])
                    sc.op("act", lambda e, hv=hv, c=c: e.activation(out=hv, in_=hv, func=AF.Identity,
                                                                    scale=lng_sb[:, l, i, c:c + 1], bias=lnb_sb[:, l, i, c:c + 1]),
                          rd=[R_ln, R_h[c][t]], wr=[R_h[c][t]])
                if nxt is not None and t in nxt[2]:
                    assert not (modq and (modq[0][0] < nxt[0] or (modq[0][0] == nxt[0] and modq[0][1] < (NMOD // 3) * (nxt[1] + 1))))
                    modulate_tile(nxt[0], nxt[1], t, pool_ok=not late)
                if nxt is None and stop is None:
                    sc.dma("sp", lambda e: e.dma_start(out=outT.rearrange("(c p) t -> p c t", p=128)[:, :, off:off + n],
                                                       in_=hT[:, :, off:off + n]),
                           "outd", rd=[R_h[c][t] for c in range(KC)])

            cnt = {"sa": 0, "av": 0, "y": 0}

            def z_update(l, i, c, t, bank):
                off, n = TILES[t]
                s_ = 1 if t == 4 else 0
                hv = hT[:, c, off:off + n]
                sc.op("dve", lambda e: e.scalar_tensor_tensor(out=hv, in0=banks[bank][:, 0:n], scalar=Gcol[:, l, i, c, s_:s_ + 1],
                                                              in1=hv, op0=ALU.mult, op1=ALU.add),
                      rd=[R_bank[bank], R_cols[l][i], R_h[c][t]], wr=[R_h[c][t]])

            def tail_pipeline(l, i, tiles, nxt, emit_outproj):
                tl = list(tiles)
                for k, t in enumerate(tl):
                    emit_outproj(t)
                    pend[t] = [(lambda t=t, k=k: ln_stats(l, i, t, k % 2)),
                               (lambda t=t, k=k: ln_stats2(l, i, t, k % 2)),
                               (lambda t=t, k=k: ln_apply(l, i, t, k % 2, nxt, late=(k == len(tl) - 1)))]
                    if k >= 1:
                        advance(tl[k - 1])
                    if k >= 2:
                        advance(tl[k - 2])
                    if k >= 1:
                        advance(tl[k - 1])
                advance(tl[-1])
                if len(tl) >= 2:
                    advance(tl[-2])
                advance(tl[-1])

            def ffn(l, w, tiles, nxt):
                i = 0 if w == 0 else 2
                tiles = list(tiles)
                mod_ensure(l, i)
                first = True
                for pi, part in enumerate(PARTS):
                    j0, npart = part
                    for (j, g) in in_groups(part):
                        si = w_get(("win", l, w, j), win_dmas(l, w, j, g))
                        wv = ring_view(si, 0, [KC, 2, g * 128])
                        for jj in range(g):
                            jl = j + jj - j0
                            for t in tiles:
                                flush(t)
                                off, n = TILES[t]
                                k = cnt["av"]; cnt["av"] += 1
                                ba, bv = (k % 2), 2 + (k % 2)

                                def f(e, jj=jj, off=off, n=n, ba=ba, bv=bv, wv=wv):
                                    for kc in range(KC):
                                        e.matmul(banks[ba][:, 0:n], lhsT=wv[:, kc, 0, jj * 128:(jj + 1) * 128],
                                                 rhs=uT[:, kc, off:off + n], start=(kc == 0), stop=(kc == KC - 1))
                                    last = None
                                    for kc in range(KC):
                                        last = e.matmul(banks[bv][:, 0:n], lhsT=wv[:, kc, 1, jj * 128:(jj + 1) * 128],
                                                        rhs=uT[:, kc, off:off + n], start=(kc == 0), stop=(kc == KC - 1))
                                    return last
                                rdx = ffn_arena() if first else []
                                sc.op("pe", f, rd=[R_slot[si], R_u[t]], wr=[R_bank[ba], R_bank[bv]])
                                b = cnt["sa"] % 2; cnt["sa"] += 1
                                sc.op("act", lambda e, b=b, ba=ba, n=n: e.activation(out=sa[:, b, 0:n], in_=banks[ba][:, 0:n], func=AF.Silu),
                                      rd=[R_bank[ba]], wr=[R_sa[b]])
                                sc.op("dve", lambda e, b=b, bv=bv, n=n, jl=jl, off=off: e.tensor_tensor(
                                    out=gT[:, jl, off:off + n], in0=banks[bv][:, 0:n], in1=sa[:, b, 0:n], op=ALU.mult),
                                    rd=[R_bank[bv], R_sa[b]], wr=[R_g[jl][t]])
                                first = False
                        mod_step()
                    si = w_get(("wout", l, w, pi), wout_dmas(l, w, j0, npart))
                    wo = ring_view(si, 0, [npart, D])

                    last_part = (pi == len(PARTS) - 1)

                    def outproj(c, t, si=si, wo=wo, npart=npart, last_part=last_part):
                        off, n = TILES[t]
                        bank = (cnt["y"] % 6) if last_part else (4 + cnt["y"] % 4)
                        cnt["y"] += 1

                        def f(e):
                            last = None
                            for jl in range(npart):
                                last = e.matmul(banks[bank][:, 0:n], lhsT=wo[:, jl, c * 128:(c + 1) * 128],
                                                rhs=gT[:, jl, off:off + n], start=(jl == 0), stop=(jl == npart - 1))
                            return last
                        sc.op("pe", f, rd=[R_slot[si]] + [R_g[jl][t] for jl in range(npart)], wr=[R_bank[bank]])
                        z_update(l, i, c, t, bank)
                    if pi < len(PARTS) - 1:
                        for c in range(KC):
                            for t in tiles:
                                outproj(c, t)
                    else:
                        tail_pipeline(l, i, tiles, nxt, lambda t: [outproj(c, t) for c in range(KC)])
                    mod_step()

            def rs_na(r):
                return min(max(r - 4, 0), ROWS - 8)

            def na_items(qb):
                if qb == 4:
                    return [(32 + k, 0, 256, None) for k in range(4)]
                items = [(32 + k, 0, 512, None) for k in range(4)]
                for kr in range(ROWS):
                    rr = [r for r in range(8 * qb, 8 * qb + 8) if rs_na(r) <= kr <= rs_na(r) + 7]
                    if not rr:
                        continue
                    ra, rb = rr[0], rr[-1]
                    assert rr == list(range(ra, rb + 1))
                    e0 = ra - kr + 7
                    assert 0 <= e0 and e0 + (rb - ra) <= 14
                    items.append((kr, (ra - 8 * qb) * 64, (rb - 8 * qb + 1) * 64, ("tb", e0 * 64, (e0 + rb - ra + 1) * 64)))
                return items

            def wa_items(qb):
                items = [(32 + k, 0, 512, None) for k in range(4)]
                for kt in range(32):
                    lo = 64 * kt - 128
                    g0 = max(lo, 512 * qb, 0)
                    g1 = min(lo + 320, 512 * qb + 512, S)
                    if g1 <= g0:
                        continue
                    items.append((kt, g0 - 512 * qb, g1 - 512 * qb, ("mask", g0 - lo, g1 - lo)))
                return items

            def attention_pair(l, pr, pl, si, qbanks):
                exp_scale = 1.0 if l == 0 else 0.125
                flat = []
                for qi, qb in enumerate(qbanks):
                    its = na_items(qb) if l == 0 else wa_items(qb)
                    for ii, it in enumerate(its):
                        flat.append((qi, qb, ii, len(its), it))
                nI = len(flat)
                sbank = [None] * nI

                def emit_qk(x):
                    qi, qb, ii, nits, (kt, q0, q1, bias) = flat[x]
                    bk = x % 4
                    sbank[x] = bk
                    qoff = TILES[qb][0]
                    n = q1 - q0

                    extra = []
                    if bias is not None:
                        if bias[0] == "tb":
                            extra.append((0, n, ring_view(si, 3072, [960])[:, bias[1]:bias[2]]))
                        else:
                            for (lo, hi) in ((0, 63), (257, 320)):
                                a_, b_ = max(bias[1], lo), min(bias[2], hi)
                                if a_ < b_:
                                    extra.append((a_ - bias[1], b_ - bias[1], maskw[:, a_:b_]))

                    def f(e):
                        last = e.matmul(banks[bk][:, 0:n], lhsT=KBD[:, kt, :], rhs=qT[:, qoff + q0:qoff + q1],
                                        start=True, stop=(len(extra) == 0))
                        for xi, (c0, c1, bap) in enumerate(extra):
                            last = e.matmul(banks[bk][:, c0:c1], lhsT=ident, rhs=bap, start=False,
                                            stop=(xi == len(extra) - 1))
                        return last
                    rd = [R_KBD[kt // 8], R_q[qb], R_cst]
                    if bias is not None:
                        rd.append(R_slot[si] if bias[0] == "tb" else R_mask)
                    sc.op("pe", f, rd=rd, wr=[R_bank[bk]])

                LA = 2
                for x in range(min(LA, nI)):
                    emit_qk(x)
                pending = []
                for x in range(nI):
                    qi, qb, ii, nits, (kt, q0, q1, bias) = flat[x]
                    n = q1 - q0
                    bk = sbank[x]
                    pb = x % 3
                    while pending and pending[0][0] <= x:
                        pending.pop(0)[1]()
                    sc.op("act", lambda e, bk=bk, pb=pb, n=n: e.activation(out=PT[:, pb, 0:n], in_=banks[bk][:, 0:n],
                                                                           func=AF.Exp, scale=exp_scale),
                          rd=[R_bank[bk]], wr=[R_PT[pb]])
                    if x + LA < nI:
                        emit_qk(x + LA)
                    ob = 4 + 2 * (qi % 2)

                    def f(e, kt=kt, q0=q0, q1=q1, n=n, pb=pb, ob=ob, ii=ii, nits=nits):
                        e.matmul(banks[ob][:, q0:q1], lhsT=VBD[:, kt, :], rhs=PT[:, pb, 0:n],
                                 start=(ii == 0), stop=(ii == nits - 1))
                        return e.matmul(banks[ob + 1][:, q0:q1], lhsT=onesbd, rhs=PT[:, pb, 0:n],
                                        start=(ii == 0), stop=(ii == nits - 1))
                    sc.op("pe", f, rd=[R_VBD[kt // 4], R_PT[pb], R_cst], wr=[R_bank[ob], R_bank[ob + 1]])
                    if ii == nits - 1:
                        qoff, nq = TILES[qb]
                        escol = es_sb[:, pr:pr + 1] if l == 1 else zero_col[:, 0:1]

                        def norm(ob=ob, nq=nq, escol=escol, qoff=qoff, qb=qb):
                            sc.op("act", lambda e: e.activation(
                                out=rsb[:, 0, 0:nq], in_=banks[ob + 1][:, 0:nq], func=AF.Ln, bias=escol),
                                rd=[R_bank[ob + 1], R_es], wr=[R_rsb[0]])
                            sc.op("act", lambda e: e.activation(out=rsb[:, 0, 0:nq], in_=rsb[:, 0, 0:nq], func=AF.Exp, scale=-1.0),
                                  rd=[R_rsb[0]], wr=[R_rsb[0]])
                            sc.op("dve", lambda e: e.tensor_tensor(
                                out=OT[:, pl, qoff:qoff + nq], in0=banks[ob][:, 0:nq], in1=rsb[:, 0, 0:nq], op=ALU.mult),
                                rd=[R_bank[ob], R_rsb[0]], wr=[R_OT[pl][qb]])
                        pending.append((x + 3, norm))
                for _, fn in pending:
                    fn()

            def qkv_pair(l, pr, si, qtiles, part):
                pc = {"k": 0}

                def nb():
                    b = pc["k"] % 4
                    pc["k"] += 1
                    return b
                if l == 0:
                    wv = ring_view(si, 0, [KC, 3, 128])
                    wq = lambda kc: wv[:, kc, 0, :]
                    wk = lambda kc: wv[:, kc, 1, :]
                    wvv = lambda kc: wv[:, kc, 2, :]
                elif part == "q":
                    wv = ring_view(si, 0, [KC, 256])
                    wq = lambda kc: wv[:, kc, 0:128]
                    wqp = lambda kc: wv[:, kc, 128:256]
                else:
                    wv = ring_view(si, 0, [KC, 384])
                    wk = lambda kc: wv[:, kc, 0:128]
                    wkp = lambda kc: wv[:, kc, 128:256]
                    wvv = lambda kc: wv[:, kc, 256:384]

                def proj(bank, wf, off, n):
                    def f(e):
                        last = None
                        for kc in range(KC):
                            last = e.matmul(banks[bank][:, 0:n], lhsT=wf(kc), rhs=uT[:, kc, off:off + n],
                                            start=(kc == 0), stop=(kc == KC - 1))
                        return last
                    return f

                def rope(bq, bp, off, n, dst_fn):
                    sc.op("dve", lambda e: e.tensor_tensor(out=t1[:, 0, 0:n], in0=banks[bq][:, 0:n], in1=ropeC[:, off:off + n], op=ALU.mult),
                          rd=[R_bank[bq], R_rope], wr=[R_t1[0]])
                    sc.op("dve", lambda e: e.tensor_tensor(out=t2[:, 0, 0:n], in0=banks[bp][:, 0:n], in1=ropeS[:, off:off + n], op=ALU.mult),
                          rd=[R_bank[bp], R_rope2], wr=[R_t2[0]])
                    dst_fn()

                for t in (qtiles if part == "q" else []):
                    flush(t)
                    off, n = TILES[t]
                    b = nb()
                    sc.op("pe", proj(b, wq, off, n), rd=[R_slot[si], R_u[t]], wr=[R_bank[b]])
                    if l == 0:
                        sc.op("act", lambda e, b=b, off=off, n=n: e.activation(out=qT[:, off:off + n], in_=banks[b][:, 0:n],
                                                                               func=AF.Copy, scale=0.125),
                              rd=[R_bank[b]], wr=[R_q[t]])
                    else:
                        b2 = nb()
                        sc.op("pe", proj(b2, wqp, off, n), rd=[R_slot[si], R_u[t]], wr=[R_bank[b2]])

                        def dst(off=off, n=n, t=t):
                            sc.op("dve", lambda e: e.tensor_tensor(out=qT[:, off:off + n], in0=t1[:, 0, 0:n], in1=t2[:, 0, 0:n], op=ALU.add),
                                  rd=[R_t1[0], R_t2[0]], wr=[R_q[t]])
                        rope(b, b2, off, n, dst)
                if part != "kv":
                    return
                for t in range(5):
                    flush(t)
                    off, n = TILES[t]
                    nk = n // 64
                    kt0 = off // 64
                    b = nb()
                    sc.op("pe", proj(b, wk, off, n), rd=[R_slot[si], R_u[t]], wr=[R_bank[b]])
                    dA = KBD[0:64, kt0:kt0 + nk, 0:64]
                    dB = KBD[64:128, kt0:kt0 + nk, 64:128]
                    if l == 0 or t == 4:
                        sA = banks[b][0:64, 0:n].rearrange("p (k c) -> p k c", c=64)
                        sB = banks[b][64:128, 0:n].rearrange("p (k c) -> p k c", c=64)
                        sc.op("act", lambda e, dA=dA, sA=sA: e.activation(out=dA, in_=sA, func=AF.Copy),
                              rd=[R_bank[b]], wr=[R_KBD[t]])
                        sc.op("dve", lambda e, dB=dB, sB=sB: e.tensor_copy(out=dB, in_=sB), rd=[R_bank[b]], wr=[R_KBD[t]])
                    else:
                        b2 = nb()
                        sc.op("pe", proj(b2, wkp, off, n), rd=[R_slot[si], R_u[t]], wr=[R_bank[b2]])

                        def dst(dA=dA, dB=dB, n=n, t=t):
                            sc.op("dve", lambda e: e.tensor_tensor(
                                out=dA, in0=t1[0:64, 0, 0:n].rearrange("p (k c) -> p k c", c=64),
                                in1=t2[0:64, 0, 0:n].rearrange("p (k c) -> p k c", c=64), op=ALU.add),
                                rd=[R_t1[0], R_t2[0]], wr=[R_KBD[t]])
                            sc.op("dve", lambda e: e.tensor_tensor(
                                out=dB, in0=t1[64:128, 0, 0:n].rearrange("p (k c) -> p k c", c=64),
                                in1=t2[64:128, 0, 0:n].rearrange("p (k c) -> p k c", c=64), op=ALU.add),
                                rd=[R_t1[0], R_t2[0]], wr=[R_KBD[t]])
                        rope(b, b2, off, n, dst)
                for gi in range(9):
                    b = nb()

                    def f(e, gi=gi, b=b):
                        last = None
                        for k4 in range(4):
                            m = gi * 4 + k4
                            if m < 35:
                                for kc in range(KC):
                                    last = e.matmul(banks[b][:, k4 * 128:(k4 + 1) * 128], lhsT=uT[:, kc, 64 * m:64 * m + 128],
                                                    rhs=wvv(kc), start=(kc == 0), stop=(kc == KC - 1))
                            else:
                                for kc in range(KC):
                                    e.matmul(banks[b][0:64, k4 * 128:k4 * 128 + 64], lhsT=uT[:, kc, 64 * m:64 * m + 64],
                                             rhs=wvv(kc)[:, 0:64], start=(kc == 0), stop=(kc == KC - 1))
                                for kc in range(KC):
                                    last = e.matmul(banks[b][64:128, k4 * 128 + 64:k4 * 128 + 128], lhsT=uT[:, kc, 0:64],
                                                    rhs=wvv(kc)[:, 64:128], start=(kc == 0), stop=(kc == KC - 1))
                        return last
                    sc.op("pe", f, rd=[R_slot[si]] + R_u, wr=[R_bank[b]])
                    bv = banks[b][:, :].rearrange("p (k c) -> p k c", c=128)
                    sA = bv[0:64, :, 0:64]
                    dA = VBD[0:64, gi * 4:gi * 4 + 4, 0:64]
                    sc.op("act", lambda e, dA=dA, sA=sA: e.activation(out=dA, in_=sA, func=AF.Copy), rd=[R_bank[b]], wr=[R_VBD[gi]])
                    if gi < 8:
                        sB = bv[64:128, :, 64:128]
                        dB = VBD[64:128, gi * 4 + 1:gi * 4 + 5, 64:128]
                        sc.op("dve", lambda e, dB=dB, sB=sB: e.tensor_copy(out=dB, in_=sB), rd=[R_bank[b]],
                              wr=[R_VBD[gi], R_VBD[gi + 1]])
                    else:
                        sB = bv[64:128, 0:3, 64:128]
                        dB = VBD[64:128, 33:36, 64:128]
                        sc.op("dve", lambda e, dB=dB, sB=sB: e.tensor_copy(out=dB, in_=sB), rd=[R_bank[b]], wr=[R_VBD[8]])
                        sB2 = bv[64:128, 3, 64:128]
                        dB2 = VBD[64:128, 0, 64:128]
                        sc.op("dve", lambda e, dB2=dB2, sB2=sB2: e.tensor_copy(out=dB2, in_=sB2), rd=[R_bank[b]], wr=[R_VBD[0]])

            def mixer(l, ctx_out, nxt):
                i = 1
                mod_ensure(l, i)
                qtiles = list(range(5)) if ctx_out else list(range(4))
                if l == 1:
                    sc.op("dve", lambda e: e.memset(rsb[:, 0, 0:2], 0.0), rd=[], wr=[R_rope, R_rope2, R_mask] + R_rsb + ffn_arena())
                    sc.dma("pool", lambda e: e.dma_start(out=ropeC[:], in_=rope_cs[:, 0, :]), "aux", wr=[R_rope])
                    sc.dma("pool", lambda e: e.dma_start(out=ropeS[:], in_=rope_cs[:, 1, :]), "aux", wr=[R_rope2])
                    sc.dma("pool", lambda e: e.dma_start(out=maskw[:], in_=wa_mask), "aux", wr=[R_mask])
                for pr in range(8):
                    if l == 0:
                        si = w_get(("qkv", l, pr), qkv_dmas(l, pr))
                        qkv_pair(l, pr, si, qtiles, "kv")
                        qkv_pair(l, pr, si, qtiles, "q")
                    else:
                        if pr % 2 == 0:
                            skv = w_get(("kv", l, pr), kv_dmas(pr))
                            qkv_pair(l, pr, skv, qtiles, "kv")
                            mod_step()
                        si = w_get(("qkv", l, pr), qkv_dmas(l, pr))
                        qkv_pair(l, pr, si, qtiles, "q")
                    pl = pr % 2
                    attention_pair(l, pr, pl, si, qtiles)
                    mod_step()
                    if pl == 0:
                        continue
                    so = w_get(("wo", l, pr // 2), wo_dmas(l, pr // 2))
                    wo = ring_view(so, 0, [2, D])
                    lastp = (pr == 7)

                    def outproj(c, t, so=so, wo=wo, lastp=lastp):
                        off, n = TILES[t]
                        bank = (cnt["y"] % 6) if lastp else (4 + cnt["y"] % 4)
                        cnt["y"] += 1

                        def f(e):
                            e.matmul(banks[bank][:, 0:n], lhsT=wo[:, 0, c * 128:(c + 1) * 128], rhs=OT[:, 0, off:off + n],
                                     start=True, stop=False)
                            return e.matmul(banks[bank][:, 0:n], lhsT=wo[:, 1, c * 128:(c + 1) * 128], rhs=OT[:, 1, off:off + n],
                                            start=False, stop=True)
                        sc.op("pe", f, rd=[R_slot[so], R_OT[0][t], R_OT[1][t]], wr=[R_bank[bank]])
                        z_update(l, i, c, t, bank)
                    if not lastp:
                        for c in range(KC):
                            for t in qtiles:
                                outproj(c, t)
                    else:
                        tail_pipeline(l, i, qtiles, nxt, lambda t: [outproj(c, t) for c in range(KC)])
                    mod_step()

            def arena_to_ffn():
                sc.op("dve", lambda e: e.memset(sa[:, 0, 0:2], 0.0), rd=[], wr=att_res() + ffn_arena())

            sc.op("dve", lambda e: e.memset(eps_col[:], LN_EPS), wr=[R_es])
            sc.op("dve", lambda e: e.memset(KBD[:, :, :], 0.0), rd=[], wr=R_KBD)
            sc.op("dve", lambda e: e.memset(VBD[:, :, :], 0.0), rd=[], wr=R_VBD)
            for _ in range(4):
                mod_step()
            for t in range(5):
                modulate_tile(0, 0, t)
            mod_ensure(0, 0)
            seq = [("ffn", 0, 0), ("mix", 0), ("ffn", 0, 1), ("ffn", 1, 0), ("mix", 1), ("ffn", 1, 1)]
            stops = {"ffn00": 1, "mix0": 2, "ffn01": 3, "ffn10": 4, "mix1": 5, None: 6}
            seq = seq[:stops[stop]]
            for k, sub in enumerate(seq):
                nxt = None
                if k + 1 < len(seq):
                    ns = seq[k + 1]
                    if ns[0] == "mix":
                        nxt = (ns[1], 1, list(range(5)))
                    else:
                        ni = 0 if ns[2] == 0 else 2
                        ntl = list(range(4)) if (ns[1] == 1 and ns[2] == 1) else list(range(5))
                        nxt = (ns[1], ni, ntl)
                if sub[0] == "ffn":
                    if k > 0 and seq[k - 1][0] == "mix":
                        arena_to_ffn()
                    tl = range(4) if (sub[1] == 1 and sub[2] == 1) else range(5)
                    ffn(sub[1], sub[2], tl, nxt)
                else:
                    mixer(sub[1], sub[1] == 0, nxt)
            flush_all()

            ntile_out = 5 if stop is not None else 4
            for c in (range(KC) if stop is not None else []):
                sc.dma("sp", lambda e, c=c: e.dma_start(out=outT[c * 128:(c + 1) * 128, :], in_=hT[:, c, 0:n_out_tok]),
                       "outd", rd=[R_h[c][t] for t in range(ntile_out)])

            for coarse in ("init_sp", "aux"):
                tot = sc.cnt[coarse]
                for k in sc.ops:
                    sc.ops[k] = [([(en, (tot if en == coarse else v)) for en, v in waits], fn, sn, c)
                                 for waits, fn, sn, c in sc.ops[k]]
            return rec

        eps_col = sb("eps_col", [128, 1], F32)
        plan = emit(Sched(), None)
        sc = Sched()
        sc.sems = SEMS
        emit(sc, plan)

        block = es.enter_context(nc.Block())

        @block.tensor
        def _(e):
            sc.replay("pe", e)

        @block.scalar
        def _(e):
            sc.replay("act", e)

        @block.vector
        def _(e):
            sc.replay("dve", e)

        @block.gpsimd
        def _(e):
            sc.replay("pool", e)

        @block.sync
        def _(e):
            sc.replay("sp", e, final_waits=[("outd", sc.cnt["outd"])])

    return nc


def _host_tables(na_rpb, wa_sinks):
    kc = np.arange(64)[:, None]
    qc = np.arange(64)[None, :]
    ws = np.clip(qc - 8, 0, 48)
    valid = (kc >= ws) & (kc < ws + 16)
    coff = np.clip(kc - qc, -15, 15) + 15
    rpb = np.asarray(na_rpb[0], dtype=np.float32)
    tb = np.empty((2, 64, 8, 15, 64), np.float32)
    for hh in range(2):
        for e in range(15):
            g = rpb[hh::2][:, 14 - e][:, coff]
            g = np.where(valid[None], g, np.float32(NEG))
            tb[hh, :, :, e, :] = g.transpose(1, 0, 2)
    tb = tb.reshape(128, 8, 15 * 64)
    sk = np.asarray(wa_sinks[0], np.float32)
    wa_sk = np.empty((128, 8), np.float32)
    for p in range(128):
        wa_sk[p] = sk[(p // 64)::2]
    kl = np.arange(64)[:, None]
    j = np.arange(320)[None, :]
    m = np.where((j >= kl) & (j <= kl + 256), np.float32(0), np.float32(NEG)).astype(np.float32)
    wa_mask = np.concatenate([m, m], 0)
    t = np.arange(S)
    rows = (t // GW).astype(np.float32)
    cols = (t % GW).astype(np.float32)
    inv = (np.float32(10000.0) ** (-np.arange(16, dtype=np.float32) / np.float32(16))).astype(np.float32)
    cs = np.empty((128, 2, S), np.float32)
    for p in range(128):
        d = p % 64
        pos = rows if d < 32 else cols
        ang = (pos * inv[d % 16]).astype(np.float32)
        cs[p, 0] = np.cos(ang)
        sn = np.sin(ang)
        cs[p, 1] = -sn if (d % 32) < 16 else sn
    perm = np.empty(64, np.int64)
    for d in range(64):
        perm[d] = d + 16 if (d % 32) < 16 else d - 16
    consts = np.zeros((128, 3, 128), np.float32)
    consts[:, 0, :] = np.eye(128, dtype=np.float32)
    consts[0:64, 1, 0:64] = 1.0
    consts[64:128, 1, 64:128] = 1.0
    consts[:, 2, :] = 1.0 / D
    return tb, wa_sk, wa_mask, cs, perm, consts


_CACHE = {}


def _prep_shared(c_ctx, w_mod, b_mod, ln_g, ln_b, ffn_w_in, ffn_w_out, na_w_qkv, na_w_o, na_rpb,
                 wa_w_qkv, wa_w_o, wa_sinks):
    f = lambda a: np.ascontiguousarray(np.asarray(a, dtype=np.float32))
    tb, wa_sk, wa_mask, cs, perm, consts = _host_tables(np.asarray(na_rpb), np.asarray(wa_sinks))
    wq = f(wa_w_qkv)[0]
    cols = np.concatenate([(h * 64 + perm) for h in range(NH)] + [D + g * 64 + perm for g in range(4)])
    sh = {
        "w_mod": f(w_mod),
        "bmod": f(np.asarray(b_mod).reshape(DEPTH, 72, 128).transpose(2, 0, 1)),
        "lng": f(np.asarray(ln_g).reshape(DEPTH, 3, KC, 128).transpose(3, 0, 1, 2)),
        "lnb": f(np.asarray(ln_b).reshape(DEPTH, 3, KC, 128).transpose(3, 0, 1, 2)),
        "ffn_w_in": f(ffn_w_in),
        "ffn_w_out": f(ffn_w_out),
        "na_w_qkv": f(na_w_qkv)[0],
        "na_w_o": f(na_w_o)[0],
        "na_tb": f(tb),
        "wa_w_qkv": wq,
        "wa_w_perm": f(wq[:, cols]),
        "wa_w_o": f(wa_w_o)[0],
        "wa_sk": f(wa_sk),
        "wa_mask": f(wa_mask),
        "rope_cs": f(cs),
        "consts": f(consts),
    }
    return sh


def run(inputs, stop=None):
    x = np.asarray(inputs["x"], np.float32)
    c = np.asarray(inputs["c"], np.float32)
    ctx = np.asarray(inputs["ctx"], np.float32)
    c_ctx = np.asarray(inputs["c_ctx"], np.float32)
    sh = _prep_shared(c_ctx, *[inputs[k] for k in ("w_mod", "b_mod", "ln_g", "ln_b", "ffn_w_in", "ffn_w_out",
                                                     "na_w_qkv", "na_w_o", "na_rpb", "wa_w_qkv", "wa_w_o", "wa_sinks")])
    key = ("nc", stop)
    if key not in _CACHE:
        _CACHE[key] = build_program(stop)
    nc = _CACHE[key]
    B = x.shape[0]
    in_maps = []
    for b in range(B):
        m = dict(sh)
        m["xT"] = np.ascontiguousarray(x[b].T)
        m["ctxT"] = np.ascontiguousarray(ctx[b].T)
        ccb = np.stack([c[b], c_ctx], -1).reshape(KC, 128, 2).transpose(1, 0, 2)
        m["cc"] = np.ascontiguousarray(ccb)
        in_maps.append(m)
    res = run_bass_kernel_spmd(nc, in_maps, core_ids=list(range(B)))
    outs = [np.asarray(r["outT"]) for r in res.results]
    return np.stack([o.T for o in outs], 0)


def kernel(**inputs):
    out = run(inputs, None)
    return np.ascontiguousarray(out.astype(np.float32))
```

```python
import numpy as np
from contextlib import ExitStack
import concourse.bass as bass
import concourse.mybir as mybir
from concourse.bass_utils import run_bass_kernel_spmd

F32 = mybir.dt.float32
BF16 = mybir.dt.bfloat16
AF = mybir.ActivationFunctionType
ALU = mybir.AluOpType

D = 1024
KC = 8
S = 2048
CT = 256
T = S + CT
FF = 2816
FC = 22
NH = 16
HD = 64
GW = 64
ROWS = 32
DEPTH = 2
ALPHA = (2 * DEPTH) ** 0.25
LN_EPS = 1e-5
NEG = -1e30
SLOT = 4096
NSLOT = 3
LOOKAHEAD = 2
PARTS = [(0, 4), (4, 4), (8, 4), (12, 4), (16, 3), (19, 3)]
MODW = 512
NMOD = 9 * D // MODW
TILES = [(0, 512), (512, 512), (1024, 512), (1536, 512), (2048, 256)]


class Res:
    __slots__ = ("w", "r")

    def __init__(self):
        self.w = None
        self.r = {}


class Sched:
    def __init__(self):
        self.ops = {k: [] for k in ("pe", "act", "dve", "pool", "sp")}
        self.cnt = {k: 0 for k in self.ops}
        self.seen = {k: {} for k in self.ops}
        self.sems = {}
        self.step = {k: 1 for k in self.ops}

    def add_dma_sem(self, name):
        self.cnt[name] = 0
        self.step[name] = 16

    def _deps(self, e, rd, wr):
        deps = {}

        def need(x):
            if x is None:
                return
            en, c = x
            if c > deps.get(en, 0):
                deps[en] = c

        for r in rd:
            need(r.w)
        for w in wr:
            need(w.w)
            for en, c in w.r.items():
                need((en, c))
        waits = []
        seen = self.seen[e]
        for en, c in deps.items():
            if en == e and e == "pe":
                continue
            if seen.get(en, 0) >= c:
                continue
            seen[en] = c
            waits.append((en, c))
        return waits

    def op(self, e, fn, rd=(), wr=()):
        waits = self._deps(e, rd, wr)
        self.cnt[e] += 1
        c = self.cnt[e]
        self.ops[e].append((waits, fn, e, c))
        for r in rd:
            if r.r.get(e, 0) < c:
                r.r[e] = c
        for w in wr:
            w.w = (e, c)
            w.r = {}

    def dma(self, q, fn, dsem, rd=(), wr=()):
        waits = self._deps(q, rd, wr)
        self.cnt[dsem] += 16
        c = self.cnt[dsem]
        self.ops[q].append((waits, fn, dsem, c))
        for r in rd:
            if r.r.get(dsem, 0) < c:
                r.r[dsem] = c
        for w in wr:
            w.w = (dsem, c)
            w.r = {}

    def replay(self, name, eng, final_waits=()):
        for waits, fn, sname, c in self.ops[name]:
            for en, v in waits:
                eng.wait_ge(self.sems[en], v)
            inst = fn(eng)
            inst.then_inc(self.sems[sname], self.step[sname])
        for en, v in final_waits:
            eng.wait_ge(self.sems[en], v)


def build_program(stop=None):
    nc = bass.Bass("TRN2", target_bir_lowering=False)

    def din(name, shape):
        return nc.dram_tensor(name, list(shape), F32, kind="ExternalInput").ap()

    xT = din("xT", [D, S])
    ctxT = din("ctxT", [D, CT])
    cc = din("cc", [128, KC, 2])
    w_mod = din("w_mod", [DEPTH, D, 9 * D])
    bmod = din("bmod", [128, DEPTH, 72])
    lng = din("lng", [128, DEPTH, 3, KC])
    lnb = din("lnb", [128, DEPTH, 3, KC])
    w_in = din("ffn_w_in", [DEPTH, 2, D, 2 * FF])
    w_out = din("ffn_w_out", [DEPTH, 2, FF, D])
    na_qkv = din("na_w_qkv", [D, 3 * D])
    na_o = din("na_w_o", [D, D])
    na_tb = din("na_tb", [128, 8, 15 * 64])
    wa_qkv = din("wa_w_qkv", [D, D + 512])
    wa_perm = din("wa_w_perm", [D, D + 256])
    wa_o = din("wa_w_o", [D, D])
    wa_sk = din("wa_sk", [128, 8])
    wa_mask = din("wa_mask", [128, 320])
    rope_cs = din("rope_cs", [128, 2, S])
    consts = din("consts", [128, 3, 128])
    n_out_tok = T if stop is not None else S
    outT = nc.dram_tensor("outT", [D, n_out_tok], F32, kind="ExternalOutput").ap()

    es = ExitStack()
    with es:
        def sb(name, shape, dt):
            return es.enter_context(nc.sbuf_tensor(name, list(shape), dt))

        hT = sb("hT", [128, KC, T], F32)
        uT = sb("uT", [128, KC, T], BF16)
        ring = sb("ring", [128, NSLOT, SLOT], BF16)
        ARENA = 4 * T + 2 * 1024
        ARENA_N = 25152
        arena = sb("arena", [128, ARENA_N], BF16)
        zb = sb("zb", [128, 2, 512], BF16)
        zq = sb("zq", [128, 2, 512], BF16)
        mean_sb2 = sb("mean_sb", [128, 2, 512], F32)
        tmpA2 = sb("tmpA", [128, 2, 512], F32)
        cst = sb("cst", [128, 3, 128], BF16)
        ccs = sb("ccs", [128, KC, 2], F32)
        scb = sb("scb", [128, KC, 2], BF16)
        bmod_sb = sb("bmod_sb", [128, DEPTH, 72], F32)
        raw = sb("raw", [128, DEPTH, 72, 2], F32)
        Acol = sb("Acol", [128, DEPTH, 3, KC, 2], F32)
        Gcol = sb("Gcol", [128, DEPTH, 3, KC, 2], F32)
        lng_sb = sb("lng_sb", [128, DEPTH, 3, KC], F32)
        lnb_sb = sb("lnb_sb", [128, DEPTH, 3, KC], F32)
        sk_sb = sb("sk_sb", [128, 8], F32)
        es_sb = sb("es_sb", [128, 8], F32)
        zero_col = sb("zero_col", [128, 1], F32)
        banks = [es.enter_context(nc.psum_tensor(f"bank{i}", [128, 512], F32)) for i in range(8)]

        ident = cst[:, 0, :]
        onesbd = cst[:, 1, :]
        onesd = cst[:, 2, :]

        def shaped(flat, shape):
            if len(shape) == 1:
                return flat
            names = " ".join(f"a{i}" for i in range(len(shape)))
            kw = {f"a{i}": int(s_) for i, s_ in enumerate(shape)}
            return flat.rearrange(f"p ({names}) -> p {names}", **kw)

        def aview(off, shape, dt=BF16):
            n = int(np.prod(shape))
            if dt == F32:
                assert off % 2 == 0
                flat = arena[:, off:off + 2 * n].bitcast(F32)
            else:
                flat = arena[:, off:off + n]
            return shaped(flat, shape)

        def ring_view(si, off, shape):
            n = int(np.prod(shape))
            return shaped(ring[:, si, off:off + n], shape)

        gT = aview(0, [4, T])
        sa = aview(4 * T, [2, 512], F32)
        o = 0
        qT = aview(o, [T]); o += T
        OT = aview(o, [2, T]); o += 2 * T
        PT = aview(o, [3, 512]); o += 3 * 512
        rsb = aview(o, [1, 512], F32); o += 1024
        t1 = aview(o, [1, 512], F32); o += 1024
        t2 = aview(o, [1, 512], F32); o += 1024
        ropeC = aview(o, [S]); o += S
        ropeS = aview(o, [S]); o += S
        maskw = aview(o, [320]); o += 320
        assert o >= ARENA
        KBD = aview(o, [36, 128]); o += 36 * 128
        VBD = aview(o, [36, 128]); o += 36 * 128
        assert o <= ARENA_N, o
        assert ARENA <= ARENA_N

        SEMS = {}
        for k in ("pe", "act", "dve", "pool", "sp"):
            SEMS[k] = es.enter_context(nc.semaphore(f"s_{k}"))
        dma_names = [f"slot{i}" for i in range(NSLOT)] + ["init_sp", "init_pool", "aux", "outd"] + [f"init_x{t}" for t in range(5)]
        for k in dma_names:
            SEMS[k] = es.enter_context(nc.semaphore(f"s_{k}"))

        def wrows(ap2d):
            return ap2d.rearrange("(c p) f -> p c f", p=128)

        def in_groups(part):
            j0, n = part
            gs = []
            j = j0
            while j < j0 + n:
                g = min(2, j0 + n - j)
                gs.append((j, g))
                j += g
            return gs

        def emit(sc, plan):
            record = plan is None
            rec = []
            for k in dma_names:
                sc.add_dma_sem(k)

            R_h = [[Res() for _ in TILES] for _ in range(KC)]
            R_u = [Res() for _ in TILES]
            R_g = [[Res() for _ in TILES] for _ in range(4)]
            R_slot = [Res() for _ in range(NSLOT)]
            R_bank = [Res() for _ in range(8)]
            R_sa = [Res(), Res()]
            R_zb = [Res(), Res()]
            R_zq = [Res(), Res()]
            R_mean2 = [Res(), Res()]
            R_tmpA2 = [Res(), Res()]
            R_cst = Res()
            R_small = Res()
            R_scb = Res()
            R_raw = [Res() for _ in range(DEPTH)]
            R_cols = [[Res() for _ in range(3)] for _ in range(DEPTH)]
            R_colA = [[Res() for _ in range(3)] for _ in range(DEPTH)]
            R_ln = Res()
            R_es = Res()
            R_KBD = [Res() for _ in range(5)]
            R_VBD = [Res() for _ in range(9)]
            R_q = [Res() for _ in TILES]
            R_OT = [[Res() for _ in TILES] for _ in range(2)]
            R_PT = [Res() for _ in range(3)]
            R_rsb = [Res()]
            R_t1 = [Res()]
            R_t2 = [Res()]
            R_rope = Res()
            R_rope2 = Res()
            R_mask = Res()

            def h_res(tiles, cs=range(KC)):
                return [R_h[c][t] for c in cs for t in tiles]

            def ffn_arena():
                return [r for row in R_g for r in row] + R_sa

            def att_res():
                return (R_q + R_OT[0] + R_OT[1] + R_PT + R_rsb + R_t1 + R_t2 + [R_rope, R_rope2, R_mask])

            wstate = {"issued": 0, "next": 0}

            def w_issue_upto(n):
                while wstate["issued"] < min(n, len(plan)):
                    i = wstate["issued"]
                    si = i % NSLOT
                    key, dmas_fn = plan[i]
                    for dst, src in dmas_fn(si):
                        sc.dma("pool", (lambda e, dst=dst, src=src: e.dma_start(out=dst, in_=src)),
                               f"slot{si}", wr=[R_slot[si]])
                    wstate["issued"] += 1

            def w_get(key, dmas_fn):
                i = wstate["next"]
                wstate["next"] += 1
                if record:
                    rec.append((key, dmas_fn))
                    return i % NSLOT
                assert plan[i][0] == key, (plan[i][0], key)
                w_issue_upto(i + 1 + LOOKAHEAD)
                return i % NSLOT

            def mod_dmas(l, i):
                src = wrows(w_mod[l][:, i * MODW:(i + 1) * MODW])
                return lambda si: [(ring_view(si, 0, [KC, MODW]), src)]

            def win_dmas(l, w, j, g):
                sa_ = wrows(w_in[l, w][:, j * 128:(j + g) * 128])
                sv_ = wrows(w_in[l, w][:, FF + j * 128:FF + (j + g) * 128])
                return lambda si: [(ring_view(si, 0, [KC, 2, g * 128])[:, :, 0, :], sa_),
                                   (ring_view(si, 0, [KC, 2, g * 128])[:, :, 1, :], sv_)]

            def wout_dmas(l, w, j0, n):
                so = w_out[l, w][j0 * 128:(j0 + n) * 128, :].rearrange("(j p) d -> p j d", p=128)
                return lambda si: [(ring_view(si, 0, [n, D]), so)]

            def qkv_dmas(l, pr):
                if l == 0:
                    def f(si):
                        dm = []
                        for i in range(3):
                            src = wrows(na_qkv[:, i * D + pr * 128:i * D + (pr + 1) * 128])
                            dm.append((ring_view(si, 0, [KC, 3, 128])[:, :, i, :], src))
                        dm.append((ring_view(si, 3072, [960]), na_tb[:, pr, :]))
                        return dm
                    return f
                def f(si):
                    rv = ring_view(si, 0, [KC, 256])
                    return [(rv[:, :, 0:128], wrows(wa_qkv[:, pr * 128:(pr + 1) * 128])),
                            (rv[:, :, 128:256], wrows(wa_perm[:, pr * 128:(pr + 1) * 128]))]
                return f

            def kv_dmas(pr):
                g = pr // 2

                def f(si):
                    rv = ring_view(si, 0, [KC, 384])
                    kq = wrows(wa_qkv[:, D + g * 64:D + (g + 1) * 64])
                    kp = wrows(wa_perm[:, D + g * 64:D + (g + 1) * 64])
                    vq = wrows(wa_qkv[:, D + 256 + g * 64:D + 256 + (g + 1) * 64])
                    return [(rv[:, :, 0:64], kq), (rv[:, :, 64:128], kq),
                            (rv[:, :, 128:192], kp), (rv[:, :, 192:256], kp),
                            (rv[:, :, 256:320], vq), (rv[:, :, 320:384], vq)]
                return f

            def wo_dmas(l, q):
                wo = na_o if l == 0 else wa_o
                src = wo[2 * q * 128:(2 * q + 2) * 128, :].rearrange("(j p) d -> p j d", p=128)
                return lambda si: [(ring_view(si, 0, [2, D]), src)]

            sc.dma("pool", lambda e: e.dma_start(out=cst[:], in_=consts), "init_pool", wr=[R_cst])
            sc.dma("sp", lambda e: e.dma_start(out=ccs[:], in_=cc), "init_sp", wr=[Res()])
            sc.dma("sp", lambda e: e.dma_start(out=bmod_sb[:], in_=bmod), "init_sp", wr=[Res()])
            sc.dma("sp", lambda e: e.dma_start(out=lng_sb[:], in_=lng), "init_sp", wr=[Res()])
            sc.dma("sp", lambda e: e.dma_start(out=lnb_sb[:], in_=lnb), "init_sp", wr=[Res()])
            sc.dma("sp", lambda e: e.dma_start(out=sk_sb[:], in_=wa_sk), "init_sp", wr=[Res()])
            R_small.w = ("init_sp", sc.cnt["init_sp"])
            for t in range(4):
                off_, n_ = TILES[t]
                sc.dma("sp", lambda e, off_=off_, n_=n_: e.dma_start(
                    out=hT[:, :, off_:off_ + n_], in_=xT.rearrange("(c p) t -> p c t", p=128)[:, :, off_:off_ + n_]),
                    f"init_x{t}", wr=[R_h[c][t] for c in range(KC)])
            sc.dma("sp", lambda e: e.dma_start(out=hT[:, :, S:T], in_=ctxT.rearrange("(c p) t -> p c t", p=128)),
                   "init_x4", wr=[R_h[c][4] for c in range(KC)])
            if not record:
                w_issue_upto(LOOKAHEAD)

            sc.op("act", lambda e: e.activation(out=scb[:], in_=ccs[:], func=AF.Silu), rd=[R_small], wr=[R_scb])
            sc.op("dve", lambda e: e.memset(zero_col[:], 0.0), wr=[R_es])
            sc.op("act", lambda e: e.activation(out=es_sb[:], in_=sk_sb[:], func=AF.Exp), rd=[R_small, R_es], wr=[R_es])
            for l in range(DEPTH):
                for i in range(3):
                    a = 1.0 if (l == DEPTH - 1 and i == 2) else ALPHA
                    sc.op("dve", lambda e, l=l, i=i, a=a: e.tensor_scalar(
                        out=lng_sb[:, l, i, :], in0=lng_sb[:, l, i, :], scalar1=a, scalar2=None, op0=ALU.mult),
                        rd=[R_small], wr=[R_ln])
                    sc.op("dve", lambda e, l=l, i=i, a=a: e.tensor_scalar(
                        out=lnb_sb[:, l, i, :], in0=lnb_sb[:, l, i, :], scalar1=a, scalar2=None, op0=ALU.mult),
                        rd=[R_small], wr=[R_ln])
            for t in range(5):
                off_, n_ = TILES[t]
                if t % 2 == 0:
                    sc.op("act", lambda e, off_=off_, n_=n_: e.activation(out=hT[:, :, off_:off_ + n_], in_=hT[:, :, off_:off_ + n_],
                                                                          func=AF.Copy, scale=ALPHA),
                          rd=[], wr=h_res([t]))
                else:
                    sc.op("dve", lambda e, off_=off_, n_=n_: e.tensor_scalar(out=hT[:, :, off_:off_ + n_], in0=hT[:, :, off_:off_ + n_],
                                                                             scalar1=ALPHA, scalar2=None, op0=ALU.mult),
                          rd=[], wr=h_res([t]))

            modq = [(l, i) for l in range(DEPTH) for i in range(NMOD)]

            def mod_step():
                if not modq:
                    return
                l, i = modq.pop(0)
                bk = 6
                si = w_get(("mod", l, i), mod_dmas(l, i))
                wv = ring_view(si, 0, [KC, MODW])
                NJ = MODW // 128

                def f(e):
                    last = None
                    for jj in range(NJ):
                        for kc in range(KC):
                            last = e.matmul(banks[bk][:, 2 * jj:2 * jj + 2], lhsT=wv[:, kc, jj * 128:(jj + 1) * 128],
                                            rhs=scb[:, kc, :], start=(kc == 0), stop=(kc == KC - 1))
                    return last
                sc.op("pe", f, rd=[R_slot[si], R_scb], wr=[R_bank[bk]])
                sc.op("dve", lambda e: e.tensor_tensor(
                    out=raw[:, l, NJ * i:NJ * i + NJ, :], in0=banks[bk][:, 0:2 * NJ].rearrange("p (j s) -> p j s", s=2),
                    in1=bmod_sb[:, l, NJ * i:NJ * i + NJ].unsqueeze(2).broadcast_to([128, NJ, 2]), op=ALU.add),
                    rd=[R_bank[bk], R_small], wr=[R_raw[l]])
                LPS = NMOD // 3
                ii = i // LPS
                n0 = 3 * ii
                if i % LPS == 3:
                    sc.op("dve", lambda e: e.tensor_scalar(
                        out=Acol[:, l, ii, :, :], in0=raw[:, l, (n0 + 1) * 8:(n0 + 2) * 8, :], scalar1=1.0,
                        scalar2=1.0 / ALPHA, op0=ALU.add, op1=ALU.mult), rd=[R_raw[l]], wr=[R_colA[l][ii]])
                if i % LPS == LPS - 1:
                    wres = 1.0 if ii == 1 else 0.5
                    sc.op("dve", lambda e: e.tensor_scalar(
                        out=Gcol[:, l, ii, :, :], in0=raw[:, l, (n0 + 2) * 8:(n0 + 3) * 8, :], scalar1=wres,
                        scalar2=None, op0=ALU.mult), rd=[R_raw[l]], wr=[R_cols[l][ii]])

            def mod_ensure(l, ii):
                while modq and (modq[0][0] < l or (modq[0][0] == l and modq[0][1] < (NMOD // 3) * (ii + 1))):
                    mod_step()

            pend = {}

            def advance(t):
                lst = pend.get(t)
                if lst:
                    lst.pop(0)()
                    if not lst:
                        del pend[t]

            def flush(t):
                while t in pend:
                    advance(t)

            def flush_all():
                for t in sorted(list(pend.keys())):
                    flush(t)

            def modulate_tile(l, i, t, pool_ok=True):
                n0 = 3 * i
                off, n = TILES[t]
                s_ = 1 if t == 4 else 0
                for c in range(KC):
                    if (c < 2 or (not pool_ok and c < 5)) and t != 4:
                        sc.op("dve", lambda e, c=c: e.tensor_scalar(
                            out=uT[:, c, off:off + n], in0=hT[:, c, off:off + n], scalar1=Acol[:, l, i, c, s_:s_ + 1],
                            scalar2=raw[:, l, n0 * 8 + c, s_:s_ + 1], op0=ALU.mult, op1=ALU.add),
                            rd=[R_h[c][t], R_colA[l][i], R_raw[l]], wr=[R_u[t]])
                    elif t == 4 or not pool_ok:
                        sc.op("act", lambda e, c=c: e.activation(
                            out=uT[:, c, off:off + n], in_=hT[:, c, off:off + n], func=AF.Identity,
                            scale=Acol[:, l, i, c, s_:s_ + 1], bias=raw[:, l, n0 * 8 + c, s_:s_ + 1]),
                            rd=[R_h[c][t], R_colA[l][i], R_raw[l]], wr=[R_u[t]])
                    else:
                        sc.op("pool", lambda e, c=c: e.tensor_scalar(
                            out=uT[:, c, off:off + n], in0=hT[:, c, off:off + n], scalar1=Acol[:, l, i, c, s_:s_ + 1],
                            scalar2=raw[:, l, n0 * 8 + c, s_:s_ + 1], op0=ALU.mult, op1=ALU.add),
                            rd=[R_h[c][t], R_colA[l][i], R_raw[l]], wr=[R_u[t]])

            def ln_stats(l, i, t, k):
                off, n = TILES[t]
                mean_sb = mean_sb2[:, k, :]
                tmpA = tmpA2[:, k, :]
                R_mean, R_tmpA = R_mean2[k], R_tmpA2[k]
                for c in range(KC):
                    b = c % 2
                    sc.op("act", lambda e, c=c, b=b: e.activation(out=zb[:, b, 0:n], in_=hT[:, c, off:off + n], func=AF.Copy),
                          rd=[R_h[c][t]], wr=[R_zb[b]])
                    sc.op("act", lambda e, c=c, b=b: e.activation(out=zq[:, b, 0:n], in_=hT[:, c, off:off + n], func=AF.Square),
                          rd=[R_h[c][t]], wr=[R_zq[b]])

                    def f(e, c=c, b=b):
                        e.matmul(banks[6][:, 0:n], lhsT=onesd, rhs=zb[:, b, 0:n], start=(c == 0), stop=(c == KC - 1))
                        return e.matmul(banks[7][:, 0:n], lhsT=onesd, rhs=zq[:, b, 0:n], start=(c == 0), stop=(c == KC - 1))
                    sc.op("pe", f, rd=[R_zb[b], R_zq[b], R_cst], wr=[R_bank[6], R_bank[7]])

            def ln_stats2(l, i, t, k):
                off, n = TILES[t]
                mean_sb = mean_sb2[:, k, :]
                tmpA = tmpA2[:, k, :]
                R_mean, R_tmpA = R_mean2[k], R_tmpA2[k]
                sc.op("act", lambda e: e.activation(out=mean_sb[:, 0:n], in_=banks[6][:, 0:n], func=AF.Copy),
                      rd=[R_bank[6]], wr=[R_mean])
                sc.op("dve", lambda e: e.tensor_tensor(out=tmpA[:, 0:n], in0=mean_sb[:, 0:n], in1=mean_sb[:, 0:n], op=ALU.mult),
                      rd=[R_mean], wr=[R_tmpA])
                sc.op("dve", lambda e: e.tensor_tensor(out=tmpA[:, 0:n], in0=banks[7][:, 0:n], in1=tmpA[:, 0:n], op=ALU.subtract),
                      rd=[R_bank[7], R_tmpA], wr=[R_tmpA])
                sc.op("act", lambda e: e.activation(out=tmpA[:, 0:n], in_=tmpA[:, 0:n], func=AF.Ln, bias=eps_col[:, 0:1]),
                      rd=[R_tmpA, R_es], wr=[R_tmpA])
                sc.op("act", lambda e: e.activation(out=tmpA[:, 0:n], in_=tmpA[:, 0:n], func=AF.Exp, scale=-0.5),
                      rd=[R_tmpA], wr=[R_tmpA])

            def ln_apply(l, i, t, k, nxt, late=False):
                off, n = TILES[t]
                mean_sb = mean_sb2[:, k, :]
                tmpA = tmpA2[:, k, :]
                R_mean, R_tmpA = R_mean2[k], R_tmpA2[k]
                for c in range(KC):
                    hv = hT[:, c, off:off + n]
                    sc.op("dve" if late else "pool",
                          lambda e, hv=hv: e.tensor_tensor(out=hv, in0=hv, in1=mean_sb[:, 0:n], op=ALU.subtract),
                          rd=[R_mean, R_h[c][t]], wr=[R_h[c][t]])
                    sc.op("dve", lambda e, hv=hv: e.tensor_tensor(out=hv, in0=hv, in1=tmpA[:, 0:n], op=ALU.mult),
                          rd=[R_tmpA, R_h[c][t]], wr=[R_h[c][t]])
                    sc.op("act", lambda e, hv=hv, c=c: e.activation(out=hv, in_=hv, func=AF.Identity,
                                                                    scale=lng_sb[:, l, i, c:c + 1], bias=lnb_sb[:, l, i, c:c + 1]),
                          rd=[R_ln, R_h[c][t]], wr=[R_h[c][t]])
                if nxt is not None and t in nxt[2]:
                    assert not (modq and (modq[0][0] < nxt[0] or (modq[0][0] == nxt[0] and modq[0][1] < (NMOD // 3) * (nxt[1] + 1))))
                    modulate_tile(nxt[0], nxt[1], t, pool_ok=not late)
                if nxt is None and stop is None:
                    sc.dma("sp", lambda e: e.dma_start(out=outT.rearrange("(c p) t -> p c t", p=128)[:, :, off:off + n],
                                                       in_=hT[:, :, off:off + n]),
                           "outd", rd=[R_h[c][t] for c in range(KC)])

            cnt = {"sa": 0, "av": 0, "y": 0}

            def z_update(l, i, c, t, bank):
                off, n = TILES[t]
                s_ = 1 if t == 4 else 0
                hv = hT[:, c, off:off + n]
                sc.op("dve", lambda e: e.scalar_tensor_tensor(out=hv, in0=banks[bank][:, 0:n], scalar=Gcol[:, l, i, c, s_:s_ + 1],
                                                              in1=hv, op0=ALU.mult, op1=ALU.add),
                      rd=[R_bank[bank], R_cols[l][i], R_h[c][t]], wr=[R_h[c][t]])

            def tail_pipeline(l, i, tiles, nxt, emit_outproj):
                tl = list(tiles)
                for k, t in enumerate(tl):
                    emit_outproj(t)
                    pend[t] = [(lambda t=t, k=k: ln_stats(l, i, t, k % 2)),
                               (lambda t=t, k=k: ln_stats2(l, i, t, k % 2)),
                               (lambda t=t, k=k: ln_apply(l, i, t, k % 2, nxt, late=(k == len(tl) - 1)))]
                    if k >= 1:
                        advance(tl[k - 1])
                    if k >= 2:
                        advance(tl[k - 2])
                    if k >= 1:
                        advance(tl[k - 1])
                advance(tl[-1])
                if len(tl) >= 2:
                    advance(tl[-2])
                advance(tl[-1])

            def ffn(l, w, tiles, nxt):
                i = 0 if w == 0 else 2
                tiles = list(tiles)
                mod_ensure(l, i)
                first = True
                for pi, part in enumerate(PARTS):
                    j0, npart = part
                    for (j, g) in in_groups(part):
                        si = w_get(("win", l, w, j), win_dmas(l, w, j, g))
                        wv = ring_view(si, 0, [KC, 2, g * 128])
                        for jj in range(g):
                            jl = j + jj - j0
                            for t in tiles:
                                flush(t)
                                off, n = TILES[t]
                                k = cnt["av"]; cnt["av"] += 1
                                ba, bv = (k % 2), 2 + (k % 2)

                                def f(e, jj=jj, off=off, n=n, ba=ba, bv=bv, wv=wv):
                                    for kc in range(KC):
                                        e.matmul(banks[ba][:, 0:n], lhsT=wv[:, kc, 0, jj * 128:(jj + 1) * 128],
                                                 rhs=uT[:, kc, off:off + n], start=(kc == 0), stop=(kc == KC - 1))
                                    last = None
                                    for kc in range(KC):
                                        last = e.matmul(banks[bv][:, 0:n], lhsT=wv[:, kc, 1, jj * 128:(jj + 1) * 128],
                                                        rhs=uT[:, kc, off:off + n], start=(kc == 0), stop=(kc == KC - 1))
                                    return last
                                rdx = ffn_arena() if first else []
                                sc.op("pe", f, rd=[R_slot[si], R_u[t]], wr=[R_bank[ba], R_bank[bv]])
                                b = cnt["sa"] % 2; cnt["sa"] += 1
                                sc.op("act", lambda e, b=b, ba=ba, n=n: e.activation(out=sa[:, b, 0:n], in_=banks[ba][:, 0:n], func=AF.Silu),
                                      rd=[R_bank[ba]], wr=[R_sa[b]])
                                sc.op("dve", lambda e, b=b, bv=bv, n=n, jl=jl, off=off: e.tensor_tensor(
                                    out=gT[:, jl, off:off + n], in0=banks[bv][:, 0:n], in1=sa[:, b, 0:n], op=ALU.mult),
                                    rd=[R_bank[bv], R_sa[b]], wr=[R_g[jl][t]])
                                first = False
                        mod_step()
                    si = w_get(("wout", l, w, pi), wout_dmas(l, w, j0, npart))
                    wo = ring_view(si, 0, [npart, D])

                    last_part = (pi == len(PARTS) - 1)

                    def outproj(c, t, si=si, wo=wo, npart=npart, last_part=last_part):
                        off, n = TILES[t]
                        bank = (cnt["y"] % 6) if last_part else (4 + cnt["y"] % 4)
                        cnt["y"] += 1

                        def f(e):
                            last = None
                            for jl in range(npart):
                                last = e.matmul(banks[bank][:, 0:n], lhsT=wo[:, jl, c * 128:(c + 1) * 128],
                                                rhs=gT[:, jl, off:off + n], start=(jl == 0), stop=(jl == npart - 1))
                            return last
                        sc.op("pe", f, rd=[R_slot[si]] + [R_g[jl][t] for jl in range(npart)], wr=[R_bank[bank]])
                        z_update(l, i, c, t, bank)
                    if pi < len(PARTS) - 1:
                        for c in range(KC):
                            for t in tiles:
                                outproj(c, t)
                    else:
                        tail_pipeline(l, i, tiles, nxt, lambda t: [outproj(c, t) for c in range(KC)])
                    mod_step()

            def rs_na(r):
                return min(max(r - 4, 0), ROWS - 8)

            def na_items(qb):
                if qb == 4:
                    return [(32 + k, 0, 256, None) for k in range(4)]
                items = [(32 + k, 0, 512, None) for k in range(4)]
                for kr in range(ROWS):
                    rr = [r for r in range(8 * qb, 8 * qb + 8) if rs_na(r) <= kr <= rs_na(r) + 7]
                    if not rr:
                        continue
                    ra, rb = rr[0], rr[-1]
                    assert rr == list(range(ra, rb + 1))
                    e0 = ra - kr + 7
                    assert 0 <= e0 and e0 + (rb - ra) <= 14
                    items.append((kr, (ra - 8 * qb) * 64, (rb - 8 * qb + 1) * 64, ("tb", e0 * 64, (e0 + rb - ra + 1) * 64)))
                return items

            def wa_items(qb):
                items = [(32 + k, 0, 512, None) for k in range(4)]
                for kt in range(32):
                    lo = 64 * kt - 128
                    g0 = max(lo, 512 * qb, 0)
                    g1 = min(lo + 320, 512 * qb + 512, S)
                    if g1 <= g0:
                        continue
                    items.append((kt, g0 - 512 * qb, g1 - 512 * qb, ("mask", g0 - lo, g1 - lo)))
                return items

            def attention_pair(l, pr, pl, si, qbanks):
                exp_scale = 1.0 if l == 0 else 0.125
                flat = []
                for qi, qb in enumerate(qbanks):
                    its = na_items(qb) if l == 0 else wa_items(qb)
                    for ii, it in enumerate(its):
                        flat.append((qi, qb, ii, len(its), it))
                nI = len(flat)
                sbank = [None] * nI

                def emit_qk(x):
                    qi, qb, ii, nits, (kt, q0, q1, bias) = flat[x]
                    bk = x % 4
                    sbank[x] = bk
                    qoff = TILES[qb][0]
                    n = q1 - q0

                    def f(e):
                        return e.matmul(banks[bk][:, 0:n], lhsT=KBD[:, kt, :], rhs=qT[:, qoff + q0:qoff + q1],
                                        start=True, stop=True)
                    sc.op("pe", f, rd=[R_KBD[kt // 8], R_q[qb]], wr=[R_bank[bk]])

                LA = 3
                for x in range(min(LA, nI)):
                    emit_qk(x)
                pending = []
                for x in range(nI):
                    qi, qb, ii, nits, (kt, q0, q1, bias) = flat[x]
                    n = q1 - q0
                    bk = sbank[x]
                    pb = x % 3
                    while pending and pending[0][0] <= x:
                        pending.pop(0)[1]()
                    sc.op("act", lambda e, bk=bk, pb=pb, n=n: e.activation(out=PT[:, pb, 0:n], in_=banks[bk][:, 0:n],
                                                                           func=AF.Exp, scale=exp_scale),
                          rd=[R_bank[bk]], wr=[R_PT[pb]])
                    if bias is not None:
                        if bias[0] == "tb":
                            ext = [(0, n, ring_view(si, 3072, [960])[:, bias[1]:bias[2]], R_slot[si])]
                        else:
                            ext = []
                            for (lo, hi) in ((0, 63), (257, 320)):
                                a_, b_ = max(bias[1], lo), min(bias[2], hi)
                                if a_ < b_:
                                    ext.append((a_ - bias[1], b_ - bias[1], maskw[:, a_:b_], R_mask))
                        for (c0, c1, tab, rtab) in ext:
                            sc.op("dve", lambda e, pb=pb, c0=c0, c1=c1, tab=tab: e.tensor_tensor(
                                out=PT[:, pb, c0:c1], in0=PT[:, pb, c0:c1], in1=tab, op=ALU.mult),
                                rd=[rtab, R_PT[pb]], wr=[R_PT[pb]])
                    if x + LA < nI:
                        emit_qk(x + LA)
                    ob = 4 + 2 * (qi % 2)

                    def f(e, kt=kt, q0=q0, q1=q1, n=n, pb=pb, ob=ob, ii=ii, nits=nits):
                        e.matmul(banks[ob][:, q0:q1], lhsT=VBD[:, kt, :], rhs=PT[:, pb, 0:n],
                                 start=(ii == 0), stop=(ii == nits - 1))
                        return e.matmul(banks[ob + 1][:, q0:q1], lhsT=onesbd, rhs=PT[:, pb, 0:n],
                                        start=(ii == 0), stop=(ii == nits - 1))
                    sc.op("pe", f, rd=[R_VBD[kt // 4], R_PT[pb], R_cst], wr=[R_bank[ob], R_bank[ob + 1]])
                    if ii == nits - 1:
                        qoff, nq = TILES[qb]
                        escol = es_sb[:, pr:pr + 1] if l == 1 else zero_col[:, 0:1]

                        def norm(ob=ob, nq=nq, escol=escol, qoff=qoff, qb=qb):
                            sc.op("act", lambda e: e.activation(
                                out=rsb[:, 0, 0:nq], in_=banks[ob + 1][:, 0:nq], func=AF.Ln, bias=escol),
                                rd=[R_bank[ob + 1], R_es], wr=[R_rsb[0]])
                            sc.op("act", lambda e: e.activation(out=rsb[:, 0, 0:nq], in_=rsb[:, 0, 0:nq], func=AF.Exp, scale=-1.0),
                                  rd=[R_rsb[0]], wr=[R_rsb[0]])
                            sc.op("dve", lambda e: e.tensor_tensor(
                                out=OT[:, pl, qoff:qoff + nq], in0=banks[ob][:, 0:nq], in1=rsb[:, 0, 0:nq], op=ALU.mult),
                                rd=[R_bank[ob], R_rsb[0]], wr=[R_OT[pl][qb]])
                        pending.append((x + 3, norm))
                for _, fn in pending:
                    fn()

            def qkv_pair(l, pr, si, qtiles, part):
                pc = {"k": 0}

                def nb():
                    b = pc["k"] % 4
                    pc["k"] += 1
                    return b
                if l == 0:
                    wv = ring_view(si, 0, [KC, 3, 128])
                    wq = lambda kc: wv[:, kc, 0, :]
                    wk = lambda kc: wv[:, kc, 1, :]
                    wvv = lambda kc: wv[:, kc, 2, :]
                elif part == "q":
                    wv = ring_view(si, 0, [KC, 256])
                    wq = lambda kc: wv[:, kc, 0:128]
                    wqp = lambda kc: wv[:, kc, 128:256]
                else:
                    wv = ring_view(si, 0, [KC, 384])
                    wk = lambda kc: wv[:, kc, 0:128]
                    wkp = lambda kc: wv[:, kc, 128:256]
                    wvv = lambda kc: wv[:, kc, 256:384]

                def proj(bank, wf, off, n):
                    def f(e):
                        last = None
                        for kc in range(KC):
                            last = e.matmul(banks[bank][:, 0:n], lhsT=wf(kc), rhs=uT[:, kc, off:off + n],
                                            start=(kc == 0), stop=(kc == KC - 1))
                        return last
                    return f

                def rope(bq, bp, off, n, dst_fn):
                    sc.op("dve", lambda e: e.tensor_tensor(out=t1[:, 0, 0:n], in0=banks[bq][:, 0:n], in1=ropeC[:, off:off + n], op=ALU.mult),
                          rd=[R_bank[bq], R_rope], wr=[R_t1[0]])
                    sc.op("dve", lambda e: e.tensor_tensor(out=t2[:, 0, 0:n], in0=banks[bp][:, 0:n], in1=ropeS[:, off:off + n], op=ALU.mult),
                          rd=[R_bank[bp], R_rope2], wr=[R_t2[0]])
                    dst_fn()

                for t in (qtiles if part == "q" else []):
                    flush(t)
                    off, n = TILES[t]
                    b = nb()
                    sc.op("pe", proj(b, wq, off, n), rd=[R_slot[si], R_u[t]], wr=[R_bank[b]])
                    if l == 0:
                        sc.op("act", lambda e, b=b, off=off, n=n: e.activation(out=qT[:, off:off + n], in_=banks[b][:, 0:n],
                                                                               func=AF.Copy, scale=0.125),
                              rd=[R_bank[b]], wr=[R_q[t]])
                    else:
                        b2 = nb()
                        sc.op("pe", proj(b2, wqp, off, n), rd=[R_slot[si], R_u[t]], wr=[R_bank[b2]])

                        def dst(off=off, n=n, t=t):
                            sc.op("dve", lambda e: e.tensor_tensor(out=qT[:, off:off + n], in0=t1[:, 0, 0:n], in1=t2[:, 0, 0:n], op=ALU.add),
                                  rd=[R_t1[0], R_t2[0]], wr=[R_q[t]])
                        rope(b, b2, off, n, dst)
                if part != "kv":
                    return
                for t in range(5):
                    flush(t)
                    off, n = TILES[t]
                    nk = n // 64
                    kt0 = off // 64
                    b = nb()
                    sc.op("pe", proj(b, wk, off, n), rd=[R_slot[si], R_u[t]], wr=[R_bank[b]])
                    dA = KBD[0:64, kt0:kt0 + nk, 0:64]
                    dB = KBD[64:128, kt0:kt0 + nk, 64:128]
                    if l == 0 or t == 4:
                        sA = banks[b][0:64, 0:n].rearrange("p (k c) -> p k c", c=64)
                        sB = banks[b][64:128, 0:n].rearrange("p (k c) -> p k c", c=64)
                        sc.op("act", lambda e, dA=dA, sA=sA: e.activation(out=dA, in_=sA, func=AF.Copy),
                              rd=[R_bank[b]], wr=[R_KBD[t]])
                        sc.op("dve", lambda e, dB=dB, sB=sB: e.tensor_copy(out=dB, in_=sB), rd=[R_bank[b]], wr=[R_KBD[t]])
                    else:
                        b2 = nb()
                        sc.op("pe", proj(b2, wkp, off, n), rd=[R_slot[si], R_u[t]], wr=[R_bank[b2]])

                        def dst(dA=dA, dB=dB, n=n, t=t):
                            sc.op("dve", lambda e: e.tensor_tensor(
                                out=dA, in0=t1[0:64, 0, 0:n].rearrange("p (k c) -> p k c", c=64),
                                in1=t2[0:64, 0, 0:n].rearrange("p (k c) -> p k c", c=64), op=ALU.add),
                                rd=[R_t1[0], R_t2[0]], wr=[R_KBD[t]])
                            sc.op("dve", lambda e: e.tensor_tensor(
                                out=dB, in0=t1[64:128, 0, 0:n].rearrange("p (k c) -> p k c", c=64),
                                in1=t2[64:128, 0, 0:n].rearrange("p (k c) -> p k c", c=64), op=ALU.add),
                                rd=[R_t1[0], R_t2[0]], wr=[R_KBD[t]])
                        rope(b, b2, off, n, dst)
                for gi in range(9):
                    b = nb()

                    def f(e, gi=gi, b=b):
                        last = None
                        for k4 in range(4):
                            m = gi * 4 + k4
                            if m < 35:
                                for kc in range(KC):
                                    last = e.matmul(banks[b][:, k4 * 128:(k4 + 1) * 128], lhsT=uT[:, kc, 64 * m:64 * m + 128],
                                                    rhs=wvv(kc), start=(kc == 0), stop=(kc == KC - 1))
                            else:
                                for kc in range(KC):
                                    e.matmul(banks[b][0:64, k4 * 128:k4 * 128 + 64], lhsT=uT[:, kc, 64 * m:64 * m + 64],
                                             rhs=wvv(kc)[:, 0:64], start=(kc == 0), stop=(kc == KC - 1))
                                for kc in range(KC):
                                    last = e.matmul(banks[b][64:128, k4 * 128 + 64:k4 * 128 + 128], lhsT=uT[:, kc, 0:64],
                                                    rhs=wvv(kc)[:, 64:128], start=(kc == 0), stop=(kc == KC - 1))
                        return last
                    sc.op("pe", f, rd=[R_slot[si]] + R_u, wr=[R_bank[b]])
                    bv = banks[b][:, :].rearrange("p (k c) -> p k c", c=128)
                    sA = bv[0:64, :, 0:64]
                    dA = VBD[0:64, gi * 4:gi * 4 + 4, 0:64]
                    sc.op("act", lambda e, dA=dA, sA=sA: e.activation(out=dA, in_=sA, func=AF.Copy), rd=[R_bank[b]], wr=[R_VBD[gi]])
                    if gi < 8:
                        sB = bv[64:128, :, 64:128]
                        dB = VBD[64:128, gi * 4 + 1:gi * 4 + 5, 64:128]
                        sc.op("dve", lambda e, dB=dB, sB=sB: e.tensor_copy(out=dB, in_=sB), rd=[R_bank[b]],
                              wr=[R_VBD[gi], R_VBD[gi + 1]])
                    else:
                        sB = bv[64:128, 0:3, 64:128]
                        dB = VBD[64:128, 33:36, 64:128]
                        sc.op("dve", lambda e, dB=dB, sB=sB: e.tensor_copy(out=dB, in_=sB), rd=[R_bank[b]], wr=[R_VBD[8]])
                        sB2 = bv[64:128, 3, 64:128]
                        dB2 = VBD[64:128, 0, 64:128]
                        sc.op("dve", lambda e, dB2=dB2, sB2=sB2: e.tensor_copy(out=dB2, in_=sB2), rd=[R_bank[b]], wr=[R_VBD[0]])

            def mixer(l, ctx_out, nxt):
                i = 1
                mod_ensure(l, i)
                qtiles = list(range(5)) if ctx_out else list(range(4))
                if l == 1:
                    sc.op("dve", lambda e: e.memset(rsb[:, 0, 0:2], 0.0), rd=[], wr=[R_rope, R_rope2, R_mask] + R_rsb + ffn_arena())
                    sc.dma("pool", lambda e: e.dma_start(out=ropeC[:], in_=rope_cs[:, 0, :]), "aux", wr=[R_rope])
                    sc.dma("pool", lambda e: e.dma_start(out=ropeS[:], in_=rope_cs[:, 1, :]), "aux", wr=[R_rope2])
                    sc.dma("pool", lambda e: e.dma_start(out=maskw[:], in_=wa_mask), "aux", wr=[R_mask])
                    sc.op("act", lambda e: e.activation(out=maskw[:], in_=maskw[:], func=AF.Exp), rd=[R_mask], wr=[R_mask])
                for pr in range(8):
                    if l == 0:
                        si = w_get(("qkv", l, pr), qkv_dmas(l, pr))
                        qkv_pair(l, pr, si, qtiles, "kv")
                        qkv_pair(l, pr, si, qtiles, "q")
                    else:
                        if pr % 2 == 0:
                            skv = w_get(("kv", l, pr), kv_dmas(pr))
                            qkv_pair(l, pr, skv, qtiles, "kv")
                            mod_step()
                        si = w_get(("qkv", l, pr), qkv_dmas(l, pr))
                        qkv_pair(l, pr, si, qtiles, "q")
                    pl = pr % 2
                    if l == 0:
                        tbv = ring_view(si, 3072, [960])
                        sc.op("act", lambda e, tbv=tbv: e.activation(out=tbv, in_=tbv, func=AF.Exp),
                              rd=[R_slot[si]], wr=[R_slot[si]])
                    attention_pair(l, pr, pl, si, qtiles)
                    mod_step()
                    if pl == 0:
                        continue
                    so = w_get(("wo", l, pr // 2), wo_dmas(l, pr // 2))
                    wo = ring_view(so, 0, [2, D])
                    lastp = (pr == 7)

                    def outproj(c, t, so=so, wo=wo, lastp=lastp):
                        off, n = TILES[t]
                        bank = (cnt["y"] % 6) if lastp else (4 + cnt["y"] % 4)
                        cnt["y"] += 1

                        def f(e):
                            e.matmul(banks[bank][:, 0:n], lhsT=wo[:, 0, c * 128:(c + 1) * 128], rhs=OT[:, 0, off:off + n],
                                     start=True, stop=False)
                            return e.matmul(banks[bank][:, 0:n], lhsT=wo[:, 1, c * 128:(c + 1) * 128], rhs=OT[:, 1, off:off + n],
                                            start=False, stop=True)
                        sc.op("pe", f, rd=[R_slot[so], R_OT[0][t], R_OT[1][t]], wr=[R_bank[bank]])
                        z_update(l, i, c, t, bank)
                    if not lastp:
                        for c in range(KC):
                            for t in qtiles:
                                outproj(c, t)
                    else:
                        tail_pipeline(l, i, qtiles, nxt, lambda t: [outproj(c, t) for c in range(KC)])
                    mod_step()

            def arena_to_ffn():
                sc.op("dve", lambda e: e.memset(sa[:, 0, 0:2], 0.0), rd=[], wr=att_res() + ffn_arena())

            sc.op("dve", lambda e: e.memset(eps_col[:], LN_EPS), wr=[R_es])
            sc.op("dve", lambda e: e.memset(KBD[:, :, :], 0.0), rd=[], wr=R_KBD)
            sc.op("dve", lambda e: e.memset(VBD[:, :, :], 0.0), rd=[], wr=R_VBD)
            for _ in range(4):
                mod_step()
            for t in range(5):
                modulate_tile(0, 0, t)
            mod_ensure(0, 0)
            seq = [("ffn", 0, 0), ("mix", 0), ("ffn", 0, 1), ("ffn", 1, 0), ("mix", 1), ("ffn", 1, 1)]
            stops = {"ffn00": 1, "mix0": 2, "ffn01": 3, "ffn10": 4, "mix1": 5, None: 6}
            seq = seq[:stops[stop]]
            for k, sub in enumerate(seq):
                nxt = None
                if k + 1 < len(seq):
                    ns = seq[k + 1]
                    if ns[0] == "mix":
                        nxt = (ns[1], 1, list(range(5)))
                    else:
                        ni = 0 if ns[2] == 0 else 2
                        ntl = list(range(4)) if (ns[1] == 1 and ns[2] == 1) else list(range(5))
                        nxt = (ns[1], ni, ntl)
                if sub[0] == "ffn":
                    if k > 0 and seq[k - 1][0] == "mix":
                        arena_to_ffn()
                    tl = range(4) if (sub[1] == 1 and sub[2] == 1) else range(5)
                    ffn(sub[1], sub[2], tl, nxt)
                else:
                    mixer(sub[1], sub[1] == 0, nxt)
            flush_all()

            ntile_out = 5 if stop is not None else 4
            for c in (range(KC) if stop is not None else []):
                sc.dma("sp", lambda e, c=c: e.dma_start(out=outT[c * 128:(c + 1) * 128, :], in_=hT[:, c, 0:n_out_tok]),
                       "outd", rd=[R_h[c][t] for t in range(ntile_out)])

            for coarse in ("init_sp", "aux"):
                tot = sc.cnt[coarse]
                for k in sc.ops:
                    sc.ops[k] = [([(en, (tot if en == coarse else v)) for en, v in waits], fn, sn, c)
                                 for waits, fn, sn, c in sc.ops[k]]
            return rec

        eps_col = sb("eps_col", [128, 1], F32)
        plan = emit(Sched(), None)
        sc = Sched()
        sc.sems = SEMS
        emit(sc, plan)

        block = es.enter_context(nc.Block())

        @block.tensor
        def _(e):
            sc.replay("pe", e)

        @block.scalar
        def _(e):
            sc.replay("act", e)

        @block.vector
        def _(e):
            sc.replay("dve", e)

        @block.gpsimd
        def _(e):
            sc.replay("pool", e)

        @block.sync
        def _(e):
            sc.replay("sp", e, final_waits=[("outd", sc.cnt["outd"])])

    return nc


def _host_tables(na_rpb, wa_sinks):
    kc = np.arange(64)[:, None]
    qc = np.arange(64)[None, :]
    ws = np.clip(qc - 8, 0, 48)
    valid = (kc >= ws) & (kc < ws + 16)
    coff = np.clip(kc - qc, -15, 15) + 15
    rpb = np.asarray(na_rpb[0], dtype=np.float32)
    tb = np.empty((2, 64, 8, 15, 64), np.float32)
    for hh in range(2):
        for e in range(15):
            g = rpb[hh::2][:, 14 - e][:, coff]
            g = np.where(valid[None], g, np.float32(NEG))
            tb[hh, :, :, e, :] = g.transpose(1, 0, 2)
    tb = tb.reshape(128, 8, 15 * 64)
    sk = np.asarray(wa_sinks[0], np.float32)
    wa_sk = np.empty((128, 8), np.float32)
    for p in range(128):
        wa_sk[p] = sk[(p // 64)::2]
    kl = np.arange(64)[:, None]
    j = np.arange(320)[None, :]
    m = np.where((j >= kl) & (j <= kl + 256), np.float32(0), np.float32(NEG)).astype(np.float32)
    wa_mask = np.concatenate([m, m], 0)
    t = np.arange(S)
    rows = (t // GW).astype(np.float32)
    cols = (t % GW).astype(np.float32)
    inv = (np.float32(10000.0) ** (-np.arange(16, dtype=np.float32) / np.float32(16))).astype(np.float32)
    cs = np.empty((128, 2, S), np.float32)
    for p in range(128):
        d = p % 64
        pos = rows if d < 32 else cols
        ang = (pos * inv[d % 16]).astype(np.float32)
        cs[p, 0] = np.cos(ang)
        sn = np.sin(ang)
        cs[p, 1] = -sn if (d % 32) < 16 else sn
    perm = np.empty(64, np.int64)
    for d in range(64):
        perm[d] = d + 16 if (d % 32) < 16 else d - 16
    consts = np.zeros((128, 3, 128), np.float32)
    consts[:, 0, :] = np.eye(128, dtype=np.float32)
    consts[0:64, 1, 0:64] = 1.0
    consts[64:128, 1, 64:128] = 1.0
    consts[:, 2, :] = 1.0 / D
    return tb, wa_sk, wa_mask, cs, perm, consts


_CACHE = {}


def _prep_shared(c_ctx, w_mod, b_mod, ln_g, ln_b, ffn_w_in, ffn_w_out, na_w_qkv, na_w_o, na_rpb,
                 wa_w_qkv, wa_w_o, wa_sinks):
    f = lambda a: np.ascontiguousarray(np.asarray(a, dtype=np.float32))
    tb, wa_sk, wa_mask, cs, perm, consts = _host_tables(np.asarray(na_rpb), np.asarray(wa_sinks))
    wq = f(wa_w_qkv)[0]
    cols = np.concatenate([(h * 64 + perm) for h in range(NH)] + [D + g * 64 + perm for g in range(4)])
    sh = {
        "w_mod": f(w_mod),
        "bmod": f(np.asarray(b_mod).reshape(DEPTH, 72, 128).transpose(2, 0, 1)),
        "lng": f(np.asarray(ln_g).reshape(DEPTH, 3, KC, 128).transpose(3, 0, 1, 2)),
        "lnb": f(np.asarray(ln_b).reshape(DEPTH, 3, KC, 128).transpose(3, 0, 1, 2)),
        "ffn_w_in": f(ffn_w_in),
        "ffn_w_out": f(ffn_w_out),
        "na_w_qkv": f(na_w_qkv)[0],
        "na_w_o": f(na_w_o)[0],
        "na_tb": f(tb),
        "wa_w_qkv": wq,
        "wa_w_perm": f(wq[:, cols]),
        "wa_w_o": f(wa_w_o)[0],
        "wa_sk": f(wa_sk),
        "wa_mask": f(wa_mask),
        "rope_cs": f(cs),
        "consts": f(consts),
    }
    return sh


def run(inputs, stop=None):
    x = np.asarray(inputs["x"], np.float32)
    c = np.asarray(inputs["c"], np.float32)
    ctx = np.asarray(inputs["ctx"], np.float32)
    c_ctx = np.asarray(inputs["c_ctx"], np.float32)
    sh = _prep_shared(c_ctx, *[inputs[k] for k in ("w_mod", "b_mod", "ln_g", "ln_b", "ffn_w_in", "ffn_w_out",
                                                     "na_w_qkv", "na_w_o", "na_rpb", "wa_w_qkv", "wa_w_o", "wa_sinks")])
    key = ("nc", stop)
    if key not in _CACHE:
        _CACHE[key] = build_program(stop)
    nc = _CACHE[key]
    B = x.shape[0]
    in_maps = []
    for b in range(B):
        m = dict(sh)
        m["xT"] = np.ascontiguousarray(x[b].T)
        m["ctxT"] = np.ascontiguousarray(ctx[b].T)
        ccb = np.stack([c[b], c_ctx], -1).reshape(KC, 128, 2).transpose(1, 0, 2)
        m["cc"] = np.ascontiguousarray(ccb)
        in_maps.append(m)
    res = run_bass_kernel_spmd(nc, in_maps, core_ids=list(range(B)))
    outs = [np.asarray(r["outT"]) for r in res.results]
    return np.stack([o.T for o in outs], 0)


def kernel(**inputs):
    out = run(inputs, None)
    return np.ascontiguousarray(out.astype(np.float32))
```

```python
import numpy as np
from contextlib import ExitStack
import concourse.bass as bass
import concourse.mybir as mybir
from concourse.bass_utils import run_bass_kernel_spmd

F32 = mybir.dt.float32
BF16 = mybir.dt.bfloat16
AF = mybir.ActivationFunctionType
ALU = mybir.AluOpType

D = 1024
KC = 8
S = 2048
CT = 256
T = S + CT
FF = 2816
FC = 22
NH = 16
HD = 64
GW = 64
ROWS = 32
DEPTH = 2
ALPHA = (2 * DEPTH) ** 0.25
LN_EPS = 1e-5
NEG = -1e30
SLOT = 4096
NSLOT = 3
LOOKAHEAD = 2
PARTS = [(0, 4), (4, 4), (8, 4), (12, 4), (16, 3), (19, 3)]
MODW = 512
NMOD = 9 * D // MODW
TILES = [(0, 512), (512, 512), (1024, 512), (1536, 512), (2048, 256)]


class Res:
    __slots__ = ("w", "r")

    def __init__(self):
        self.w = None
        self.r = {}


class Sched:
    def __init__(self):
        self.ops = {k: [] for k in ("pe", "act", "dve", "pool", "sp")}
        self.cnt = {k: 0 for k in self.ops}
        self.seen = {k: {} for k in self.ops}
        self.sems = {}
        self.step = {k: 1 for k in self.ops}

    def add_dma_sem(self, name):
        self.cnt[name] = 0
        self.step[name] = 16

    def _deps(self, e, rd, wr):
        deps = {}

        def need(x):
            if x is None:
                return
            en, c = x
            if c > deps.get(en, 0):
                deps[en] = c

        for r in rd:
            need(r.w)
        for w in wr:
            need(w.w)
            for en, c in w.r.items():
                need((en, c))
        waits = []
        seen = self.seen[e]
        for en, c in deps.items():
            if en == e and e == "pe":
                continue
            if seen.get(en, 0) >= c:
                continue
            seen[en] = c
            waits.append((en, c))
        return waits

    def op(self, e, fn, rd=(), wr=()):
        waits = self._deps(e, rd, wr)
        self.cnt[e] += 1
        c = self.cnt[e]
        self.ops[e].append((waits, fn, e, c))
        for r in rd:
            if r.r.get(e, 0) < c:
                r.r[e] = c
        for w in wr:
            w.w = (e, c)
            w.r = {}

    def dma(self, q, fn, dsem, rd=(), wr=()):
        waits = self._deps(q, rd, wr)
        self.cnt[dsem] += 16
        c = self.cnt[dsem]
        self.ops[q].append((waits, fn, dsem, c))
        for r in rd:
            if r.r.get(dsem, 0) < c:
                r.r[dsem] = c
        for w in wr:
            w.w = (dsem, c)
            w.r = {}

    def replay(self, name, eng, final_waits=()):
        for waits, fn, sname, c in self.ops[name]:
            for en, v in waits:
                eng.wait_ge(self.sems[en], v)
            inst = fn(eng)
            inst.then_inc(self.sems[sname], self.step[sname])
        for en, v in final_waits:
            eng.wait_ge(self.sems[en], v)


def build_program(stop=None):
    nc = bass.Bass("TRN2", target_bir_lowering=False)

    def din(name, shape):
        return nc.dram_tensor(name, list(shape), F32, kind="ExternalInput").ap()

    xT = din("xT", [D, S])
    ctxT = din("ctxT", [D, CT])
    cc = din("cc", [128, KC, 2])
    w_mod = din("w_mod", [DEPTH, D, 9 * D])
    bmod = din("bmod", [128, DEPTH, 72])
    lng = din("lng", [128, DEPTH, 3, KC])
    lnb = din("lnb", [128, DEPTH, 3, KC])
    w_in = din("ffn_w_in", [DEPTH, 2, D, 2 * FF])
    w_out = din("ffn_w_out", [DEPTH, 2, FF, D])
    na_qkv = din("na_w_qkv", [D, 3 * D])
    na_o = din("na_w_o", [D, D])
    na_tb = din("na_tb", [128, 8, 15 * 64])
    wa_qkv = din("wa_w_qkv", [D, D + 512])
    wa_perm = din("wa_w_perm", [D, D + 256])
    wa_o = din("wa_w_o", [D, D])
    wa_sk = din("wa_sk", [128, 8])
    wa_mask = din("wa_mask", [128, 320])
    rope_cs = din("rope_cs", [128, 2, S])
    consts = din("consts", [128, 3, 128])
    n_out_tok = T if stop is not None else S
    outT = nc.dram_tensor("outT", [D, n_out_tok], F32, kind="ExternalOutput").ap()

    es = ExitStack()
    with es:
        def sb(name, shape, dt):
            return es.enter_context(nc.sbuf_tensor(name, list(shape), dt))

        hT = sb("hT", [128, KC, T], F32)
        uT = sb("uT", [128, KC, T], BF16)
        ring = sb("ring", [128, NSLOT, SLOT], BF16)
        ARENA = 4 * T + 2 * 1024
        ARENA_N = 25152
        arena = sb("arena", [128, ARENA_N], BF16)
        zb = sb("zb", [128, 2, 512], BF16)
        zq = sb("zq", [128, 2, 512], BF16)
        mean_sb2 = sb("mean_sb", [128, 2, 512], F32)
        tmpA2 = sb("tmpA", [128, 2, 512], F32)
        cst = sb("cst", [128, 3, 128], BF16)
        ccs = sb("ccs", [128, KC, 2], F32)
        scb = sb("scb", [128, KC, 2], BF16)
        bmod_sb = sb("bmod_sb", [128, DEPTH, 72], F32)
        raw = sb("raw", [128, DEPTH, 72, 2], F32)
        Acol = sb("Acol", [128, DEPTH, 3, KC, 2], F32)
        Gcol = sb("Gcol", [128, DEPTH, 3, KC, 2], F32)
        lng_sb = sb("lng_sb", [128, DEPTH, 3, KC], F32)
        lnb_sb = sb("lnb_sb", [128, DEPTH, 3, KC], F32)
        sk_sb = sb("sk_sb", [128, 8], F32)
        es_sb = sb("es_sb", [128, 8], F32)
        zero_col = sb("zero_col", [128, 1], F32)
        banks = [es.enter_context(nc.psum_tensor(f"bank{i}", [128, 512], F32)) for i in range(8)]

        ident = cst[:, 0, :]
        onesbd = cst[:, 1, :]
        onesd = cst[:, 2, :]

        def shaped(flat, shape):
            if len(shape) == 1:
                return flat
            names = " ".join(f"a{i}" for i in range(len(shape)))
            kw = {f"a{i}": int(s_) for i, s_ in enumerate(shape)}
            return flat.rearrange(f"p ({names}) -> p {names}", **kw)

        def aview(off, shape, dt=BF16):
            n = int(np.prod(shape))
            if dt == F32:
                assert off % 2 == 0
                flat = arena[:, off:off + 2 * n].bitcast(F32)
            else:
                flat = arena[:, off:off + n]
            return shaped(flat, shape)

        def ring_view(si, off, shape):
            n = int(np.prod(shape))
            return shaped(ring[:, si, off:off + n], shape)

        gT = aview(0, [4, T])
        sa = aview(4 * T, [2, 512], F32)
        o = 0
        qT = aview(o, [T]); o += T
        OT = aview(o, [2, T]); o += 2 * T
        PT = aview(o, [3, 512]); o += 3 * 512
        rsb = aview(o, [1, 512], F32); o += 1024
        t1 = aview(o, [1, 512], F32); o += 1024
        t2 = aview(o, [1, 512], F32); o += 1024
        ropeC = aview(o, [S]); o += S
        ropeS = aview(o, [S]); o += S
        maskw = aview(o, [320]); o += 320
        assert o >= ARENA
        KBD = aview(o, [36, 128]); o += 36 * 128
        VBD = aview(o, [36, 128]); o += 36 * 128
        assert o <= ARENA_N, o
        assert ARENA <= ARENA_N

        SEMS = {}
        for k in ("pe", "act", "dve", "pool", "sp"):
            SEMS[k] = es.enter_context(nc.semaphore(f"s_{k}"))
        dma_names = [f"slot{i}" for i in range(NSLOT)] + ["init_sp", "init_pool", "aux", "outd"] + [f"init_x{t}" for t in range(5)]
        for k in dma_names:
            SEMS[k] = es.enter_context(nc.semaphore(f"s_{k}"))

        def wrows(ap2d):
            return ap2d.rearrange("(c p) f -> p c f", p=128)

        def in_groups(part):
            j0, n = part
            gs = []
            j = j0
            while j < j0 + n:
                g = min(2, j0 + n - j)
                gs.append((j, g))
                j += g
            return gs

        def emit(sc, plan):
            record = plan is None
            rec = []
            for k in dma_names:
                sc.add_dma_sem(k)

            R_h = [[Res() for _ in TILES] for _ in range(KC)]
            R_u = [Res() for _ in TILES]
            R_g = [[Res() for _ in TILES] for _ in range(4)]
            R_slot = [Res() for _ in range(NSLOT)]
            R_bank = [Res() for _ in range(8)]
            R_sa = [Res(), Res()]
            R_zb = [Res(), Res()]
            R_zq = [Res(), Res()]
            R_mean2 = [Res(), Res()]
            R_tmpA2 = [Res(), Res()]
            R_cst = Res()
            R_small = Res()
            R_scb = Res()
            R_raw = [Res() for _ in range(DEPTH)]
            R_cols = [[Res() for _ in range(3)] for _ in range(DEPTH)]
            R_colA = [[Res() for _ in range(3)] for _ in range(DEPTH)]
            R_ln = Res()
            R_es = Res()
            R_KBD = [Res() for _ in range(5)]
            R_VBD = [Res() for _ in range(9)]
            R_q = [Res() for _ in TILES]
            R_OT = [[Res() for _ in TILES] for _ in range(2)]
            R_PT = [Res() for _ in range(3)]
            R_rsb = [Res()]
            R_t1 = [Res()]
            R_t2 = [Res()]
            R_rope = Res()
            R_rope2 = Res()
            R_mask = Res()
            R_onesf = Res()

            def h_res(tiles, cs=range(KC)):
                return [R_h[c][t] for c in cs for t in tiles]

            def ffn_arena():
                return [r for row in R_g for r in row] + R_sa

            def att_res():
                return (R_q + R_OT[0] + R_OT[1] + R_PT + R_rsb + R_t1 + R_t2 + [R_rope, R_rope2, R_mask])

            wstate = {"issued": 0, "next": 0}

            def w_issue_upto(n):
                while wstate["issued"] < min(n, len(plan)):
                    i = wstate["issued"]
                    si = i % NSLOT
                    key, dmas_fn = plan[i]
                    for dst, src in dmas_fn(si):
                        sc.dma("pool", (lambda e, dst=dst, src=src: e.dma_start(out=dst, in_=src)),
                               f"slot{si}", wr=[R_slot[si]])
                    wstate["issued"] += 1

            def w_get(key, dmas_fn):
                i = wstate["next"]
                wstate["next"] += 1
                if record:
                    rec.append((key, dmas_fn))
                    return i % NSLOT
                assert plan[i][0] == key, (plan[i][0], key)
                w_issue_upto(i + 1 + LOOKAHEAD)
                return i % NSLOT

            def mod_dmas(l, i):
                src = wrows(w_mod[l][:, i * MODW:(i + 1) * MODW])
                return lambda si: [(ring_view(si, 0, [KC, MODW]), src)]

            def win_dmas(l, w, j, g):
                sa_ = wrows(w_in[l, w][:, j * 128:(j + g) * 128])
                sv_ = wrows(w_in[l, w][:, FF + j * 128:FF + (j + g) * 128])
                return lambda si: [(ring_view(si, 0, [KC, 2, g * 128])[:, :, 0, :], sa_),
                                   (ring_view(si, 0, [KC, 2, g * 128])[:, :, 1, :], sv_)]

            def wout_dmas(l, w, j0, n):
                so = w_out[l, w][j0 * 128:(j0 + n) * 128, :].rearrange("(j p) d -> p j d", p=128)
                return lambda si: [(ring_view(si, 0, [n, D]), so)]

            def qkv_dmas(l, pr):
                if l == 0:
                    def f(si):
                        dm = []
                        for i in range(3):
                            src = wrows(na_qkv[:, i * D + pr * 128:i * D + (pr + 1) * 128])
                            dm.append((ring_view(si, 0, [KC, 3, 128])[:, :, i, :], src))
                        dm.append((ring_view(si, 3072, [960]), na_tb[:, pr, :]))
                        return dm
                    return f
                def f(si):
                    rv = ring_view(si, 0, [KC, 256])
                    return [(rv[:, :, 0:128], wrows(wa_qkv[:, pr * 128:(pr + 1) * 128])),
                            (rv[:, :, 128:256], wrows(wa_perm[:, pr * 128:(pr + 1) * 128]))]
                return f

            def kv_dmas(pr):
                g = pr // 2

                def f(si):
                    rv = ring_view(si, 0, [KC, 384])
                    kq = wrows(wa_qkv[:, D + g * 64:D + (g + 1) * 64])
                    kp = wrows(wa_perm[:, D + g * 64:D + (g + 1) * 64])
                    vq = wrows(wa_qkv[:, D + 256 + g * 64:D + 256 + (g + 1) * 64])
                    return [(rv[:, :, 0:64], kq), (rv[:, :, 64:128], kq),
                            (rv[:, :, 128:192], kp), (rv[:, :, 192:256], kp),
                            (rv[:, :, 256:320], vq), (rv[:, :, 320:384], vq)]
                return f

            def wo_dmas(l, q):
                wo = na_o if l == 0 else wa_o
                src = wo[2 * q * 128:(2 * q + 2) * 128, :].rearrange("(j p) d -> p j d", p=128)
                return lambda si: [(ring_view(si, 0, [2, D]), src)]

            sc.dma("pool", lambda e: e.dma_start(out=cst[:], in_=consts), "init_pool", wr=[R_cst])
            sc.dma("sp", lambda e: e.dma_start(out=ccs[:], in_=cc), "init_sp", wr=[Res()])
            sc.dma("sp", lambda e: e.dma_start(out=bmod_sb[:], in_=bmod), "init_sp", wr=[Res()])
            sc.dma("sp", lambda e: e.dma_start(out=lng_sb[:], in_=lng), "init_sp", wr=[Res()])
            sc.dma("sp", lambda e: e.dma_start(out=lnb_sb[:], in_=lnb), "init_sp", wr=[Res()])
            sc.dma("sp", lambda e: e.dma_start(out=sk_sb[:], in_=wa_sk), "init_sp", wr=[Res()])
            R_small.w = ("init_sp", sc.cnt["init_sp"])
            for t in range(4):
                off_, n_ = TILES[t]
                sc.dma("sp", lambda e, off_=off_, n_=n_: e.dma_start(
                    out=hT[:, :, off_:off_ + n_], in_=xT.rearrange("(c p) t -> p c t", p=128)[:, :, off_:off_ + n_]),
                    f"init_x{t}", wr=[R_h[c][t] for c in range(KC)])
            sc.dma("sp", lambda e: e.dma_start(out=hT[:, :, S:T], in_=ctxT.rearrange("(c p) t -> p c t", p=128)),
                   "init_x4", wr=[R_h[c][4] for c in range(KC)])
            if not record:
                w_issue_upto(LOOKAHEAD)

            sc.op("act", lambda e: e.activation(out=scb[:], in_=ccs[:], func=AF.Silu), rd=[R_small], wr=[R_scb])
            sc.op("dve", lambda e: e.memset(zero_col[:], 0.0), wr=[R_es])
            sc.op("act", lambda e: e.activation(out=es_sb[:], in_=sk_sb[:], func=AF.Exp), rd=[R_small, R_es], wr=[R_es])
            for l in range(DEPTH):
                for i in range(3):
                    a = 1.0 if (l == DEPTH - 1 and i == 2) else ALPHA
                    sc.op("dve", lambda e, l=l, i=i, a=a: e.tensor_scalar(
                        out=lng_sb[:, l, i, :], in0=lng_sb[:, l, i, :], scalar1=a, scalar2=None, op0=ALU.mult),
                        rd=[R_small], wr=[R_ln])
                    sc.op("dve", lambda e, l=l, i=i, a=a: e.tensor_scalar(
                        out=lnb_sb[:, l, i, :], in0=lnb_sb[:, l, i, :], scalar1=a, scalar2=None, op0=ALU.mult),
                        rd=[R_small], wr=[R_ln])
            for t in range(5):
                off_, n_ = TILES[t]
                if t % 2 == 0:
                    sc.op("act", lambda e, off_=off_, n_=n_: e.activation(out=hT[:, :, off_:off_ + n_], in_=hT[:, :, off_:off_ + n_],
                                                                          func=AF.Copy, scale=ALPHA),
                          rd=[], wr=h_res([t]))
                else:
                    sc.op("dve", lambda e, off_=off_, n_=n_: e.tensor_scalar(out=hT[:, :, off_:off_ + n_], in0=hT[:, :, off_:off_ + n_],
                                                                             scalar1=ALPHA, scalar2=None, op0=ALU.mult),
                          rd=[], wr=h_res([t]))

            modq = [(l, i) for l in range(DEPTH) for i in range(NMOD)]

            def mod_step():
                if not modq:
                    return
                l, i = modq.pop(0)
                bk = 6
                si = w_get(("mod", l, i), mod_dmas(l, i))
                wv = ring_view(si, 0, [KC, MODW])
                NJ = MODW // 128

                def f(e):
                    last = None
                    for jj in range(NJ):
                        for kc in range(KC):
                            last = e.matmul(banks[bk][:, 2 * jj:2 * jj + 2], lhsT=wv[:, kc, jj * 128:(jj + 1) * 128],
                                            rhs=scb[:, kc, :], start=(kc == 0), stop=(kc == KC - 1))
                    return last
                sc.op("pe", f, rd=[R_slot[si], R_scb], wr=[R_bank[bk]])
                sc.op("dve", lambda e: e.tensor_tensor(
                    out=raw[:, l, NJ * i:NJ * i + NJ, :], in0=banks[bk][:, 0:2 * NJ].rearrange("p (j s) -> p j s", s=2),
                    in1=bmod_sb[:, l, NJ * i:NJ * i + NJ].unsqueeze(2).broadcast_to([128, NJ, 2]), op=ALU.add),
                    rd=[R_bank[bk], R_small], wr=[R_raw[l]])
                LPS = NMOD // 3
                ii = i // LPS
                n0 = 3 * ii
                if i % LPS == 3:
                    sc.op("dve", lambda e: e.tensor_scalar(
                        out=Acol[:, l, ii, :, :], in0=raw[:, l, (n0 + 1) * 8:(n0 + 2) * 8, :], scalar1=1.0,
                        scalar2=1.0 / ALPHA, op0=ALU.add, op1=ALU.mult), rd=[R_raw[l]], wr=[R_colA[l][ii]])
                if i % LPS == LPS - 1:
                    wres = 1.0 if ii == 1 else 0.5
                    sc.op("dve", lambda e: e.tensor_scalar(
                        out=Gcol[:, l, ii, :, :], in0=raw[:, l, (n0 + 2) * 8:(n0 + 3) * 8, :], scalar1=wres,
                        scalar2=None, op0=ALU.mult), rd=[R_raw[l]], wr=[R_cols[l][ii]])

            def mod_ensure(l, ii):
                while modq and (modq[0][0] < l or (modq[0][0] == l and modq[0][1] < (NMOD // 3) * (ii + 1))):
                    mod_step()

            pend = {}

            def advance(t):
                lst = pend.get(t)
                if lst:
                    lst.pop(0)()
                    if not lst:
                        del pend[t]

            def flush(t):
                while t in pend:
                    advance(t)

            def flush_all():
                for t in sorted(list(pend.keys())):
                    flush(t)

            def modulate_tile(l, i, t, pool_ok=True):
                n0 = 3 * i
                off, n = TILES[t]
                s_ = 1 if t == 4 else 0
                for c in range(KC):
                    if (c < 2 or (not pool_ok and c < 5)) and t != 4:
                        sc.op("dve", lambda e, c=c: e.tensor_scalar(
                            out=uT[:, c, off:off + n], in0=hT[:, c, off:off + n], scalar1=Acol[:, l, i, c, s_:s_ + 1],
                            scalar2=raw[:, l, n0 * 8 + c, s_:s_ + 1], op0=ALU.mult, op1=ALU.add),
                            rd=[R_h[c][t], R_colA[l][i], R_raw[l]], wr=[R_u[t]])
                    elif t == 4 or not pool_ok:
                        sc.op("act", lambda e, c=c: e.activation(
                            out=uT[:, c, off:off + n], in_=hT[:, c, off:off + n], func=AF.Identity,
                            scale=Acol[:, l, i, c, s_:s_ + 1], bias=raw[:, l, n0 * 8 + c, s_:s_ + 1]),
                            rd=[R_h[c][t], R_colA[l][i], R_raw[l]], wr=[R_u[t]])
                    else:
                        sc.op("pool", lambda e, c=c: e.tensor_scalar(
                            out=uT[:, c, off:off + n], in0=hT[:, c, off:off + n], scalar1=Acol[:, l, i, c, s_:s_ + 1],
                            scalar2=raw[:, l, n0 * 8 + c, s_:s_ + 1], op0=ALU.mult, op1=ALU.add),
                            rd=[R_h[c][t], R_colA[l][i], R_raw[l]], wr=[R_u[t]])

            def ln_stats(l, i, t, k):
                off, n = TILES[t]
                for c in range(KC):
                    b = c % 2
                    sc.op("act", lambda e, c=c, b=b: e.activation(out=zq[:, b, 0:n], in_=hT[:, c, off:off + n], func=AF.Square),
                          rd=[R_h[c][t]], wr=[R_zq[b]])

                    def f(e, c=c, b=b):
                        e.matmul(banks[6][:, 0:n], lhsT=onesf[:, :], rhs=hT[:, c, off:off + n], start=(c == 0), stop=(c == KC - 1))
                        return e.matmul(banks[7][:, 0:n], lhsT=onesd, rhs=zq[:, b, 0:n], start=(c == 0), stop=(c == KC - 1))
                    sc.op("pe", f, rd=[R_h[c][t], R_zq[b], R_cst, R_onesf], wr=[R_bank[6], R_bank[7]])

            def ln_stats2(l, i, t, k):
                off, n = TILES[t]
                mean_sb = mean_sb2[:, k, :]
                tmpA = tmpA2[:, k, :]
                R_mean, R_tmpA = R_mean2[k], R_tmpA2[k]
                sc.op("act", lambda e: e.activation(out=mean_sb[:, 0:n], in_=banks[6][:, 0:n], func=AF.Copy),
                      rd=[R_bank[6]], wr=[R_mean])
                sc.op("dve", lambda e: e.tensor_tensor(out=tmpA[:, 0:n], in0=mean_sb[:, 0:n], in1=mean_sb[:, 0:n], op=ALU.mult),
                      rd=[R_mean], wr=[R_tmpA])
                sc.op("dve", lambda e: e.tensor_tensor(out=tmpA[:, 0:n], in0=banks[7][:, 0:n], in1=tmpA[:, 0:n], op=ALU.subtract),
                      rd=[R_bank[7], R_tmpA], wr=[R_tmpA])
                sc.op("act", lambda e: e.activation(out=tmpA[:, 0:n], in_=tmpA[:, 0:n], func=AF.Ln, bias=eps_col[:, 0:1]),
                      rd=[R_tmpA, R_es], wr=[R_tmpA])
                sc.op("act", lambda e: e.activation(out=tmpA[:, 0:n], in_=tmpA[:, 0:n], func=AF.Exp, scale=-0.5),
                      rd=[R_tmpA], wr=[R_tmpA])

            def ln_apply(l, i, t, k, nxt, late=False):
                off, n = TILES[t]
                mean_sb = mean_sb2[:, k, :]
                tmpA = tmpA2[:, k, :]
                R_mean, R_tmpA = R_mean2[k], R_tmpA2[k]
                for c in range(KC):
                    hv = hT[:, c, off:off + n]
                    sc.op("dve" if late else "pool",
                          lambda e, hv=hv: e.tensor_tensor(out=hv, in0=hv, in1=mean_sb[:, 0:n], op=ALU.subtract),
                          rd=[R_mean, R_h[c][t]], wr=[R_h[c][t]])
                    sc.op("dve", lambda e, hv=hv: e.tensor_tensor(out=hv, in0=hv, in1=tmpA[:, 0:n], op=ALU.mult),
                          rd=[R_tmpA, R_h[c][t]], wr=[R_h[c][t]])
                    sc.op("act", lambda e, hv=hv, c=c: e.activation(out=hv, in_=hv, func=AF.Identity,
                                                                    scale=lng_sb[:, l, i, c:c + 1], bias=lnb_sb[:, l, i, c:c + 1]),
                          rd=[R_ln, R_h[c][t]], wr=[R_h[c][t]])
                if nxt is not None and t in nxt[2]:
                    assert not (modq and (modq[0][0] < nxt[0] or (modq[0][0] == nxt[0] and modq[0][1] < (NMOD // 3) * (nxt[1] + 1))))
                    modulate_tile(nxt[0], nxt[1], t, pool_ok=not late)
                if nxt is None and stop is None:
                    sc.dma("sp", lambda e: e.dma_start(out=outT.rearrange("(c p) t -> p c t", p=128)[:, :, off:off + n],
                                                       in_=hT[:, :, off:off + n]),
                           "outd", rd=[R_h[c][t] for c in range(KC)])

            cnt = {"sa": 0, "av": 0, "y": 0}

            def z_update(l, i, c, t, bank):
                off, n = TILES[t]
                s_ = 1 if t == 4 else 0
                hv = hT[:, c, off:off + n]
                sc.op("dve", lambda e: e.scalar_tensor_tensor(out=hv, in0=banks[bank][:, 0:n], scalar=Gcol[:, l, i, c, s_:s_ + 1],
                                                              in1=hv, op0=ALU.mult, op1=ALU.add),
                      rd=[R_bank[bank], R_cols[l][i], R_h[c][t]], wr=[R_h[c][t]])

            def tail_pipeline(l, i, tiles, nxt, emit_outproj):
                tl = list(tiles)
                for k, t in enumerate(tl):
                    emit_outproj(t)
                    pend[t] = [(lambda t=t, k=k: ln_stats(l, i, t, k % 2)),
                               (lambda t=t, k=k: ln_stats2(l, i, t, k % 2)),
                               (lambda t=t, k=k: ln_apply(l, i, t, k % 2, nxt, late=(k == len(tl) - 1)))]
                    if k >= 1:
                        advance(tl[k - 1])
                    if k >= 2:
                        advance(tl[k - 2])
                    if k >= 1:
                        advance(tl[k - 1])
                advance(tl[-1])
                if len(tl) >= 2:
                    advance(tl[-2])
                advance(tl[-1])

            def ffn(l, w, tiles, nxt):
                i = 0 if w == 0 else 2
                tiles = list(tiles)
                mod_ensure(l, i)
                first = True
                for pi, part in enumerate(PARTS):
                    j0, npart = part
                    for (j, g) in in_groups(part):
                        si = w_get(("win", l, w, j), win_dmas(l, w, j, g))
                        wv = ring_view(si, 0, [KC, 2, g * 128])
                        for jj in range(g):
                            jl = j + jj - j0
                            for t in tiles:
                                flush(t)
                                off, n = TILES[t]
                                k = cnt["av"]; cnt["av"] += 1
                                ba, bv = (k % 2), 2 + (k % 2)

                                def f(e, jj=jj, off=off, n=n, ba=ba, bv=bv, wv=wv):
                                    for kc in range(KC):
                                        e.matmul(banks[ba][:, 0:n], lhsT=wv[:, kc, 0, jj * 128:(jj + 1) * 128],
                                                 rhs=uT[:, kc, off:off + n], start=(kc == 0), stop=(kc == KC - 1))
                                    last = None
                                    for kc in range(KC):
                                        last = e.matmul(banks[bv][:, 0:n], lhsT=wv[:, kc, 1, jj * 128:(jj + 1) * 128],
                                                        rhs=uT[:, kc, off:off + n], start=(kc == 0), stop=(kc == KC - 1))
                                    return last
                                rdx = ffn_arena() if first else []
                                sc.op("pe", f, rd=[R_slot[si], R_u[t]], wr=[R_bank[ba], R_bank[bv]])
                                b = cnt["sa"] % 2; cnt["sa"] += 1
                                sc.op("act", lambda e, b=b, ba=ba, n=n: e.activation(out=sa[:, b, 0:n], in_=banks[ba][:, 0:n], func=AF.Silu),
                                      rd=[R_bank[ba]], wr=[R_sa[b]])
                                sc.op("dve", lambda e, b=b, bv=bv, n=n, jl=jl, off=off: e.tensor_tensor(
                                    out=gT[:, jl, off:off + n], in0=banks[bv][:, 0:n], in1=sa[:, b, 0:n], op=ALU.mult),
                                    rd=[R_bank[bv], R_sa[b]], wr=[R_g[jl][t]])
                                first = False
                        mod_step()
                    si = w_get(("wout", l, w, pi), wout_dmas(l, w, j0, npart))
                    wo = ring_view(si, 0, [npart, D])

                    last_part = (pi == len(PARTS) - 1)

                    def outproj(c, t, si=si, wo=wo, npart=npart, last_part=last_part):
                        off, n = TILES[t]
                        bank = (cnt["y"] % 6) if last_part else (4 + cnt["y"] % 4)
                        cnt["y"] += 1

                        def f(e):
                            last = None
                            for jl in range(npart):
                                last = e.matmul(banks[bank][:, 0:n], lhsT=wo[:, jl, c * 128:(c + 1) * 128],
                                                rhs=gT[:, jl, off:off + n], start=(jl == 0), stop=(jl == npart - 1))
                            return last
                        sc.op("pe", f, rd=[R_slot[si]] + [R_g[jl][t] for jl in range(npart)], wr=[R_bank[bank]])
                        z_update(l, i, c, t, bank)
                    if pi < len(PARTS) - 1:
                        for c in range(KC):
                            for t in tiles:
                                outproj(c, t)
                    else:
                        tail_pipeline(l, i, tiles, nxt, lambda t: [outproj(c, t) for c in range(KC)])
                    mod_step()

            def rs_na(r):
                return min(max(r - 4, 0), ROWS - 8)

            def na_items(qb):
                if qb == 4:
                    return [(32 + k, 0, 256, None) for k in range(4)]
                items = [(32 + k, 0, 512, None) for k in range(4)]
                for kr in range(ROWS):
                    rr = [r for r in range(8 * qb, 8 * qb + 8) if rs_na(r) <= kr <= rs_na(r) + 7]
                    if not rr:
                        continue
                    ra, rb = rr[0], rr[-1]
                    assert rr == list(range(ra, rb + 1))
                    e0 = ra - kr + 7
                    assert 0 <= e0 and e0 + (rb - ra) <= 14
                    items.append((kr, (ra - 8 * qb) * 64, (rb - 8 * qb + 1) * 64, ("tb", e0 * 64, (e0 + rb - ra + 1) * 64)))
                return items

            def wa_items(qb):
                items = [(32 + k, 0, 512, None) for k in range(4)]
                for kt in range(32):
                    lo = 64 * kt - 128
                    g0 = max(lo, 512 * qb, 0)
                    g1 = min(lo + 320, 512 * qb + 512, S)
                    if g1 <= g0:
                        continue
                    items.append((kt, g0 - 512 * qb, g1 - 512 * qb, ("mask", g0 - lo, g1 - lo)))
                return items

            def attention_pair(l, pr, pl, si, qbanks):
                exp_scale = 1.0 if l == 0 else 0.125
                flat = []
                for qi, qb in enumerate(qbanks):
                    its = na_items(qb) if l == 0 else wa_items(qb)
                    for ii, it in enumerate(its):
                        flat.append((qi, qb, ii, len(its), it))
                nI = len(flat)
                sbank = [None] * nI

                def emit_qk(x):
                    qi, qb, ii, nits, (kt, q0, q1, bias) = flat[x]
                    bk = x % 4
                    sbank[x] = bk
                    qoff = TILES[qb][0]
                    n = q1 - q0

                    extra = []
                    if bias is not None:
                        if bias[0] == "tb":
                            extra.append((0, n, ring_view(si, 3072, [960])[:, bias[1]:bias[2]]))
                        else:
                            for (lo, hi) in ((0, 63), (257, 320)):
                                a_, b_ = max(bias[1], lo), min(bias[2], hi)
                                if a_ < b_:
                                    extra.append((a_ - bias[1], b_ - bias[1], maskw[:, a_:b_]))

                    def f(e):
                        last = e.matmul(banks[bk][:, 0:n], lhsT=KBD[:, kt, :], rhs=qT[:, qoff + q0:qoff + q1],
                                        start=True, stop=(len(extra) == 0))
                        for xi, (c0, c1, bap) in enumerate(extra):
                            last = e.matmul(banks[bk][:, c0:c1], lhsT=ident, rhs=bap, start=False,
                                            stop=(xi == len(extra) - 1))
                        return last
                    rd = [R_KBD[kt // 8], R_q[qb], R_cst]
                    if bias is not None:
                        rd.append(R_slot[si] if bias[0] == "tb" else R_mask)
                    sc.op("pe", f, rd=rd, wr=[R_bank[bk]])

                LA = 2
                for x in range(min(LA, nI)):
                    emit_qk(x)
                pending = []
                for x in range(nI):
                    qi, qb, ii, nits, (kt, q0, q1, bias) = flat[x]
                    n = q1 - q0
                    bk = sbank[x]
                    pb = x % 3
                    while pending and pending[0][0] <= x:
                        pending.pop(0)[1]()
                    sc.op("act", lambda e, bk=bk, pb=pb, n=n: e.activation(out=PT[:, pb, 0:n], in_=banks[bk][:, 0:n],
                                                                           func=AF.Exp, scale=exp_scale),
                          rd=[R_bank[bk]], wr=[R_PT[pb]])
                    if x + LA < nI:
                        emit_qk(x + LA)
                    ob = 4 + 2 * (qi % 2)

                    def f(e, kt=kt, q0=q0, q1=q1, n=n, pb=pb, ob=ob, ii=ii, nits=nits):
                        e.matmul(banks[ob][:, q0:q1], lhsT=VBD[:, kt, :], rhs=PT[:, pb, 0:n],
                                 start=(ii == 0), stop=(ii == nits - 1))
                        return e.matmul(banks[ob + 1][:, q0:q1], lhsT=onesbd, rhs=PT[:, pb, 0:n],
                                        start=(ii == 0), stop=(ii == nits - 1))
                    sc.op("pe", f, rd=[R_VBD[kt // 4], R_PT[pb], R_cst], wr=[R_bank[ob], R_bank[ob + 1]])
                    if ii == nits - 1:
                        qoff, nq = TILES[qb]
                        escol = es_sb[:, pr:pr + 1] if l == 1 else zero_col[:, 0:1]

                        def norm(ob=ob, nq=nq, escol=escol, qoff=qoff, qb=qb):
                            sc.op("act", lambda e: e.activation(
                                out=rsb[:, 0, 0:nq], in_=banks[ob + 1][:, 0:nq], func=AF.Ln, bias=escol),
                                rd=[R_bank[ob + 1], R_es], wr=[R_rsb[0]])
                            sc.op("act", lambda e: e.activation(out=rsb[:, 0, 0:nq], in_=rsb[:, 0, 0:nq], func=AF.Exp, scale=-1.0),
                                  rd=[R_rsb[0]], wr=[R_rsb[0]])
                            sc.op("dve", lambda e: e.tensor_tensor(
                                out=OT[:, pl, qoff:qoff + nq], in0=banks[ob][:, 0:nq], in1=rsb[:, 0, 0:nq], op=ALU.mult),
                                rd=[R_bank[ob], R_rsb[0]], wr=[R_OT[pl][qb]])
                        pending.append((x + 3, norm))
                for _, fn in pending:
                    fn()

            def qkv_pair(l, pr, si, qtiles, part):
                pc = {"k": 0}

                def nb():
                    b = pc["k"] % 4
                    pc["k"] += 1
                    return b
                if l == 0:
                    wv = ring_view(si, 0, [KC, 3, 128])
                    wq = lambda kc: wv[:, kc, 0, :]
                    wk = lambda kc: wv[:, kc, 1, :]
                    wvv = lambda kc: wv[:, kc, 2, :]
                elif part == "q":
                    wv = ring_view(si, 0, [KC, 256])
                    wq = lambda kc: wv[:, kc, 0:128]
                    wqp = lambda kc: wv[:, kc, 128:256]
                else:
                    wv = ring_view(si, 0, [KC, 384])
                    wk = lambda kc: wv[:, kc, 0:128]
                    wkp = lambda kc: wv[:, kc, 128:256]
                    wvv = lambda kc: wv[:, kc, 256:384]

                def proj(bank, wf, off, n):
                    def f(e):
                        last = None
                        for kc in range(KC):
                            last = e.matmul(banks[bank][:, 0:n], lhsT=wf(kc), rhs=uT[:, kc, off:off + n],
                                            start=(kc == 0), stop=(kc == KC - 1))
                        return last
                    return f

                def rope(bq, bp, off, n, dst_fn):
                    sc.op("dve", lambda e: e.tensor_tensor(out=t1[:, 0, 0:n], in0=banks[bq][:, 0:n], in1=ropeC[:, off:off + n], op=ALU.mult),
                          rd=[R_bank[bq], R_rope], wr=[R_t1[0]])
                    sc.op("dve", lambda e: e.tensor_tensor(out=t2[:, 0, 0:n], in0=banks[bp][:, 0:n], in1=ropeS[:, off:off + n], op=ALU.mult),
                          rd=[R_bank[bp], R_rope2], wr=[R_t2[0]])
                    dst_fn()

                for t in (qtiles if part == "q" else []):
                    flush(t)
                    off, n = TILES[t]
                    b = nb()
                    sc.op("pe", proj(b, wq, off, n), rd=[R_slot[si], R_u[t]], wr=[R_bank[b]])
                    if l == 0:
                        sc.op("act", lambda e, b=b, off=off, n=n: e.activation(out=qT[:, off:off + n], in_=banks[b][:, 0:n],
                                                                               func=AF.Copy, scale=0.125),
                              rd=[R_bank[b]], wr=[R_q[t]])
                    else:
                        b2 = nb()
                        sc.op("pe", proj(b2, wqp, off, n), rd=[R_slot[si], R_u[t]], wr=[R_bank[b2]])

                        def dst(off=off, n=n, t=t):
                            sc.op("dve", lambda e: e.tensor_tensor(out=qT[:, off:off + n], in0=t1[:, 0, 0:n], in1=t2[:, 0, 0:n], op=ALU.add),
                                  rd=[R_t1[0], R_t2[0]], wr=[R_q[t]])
                        rope(b, b2, off, n, dst)
                if part != "kv":
                    return
                for t in range(5):
                    flush(t)
                    off, n = TILES[t]
                    nk = n // 64
                    kt0 = off // 64
                    b = nb()
                    sc.op("pe", proj(b, wk, off, n), rd=[R_slot[si], R_u[t]], wr=[R_bank[b]])
                    dA = KBD[0:64, kt0:kt0 + nk, 0:64]
                    dB = KBD[64:128, kt0:kt0 + nk, 64:128]
                    if l == 0 or t == 4:
                        sA = banks[b][0:64, 0:n].rearrange("p (k c) -> p k c", c=64)
                        sB = banks[b][64:128, 0:n].rearrange("p (k c) -> p k c", c=64)
                        sc.op("act", lambda e, dA=dA, sA=sA: e.activation(out=dA, in_=sA, func=AF.Copy),
                              rd=[R_bank[b]], wr=[R_KBD[t]])
                        sc.op("dve", lambda e, dB=dB, sB=sB: e.tensor_copy(out=dB, in_=sB), rd=[R_bank[b]], wr=[R_KBD[t]])
                    else:
                        b2 = nb()
                        sc.op("pe", proj(b2, wkp, off, n), rd=[R_slot[si], R_u[t]], wr=[R_bank[b2]])

                        def dst(dA=dA, dB=dB, n=n, t=t):
                            sc.op("dve", lambda e: e.tensor_tensor(
                                out=dA, in0=t1[0:64, 0, 0:n].rearrange("p (k c) -> p k c", c=64),
                                in1=t2[0:64, 0, 0:n].rearrange("p (k c) -> p k c", c=64), op=ALU.add),
                                rd=[R_t1[0], R_t2[0]], wr=[R_KBD[t]])
                            sc.op("dve", lambda e: e.tensor_tensor(
                                out=dB, in0=t1[64:128, 0, 0:n].rearrange("p (k c) -> p k c", c=64),
                                in1=t2[64:128, 0, 0:n].rearrange("p (k c) -> p k c", c=64), op=ALU.add),
                                rd=[R_t1[0], R_t2[0]], wr=[R_KBD[t]])
                        rope(b, b2, off, n, dst)
                for gi in range(9):
                    b = nb()

                    def f(e, gi=gi, b=b):
                        last = None
                        for k4 in range(4):
                            m = gi * 4 + k4
                            if m < 35:
                                for kc in range(KC):
                                    last = e.matmul(banks[b][:, k4 * 128:(k4 + 1) * 128], lhsT=uT[:, kc, 64 * m:64 * m + 128],
                                                    rhs=wvv(kc), start=(kc == 0), stop=(kc == KC - 1))
                            else:
                                for kc in range(KC):
                                    e.matmul(banks[b][0:64, k4 * 128:k4 * 128 + 64], lhsT=uT[:, kc, 64 * m:64 * m + 64],
                                             rhs=wvv(kc)[:, 0:64], start=(kc == 0), stop=(kc == KC - 1))
                                for kc in range(KC):
                                    last = e.matmul(banks[b][64:128, k4 * 128 + 64:k4 * 128 + 128], lhsT=uT[:, kc, 0:64],
                                                    rhs=wvv(kc)[:, 64:128], start=(kc == 0), stop=(kc == KC - 1))
                        return last
                    sc.op("pe", f, rd=[R_slot[si]] + R_u, wr=[R_bank[b]])
                    bv = banks[b][:, :].rearrange("p (k c) -> p k c", c=128)
                    sA = bv[0:64, :, 0:64]
                    dA = VBD[0:64, gi * 4:gi * 4 + 4, 0:64]
                    sc.op("act", lambda e, dA=dA, sA=sA: e.activation(out=dA, in_=sA, func=AF.Copy), rd=[R_bank[b]], wr=[R_VBD[gi]])
                    if gi < 8:
                        sB = bv[64:128, :, 64:128]
                        dB = VBD[64:128, gi * 4 + 1:gi * 4 + 5, 64:128]
                        sc.op("dve", lambda e, dB=dB, sB=sB: e.tensor_copy(out=dB, in_=sB), rd=[R_bank[b]],
                              wr=[R_VBD[gi], R_VBD[gi + 1]])
                    else:
                        sB = bv[64:128, 0:3, 64:128]
                        dB = VBD[64:128, 33:36, 64:128]
                        sc.op("dve", lambda e, dB=dB, sB=sB: e.tensor_copy(out=dB, in_=sB), rd=[R_bank[b]], wr=[R_VBD[8]])
                        sB2 = bv[64:128, 3, 64:128]
                        dB2 = VBD[64:128, 0, 64:128]
                        sc.op("dve", lambda e, dB2=dB2, sB2=sB2: e.tensor_copy(out=dB2, in_=sB2), rd=[R_bank[b]], wr=[R_VBD[0]])

            def mixer(l, ctx_out, nxt):
                i = 1
                mod_ensure(l, i)
                qtiles = list(range(5)) if ctx_out else list(range(4))
                if l == 1:
                    sc.op("dve", lambda e: e.memset(rsb[:, 0, 0:2], 0.0), rd=[], wr=[R_rope, R_rope2, R_mask] + R_rsb + ffn_arena())
                    sc.dma("pool", lambda e: e.dma_start(out=ropeC[:], in_=rope_cs[:, 0, :]), "aux", wr=[R_rope])
                    sc.dma("pool", lambda e: e.dma_start(out=ropeS[:], in_=rope_cs[:, 1, :]), "aux", wr=[R_rope2])
                    sc.dma("pool", lambda e: e.dma_start(out=maskw[:], in_=wa_mask), "aux", wr=[R_mask])
                for pr in range(8):
                    if l == 0:
                        si = w_get(("qkv", l, pr), qkv_dmas(l, pr))
                        qkv_pair(l, pr, si, qtiles, "kv")
                        qkv_pair(l, pr, si, qtiles, "q")
                    else:
                        if pr % 2 == 0:
                            skv = w_get(("kv", l, pr), kv_dmas(pr))
                            qkv_pair(l, pr, skv, qtiles, "kv")
                            mod_step()
                        si = w_get(("qkv", l, pr), qkv_dmas(l, pr))
                        qkv_pair(l, pr, si, qtiles, "q")
                    pl = pr % 2
                    attention_pair(l, pr, pl, si, qtiles)
                    mod_step()
                    if pl == 0:
                        continue
                    so = w_get(("wo", l, pr // 2), wo_dmas(l, pr // 2))
                    wo = ring_view(so, 0, [2, D])
                    lastp = (pr == 7)

                    def outproj(c, t, so=so, wo=wo, lastp=lastp):
                        off, n = TILES[t]
                        bank = (cnt["y"] % 6) if lastp else (4 + cnt["y"] % 4)
                        cnt["y"] += 1

                        def f(e):
                            e.matmul(banks[bank][:, 0:n], lhsT=wo[:, 0, c * 128:(c + 1) * 128], rhs=OT[:, 0, off:off + n],
                                     start=True, stop=False)
                            return e.matmul(banks[bank][:, 0:n], lhsT=wo[:, 1, c * 128:(c + 1) * 128], rhs=OT[:, 1, off:off + n],
                                            start=False, stop=True)
                        sc.op("pe", f, rd=[R_slot[so], R_OT[0][t], R_OT[1][t]], wr=[R_bank[bank]])
                        z_update(l, i, c, t, bank)
                    if not lastp:
                        for c in range(KC):
                            for t in qtiles:
                                outproj(c, t)
                    else:
                        tail_pipeline(l, i, qtiles, nxt, lambda t: [outproj(c, t) for c in range(KC)])
                    mod_step()

            def arena_to_ffn():
                sc.op("dve", lambda e: e.memset(sa[:, 0, 0:2], 0.0), rd=[], wr=att_res() + ffn_arena())

            sc.op("dve", lambda e: e.memset(eps_col[:], LN_EPS), wr=[R_es])
            sc.op("dve", lambda e: e.memset(onesf[:], 1.0 / D), wr=[R_onesf])
            sc.op("dve", lambda e: e.memset(KBD[:, :, :], 0.0), rd=[], wr=R_KBD)
            sc.op("dve", lambda e: e.memset(VBD[:, :, :], 0.0), rd=[], wr=R_VBD)
            for _ in range(4):
                mod_step()
            for t in range(5):
                modulate_tile(0, 0, t)
            mod_ensure(0, 0)
            seq = [("ffn", 0, 0), ("mix", 0), ("ffn", 0, 1), ("ffn", 1, 0), ("mix", 1), ("ffn", 1, 1)]
            stops = {"ffn00": 1, "mix0": 2, "ffn01": 3, "ffn10": 4, "mix1": 5, None: 6}
            seq = seq[:stops[stop]]
            for k, sub in enumerate(seq):
                nxt = None
                if k + 1 < len(seq):
                    ns = seq[k + 1]
                    if ns[0] == "mix":
                        nxt = (ns[1], 1, list(range(5)))
                    else:
                        ni = 0 if ns[2] == 0 else 2
                        ntl = list(range(4)) if (ns[1] == 1 and ns[2] == 1) else list(range(5))
                        nxt = (ns[1], ni, ntl)
                if sub[0] == "ffn":
                    if k > 0 and seq[k - 1][0] == "mix":
                        arena_to_ffn()
                    tl = range(4) if (sub[1] == 1 and sub[2] == 1) else range(5)
                    ffn(sub[1], sub[2], tl, nxt)
                else:
                    mixer(sub[1], sub[1] == 0, nxt)
            flush_all()

            ntile_out = 5 if stop is not None else 4
            for c in (range(KC) if stop is not None else []):
                sc.dma("sp", lambda e, c=c: e.dma_start(out=outT[c * 128:(c + 1) * 128, :], in_=hT[:, c, 0:n_out_tok]),
                       "outd", rd=[R_h[c][t] for t in range(ntile_out)])

            for coarse in ("init_sp", "aux"):
                tot = sc.cnt[coarse]
                for k in sc.ops:
                    sc.ops[k] = [([(en, (tot if en == coarse else v)) for en, v in waits], fn, sn, c)
                                 for waits, fn, sn, c in sc.ops[k]]
            return rec

        eps_col = sb("eps_col", [128, 1], F32)
        onesf = sb("onesf", [128, 128], F32)
        plan = emit(Sched(), None)
        sc = Sched()
        sc.sems = SEMS
        emit(sc, plan)

        block = es.enter_context(nc.Block())

        @block.tensor
        def _(e):
            sc.replay("pe", e)

        @block.scalar
        def _(e):
            sc.replay("act", e)

        @block.vector
        def _(e):
            sc.replay("dve", e)

        @block.gpsimd
        def _(e):
            sc.replay("pool", e)

        @block.sync
        def _(e):
            sc.replay("sp", e, final_waits=[("outd", sc.cnt["outd"])])

    return nc


def _host_tables(na_rpb, wa_sinks):
    kc = np.arange(64)[:, None]
    qc = np.arange(64)[None, :]
    ws = np.clip(qc - 8, 0, 48)
    valid = (kc >= ws) & (kc < ws + 16)
    coff = np.clip(kc - qc, -15, 15) + 15
    rpb = np.asarray(na_rpb[0], dtype=np.float32)
    tb = np.empty((2, 64, 8, 15, 64), np.float32)
    for hh in range(2):
        for e in range(15):
            g = rpb[hh::2][:, 14 - e][:, coff]
            g = np.where(valid[None], g, np.float32(NEG))
            tb[hh, :, :, e, :] = g.transpose(1, 0, 2)
    tb = tb.reshape(128, 8, 15 * 64)
    sk = np.asarray(wa_sinks[0], np.float32)
    wa_sk = np.empty((128, 8), np.float32)
    for p in range(128):
        wa_sk[p] = sk[(p // 64)::2]
    kl = np.arange(64)[:, None]
    j = np.arange(320)[None, :]
    m = np.where((j >= kl) & (j <= kl + 256), np.float32(0), np.float32(NEG)).astype(np.float32)
    wa_mask = np.concatenate([m, m], 0)
    t = np.arange(S)
    rows = (t // GW).astype(np.float32)
    cols = (t % GW).astype(np.float32)
    inv = (np.float32(10000.0) ** (-np.arange(16, dtype=np.float32) / np.float32(16))).astype(np.float32)
    cs = np.empty((128, 2, S), np.float32)
    for p in range(128):
        d = p % 64
        pos = rows if d < 32 else cols
        ang = (pos * inv[d % 16]).astype(np.float32)
        cs[p, 0] = np.cos(ang)
        sn = np.sin(ang)
        cs[p, 1] = -sn if (d % 32) < 16 else sn
    perm = np.empty(64, np.int64)
    for d in range(64):
        perm[d] = d + 16 if (d % 32) < 16 else d - 16
    consts = np.zeros((128, 3, 128), np.float32)
    consts[:, 0, :] = np.eye(128, dtype=np.float32)
    consts[0:64, 1, 0:64] = 1.0
    consts[64:128, 1, 64:128] = 1.0
    consts[:, 2, :] = 1.0 / D
    return tb, wa_sk, wa_mask, cs, perm, consts


_CACHE = {}


def _prep_shared(c_ctx, w_mod, b_mod, ln_g, ln_b, ffn_w_in, ffn_w_out, na_w_qkv, na_w_o, na_rpb,
                 wa_w_qkv, wa_w_o, wa_sinks):
    f = lambda a: np.ascontiguousarray(np.asarray(a, dtype=np.float32))
    tb, wa_sk, wa_mask, cs, perm, consts = _host_tables(np.asarray(na_rpb), np.asarray(wa_sinks))
    wq = f(wa_w_qkv)[0]
    cols = np.concatenate([(h * 64 + perm) for h in range(NH)] + [D + g * 64 + perm for g in range(4)])
    sh = {
        "w_mod": f(w_mod),
        "bmod": f(np.asarray(b_mod).reshape(DEPTH, 72, 128).transpose(2, 0, 1)),
        "lng": f(np.asarray(ln_g).reshape(DEPTH, 3, KC, 128).transpose(3, 0, 1, 2)),
        "lnb": f(np.asarray(ln_b).reshape(DEPTH, 3, KC, 128).transpose(3, 0, 1, 2)),
        "ffn_w_in": f(ffn_w_in),
        "ffn_w_out": f(ffn_w_out),
        "na_w_qkv": f(na_w_qkv)[0],
        "na_w_o": f(na_w_o)[0],
        "na_tb": f(tb),
        "wa_w_qkv": wq,
        "wa_w_perm": f(wq[:, cols]),
        "wa_w_o": f(wa_w_o)[0],
        "wa_sk": f(wa_sk),
        "wa_mask": f(wa_mask),
        "rope_cs": f(cs),
        "consts": f(consts),
    }
    return sh


def run(inputs, stop=None):
    x = np.asarray(inputs["x"], np.float32)
    c = np.asarray(inputs["c"], np.float32)
    ctx = np.asarray(inputs["ctx"], np.float32)
    c_ctx = np.asarray(inputs["c_ctx"], np.float32)
    sh = _prep_shared(c_ctx, *[inputs[k] for k in ("w_mod", "b_mod", "ln_g", "ln_b", "ffn_w_in", "ffn_w_out",
                                                     "na_w_qkv", "na_w_o", "na_rpb", "wa_w_qkv", "wa_w_o", "wa_sinks")])
    key = ("nc", stop)
    if key not in _CACHE:
        _CACHE[key] = build_program(stop)
    nc = _CACHE[key]
    B = x.shape[0]
    in_maps = []
    for b in range(B):
        m = dict(sh)
        m["xT"] = np.ascontiguousarray(x[b].T)
        m["ctxT"] = np.ascontiguousarray(ctx[b].T)
        ccb = np.stack([c[b], c_ctx], -1).reshape(KC, 128, 2).transpose(1, 0, 2)
        m["cc"] = np.ascontiguousarray(ccb)
        in_maps.append(m)
    res = run_bass_kernel_spmd(nc, in_maps, core_ids=list(range(B)))
    outs = [np.asarray(r["outT"]) for r in res.results]
    return np.stack([o.T for o in outs], 0)


def kernel(**inputs):
    out = run(inputs, None)
    return np.ascontiguousarray(out.astype(np.float32))
```

```python
import numpy as np
from contextlib import ExitStack
import concourse.bass as bass
import concourse.mybir as mybir
from concourse.bass_utils import run_bass_kernel_spmd

F32 = mybir.dt.float32
BF16 = mybir.dt.bfloat16
AF = mybir.ActivationFunctionType
ALU = mybir.AluOpType

D = 1024
KC = 8
S = 2048
CT = 256
T = S + CT
FF = 2816
FC = 22
NH = 16
HD = 64
GW = 64
ROWS = 32
DEPTH = 2
ALPHA = (2 * DEPTH) ** 0.25
LN_EPS = 1e-5
NEG = -1e30
SLOT = 4096
NSLOT = 3
LOOKAHEAD = 2
PARTS = [(0, 4), (4, 4), (8, 4), (12, 4), (16, 3), (19, 3)]
MODW = 512
NMOD = 9 * D // MODW
TILES = [(0, 512), (512, 512), (1024, 512), (1536, 512), (2048, 256)]


class Res:
    __slots__ = ("w", "r")

    def __init__(self):
        self.w = None
        self.r = {}


class Sched:
    def __init__(self):
        self.ops = {k: [] for k in ("pe", "act", "dve", "pool", "sp")}
        self.cnt = {k: 0 for k in self.ops}
        self.seen = {k: {} for k in self.ops}
        self.sems = {}
        self.step = {k: 1 for k in self.ops}

    def add_dma_sem(self, name):
        self.cnt[name] = 0
        self.step[name] = 16

    def _deps(self, e, rd, wr):
        deps = {}

        def need(x):
            if x is None:
                return
            en, c = x
            if c > deps.get(en, 0):
                deps[en] = c

        for r in rd:
            need(r.w)
        for w in wr:
            need(w.w)
            for en, c in w.r.items():
                need((en, c))
        waits = []
        seen = self.seen[e]
        for en, c in deps.items():
            if en == e and e == "pe":
                continue
            if seen.get(en, 0) >= c:
                continue
            seen[en] = c
            waits.append((en, c))
        return waits

    def op(self, e, fn, rd=(), wr=()):
        waits = self._deps(e, rd, wr)
        self.cnt[e] += 1
        c = self.cnt[e]
        self.ops[e].append((waits, fn, e, c))
        for r in rd:
            if r.r.get(e, 0) < c:
                r.r[e] = c
        for w in wr:
            w.w = (e, c)
            w.r = {}

    def dma(self, q, fn, dsem, rd=(), wr=()):
        waits = self._deps(q, rd, wr)
        self.cnt[dsem] += 16
        c = self.cnt[dsem]
        self.ops[q].append((waits, fn, dsem, c))
        for r in rd:
            if r.r.get(dsem, 0) < c:
                r.r[dsem] = c
        for w in wr:
            w.w = (dsem, c)
            w.r = {}

    def replay(self, name, eng, final_waits=()):
        for waits, fn, sname, c in self.ops[name]:
            for en, v in waits:
                eng.wait_ge(self.sems[en], v)
            inst = fn(eng)
            inst.then_inc(self.sems[sname], self.step[sname])
        for en, v in final_waits:
            eng.wait_ge(self.sems[en], v)


def build_program(stop=None):
    nc = bass.Bass("TRN2", target_bir_lowering=False)

    def din(name, shape):
        return nc.dram_tensor(name, list(shape), F32, kind="ExternalInput").ap()

    xT = din("xT", [D, S])
    ctxT = din("ctxT", [D, CT])
    cc = din("cc", [128, KC, 2])
    w_mod = din("w_mod", [DEPTH, D, 9 * D])
    bmod = din("bmod", [128, DEPTH, 72])
    lng = din("lng", [128, DEPTH, 3, KC])
    lnb = din("lnb", [128, DEPTH, 3, KC])
    w_in = din("ffn_w_in", [DEPTH, 2, D, 2 * FF])
    w_out = din("ffn_w_out", [DEPTH, 2, FF, D])
    na_qkv = din("na_w_qkv", [D, 3 * D])
    na_o = din("na_w_o", [D, D])
    na_tb = din("na_tb", [128, 8, 15 * 64])
    wa_qkv = din("wa_w_qkv", [D, D + 512])
    wa_perm = din("wa_w_perm", [D, D + 256])
    wa_o = din("wa_w_o", [D, D])
    wa_sk = din("wa_sk", [128, 8])
    wa_mask = din("wa_mask", [128, 320])
    rope_cs = din("rope_cs", [128, 2, S])
    consts = din("consts", [128, 3, 128])
    n_out_tok = T if stop is not None else S
    outT = nc.dram_tensor("outT", [D, n_out_tok], F32, kind="ExternalOutput").ap()

    es = ExitStack()
    with es:
        def sb(name, shape, dt):
            return es.enter_context(nc.sbuf_tensor(name, list(shape), dt))

        hT = sb("hT", [128, KC, T], F32)
        uT = sb("uT", [128, KC, T], BF16)
        ring = sb("ring", [128, NSLOT, SLOT], BF16)
        ARENA = 4 * T + 2 * 1024
        ARENA_N = 25152
        arena = sb("arena", [128, ARENA_N], BF16)
        zb = sb("zb", [128, 2, 512], BF16)
        zq = sb("zq", [128, 2, 512], BF16)
        mean_sb2 = sb("mean_sb", [128, 2, 512], F32)
        tmpA2 = sb("tmpA", [128, 2, 512], F32)
        cst = sb("cst", [128, 3, 128], BF16)
        ccs = sb("ccs", [128, KC, 2], F32)
        scb = sb("scb", [128, KC, 2], BF16)
        bmod_sb = sb("bmod_sb", [128, DEPTH, 72], F32)
        raw = sb("raw", [128, DEPTH, 72, 2], F32)
        Acol = sb("Acol", [128, DEPTH, 3, KC, 2], F32)
        Gcol = sb("Gcol", [128, DEPTH, 3, KC, 2], F32)
        lng_sb = sb("lng_sb", [128, DEPTH, 3, KC], F32)
        lnb_sb = sb("lnb_sb", [128, DEPTH, 3, KC], F32)
        sk_sb = sb("sk_sb", [128, 8], F32)
        es_sb = sb("es_sb", [128, 8], F32)
        zero_col = sb("zero_col", [128, 1], F32)
        banks = [es.enter_context(nc.psum_tensor(f"bank{i}", [128, 512], F32)) for i in range(8)]

        ident = cst[:, 0, :]
        onesbd = cst[:, 1, :]
        onesd = cst[:, 2, :]

        def shaped(flat, shape):
            if len(shape) == 1:
                return flat
            names = " ".join(f"a{i}" for i in range(len(shape)))
            kw = {f"a{i}": int(s_) for i, s_ in enumerate(shape)}
            return flat.rearrange(f"p ({names}) -> p {names}", **kw)

        def aview(off, shape, dt=BF16):
            n = int(np.prod(shape))
            if dt == F32:
                assert off % 2 == 0
                flat = arena[:, off:off + 2 * n].bitcast(F32)
            else:
                flat = arena[:, off:off + n]
            return shaped(flat, shape)

        def ring_view(si, off, shape):
            n = int(np.prod(shape))
            return shaped(ring[:, si, off:off + n], shape)

        gT = aview(0, [4, T])
        sa = aview(4 * T, [2, 512], F32)
        o = 0
        qT = aview(o, [T]); o += T
        OT = aview(o, [2, T]); o += 2 * T
        PT = aview(o, [3, 512]); o += 3 * 512
        rsb = aview(o, [1, 512], F32); o += 1024
        t1 = aview(o, [1, 512], F32); o += 1024
        t2 = aview(o, [1, 512], F32); o += 1024
        ropeC = aview(o, [S]); o += S
        ropeS = aview(o, [S]); o += S
        maskw = aview(o, [320]); o += 320
        assert o >= ARENA
        KBD = aview(o, [36, 128]); o += 36 * 128
        VBD = aview(o, [36, 128]); o += 36 * 128
        assert o <= ARENA_N, o
        assert ARENA <= ARENA_N

        SEMS = {}
        for k in ("pe", "act", "dve", "pool", "sp"):
            SEMS[k] = es.enter_context(nc.semaphore(f"s_{k}"))
        dma_names = [f"slot{i}" for i in range(NSLOT)] + ["init_sp", "init_pool", "aux", "outd"] + [f"init_x{t}" for t in range(5)]
        for k in dma_names:
            SEMS[k] = es.enter_context(nc.semaphore(f"s_{k}"))

        def wrows(ap2d):
            return ap2d.rearrange("(c p) f -> p c f", p=128)

        def in_groups(part):
            j0, n = part
            gs = []
            j = j0
            while j < j0 + n:
                g = min(2, j0 + n - j)
                gs.append((j, g))
                j += g
            return gs

        def emit(sc, plan):
            record = plan is None
            rec = []
            for k in dma_names:
                sc.add_dma_sem(k)

            R_h = [[Res() for _ in TILES] for _ in range(KC)]
            R_u = [Res() for _ in TILES]
            R_g = [[Res() for _ in TILES] for _ in range(4)]
            R_slot = [Res() for _ in range(NSLOT)]
            R_bank = [Res() for _ in range(8)]
            R_sa = [Res(), Res()]
            R_zb = [Res(), Res()]
            R_zq = [Res(), Res()]
            R_mean2 = [Res(), Res()]
            R_tmpA2 = [Res(), Res()]
            R_cst = Res()
            R_small = Res()
            R_scb = Res()
            R_raw = [Res() for _ in range(DEPTH)]
            R_cols = [[Res() for _ in range(3)] for _ in range(DEPTH)]
            R_colA = [[Res() for _ in range(3)] for _ in range(DEPTH)]
            R_ln = Res()
            R_es = Res()
            R_KBD = [Res() for _ in range(5)]
            R_VBD = [Res() for _ in range(9)]
            R_q = [Res() for _ in TILES]
            R_OT = [[Res() for _ in TILES] for _ in range(2)]
            R_PT = [Res() for _ in range(3)]
            R_rsb = [Res()]
            R_t1 = [Res()]
            R_t2 = [Res()]
            R_rope = Res()
            R_rope2 = Res()
            R_mask = Res()

            def h_res(tiles, cs=range(KC)):
                return [R_h[c][t] for c in cs for t in tiles]

            def ffn_arena():
                return [r for row in R_g for r in row] + R_sa

            def att_res():
                return (R_q + R_OT[0] + R_OT[1] + R_PT + R_rsb + R_t1 + R_t2 + [R_rope, R_rope2, R_mask])

            wstate = {"issued": 0, "next": 0}

            def w_issue_upto(n):
                while wstate["issued"] < min(n, len(plan)):
                    i = wstate["issued"]
                    si = i % NSLOT
                    key, dmas_fn = plan[i]
                    for dst, src in dmas_fn(si):
                        sc.dma("pool", (lambda e, dst=dst, src=src: e.dma_start(out=dst, in_=src)),
                               f"slot{si}", wr=[R_slot[si]])
                    wstate["issued"] += 1

            def w_get(key, dmas_fn):
                i = wstate["next"]
                wstate["next"] += 1
                if record:
                    rec.append((key, dmas_fn))
                    return i % NSLOT
                assert plan[i][0] == key, (plan[i][0], key)
                w_issue_upto(i + 1 + LOOKAHEAD)
                return i % NSLOT

            def mod_dmas(l, i):
                src = wrows(w_mod[l][:, i * MODW:(i + 1) * MODW])
                return lambda si: [(ring_view(si, 0, [KC, MODW]), src)]

            def win_dmas(l, w, j, g):
                sa_ = wrows(w_in[l, w][:, j * 128:(j + g) * 128])
                sv_ = wrows(w_in[l, w][:, FF + j * 128:FF + (j + g) * 128])
                return lambda si: [(ring_view(si, 0, [KC, 2, g * 128])[:, :, 0, :], sa_),
                                   (ring_view(si, 0, [KC, 2, g * 128])[:, :, 1, :], sv_)]

            def wout_dmas(l, w, j0, n):
                so = w_out[l, w][j0 * 128:(j0 + n) * 128, :].rearrange("(j p) d -> p j d", p=128)
                return lambda si: [(ring_view(si, 0, [n, D]), so)]

            def qkv_dmas(l, pr):
                if l == 0:
                    def f(si):
                        dm = []
                        for i in range(3):
                            src = wrows(na_qkv[:, i * D + pr * 128:i * D + (pr + 1) * 128])
                            dm.append((ring_view(si, 0, [KC, 3, 128])[:, :, i, :], src))
                        dm.append((ring_view(si, 3072, [960]), na_tb[:, pr, :]))
                        return dm
                    return f
                def f(si):
                    rv = ring_view(si, 0, [KC, 256])
                    return [(rv[:, :, 0:128], wrows(wa_qkv[:, pr * 128:(pr + 1) * 128])),
                            (rv[:, :, 128:256], wrows(wa_perm[:, pr * 128:(pr + 1) * 128]))]
                return f

            def kv_dmas(pr):
                g = pr // 2

                def f(si):
                    rv = ring_view(si, 0, [KC, 384])
                    kq = wrows(wa_qkv[:, D + g * 64:D + (g + 1) * 64])
                    kp = wrows(wa_perm[:, D + g * 64:D + (g + 1) * 64])
                    vq = wrows(wa_qkv[:, D + 256 + g * 64:D + 256 + (g + 1) * 64])
                    return [(rv[:, :, 0:64], kq), (rv[:, :, 64:128], kq),
                            (rv[:, :, 128:192], kp), (rv[:, :, 192:256], kp),
                            (rv[:, :, 256:320], vq), (rv[:, :, 320:384], vq)]
                return f

            def wo_dmas(l, q):
                wo = na_o if l == 0 else wa_o
                src = wo[2 * q * 128:(2 * q + 2) * 128, :].rearrange("(j p) d -> p j d", p=128)
                return lambda si: [(ring_view(si, 0, [2, D]), src)]

            sc.dma("pool", lambda e: e.dma_start(out=cst[:], in_=consts), "init_pool", wr=[R_cst])
            sc.dma("sp", lambda e: e.dma_start(out=ccs[:], in_=cc), "init_sp", wr=[Res()])
            sc.dma("sp", lambda e: e.dma_start(out=bmod_sb[:], in_=bmod), "init_sp", wr=[Res()])
            sc.dma("sp", lambda e: e.dma_start(out=lng_sb[:], in_=lng), "init_sp", wr=[Res()])
            sc.dma("sp", lambda e: e.dma_start(out=lnb_sb[:], in_=lnb), "init_sp", wr=[Res()])
            sc.dma("sp", lambda e: e.dma_start(out=sk_sb[:], in_=wa_sk), "init_sp", wr=[Res()])
            R_small.w = ("init_sp", sc.cnt["init_sp"])
            for t in range(4):
                off_, n_ = TILES[t]
                sc.dma("sp", lambda e, off_=off_, n_=n_: e.dma_start(
                    out=hT[:, :, off_:off_ + n_], in_=xT.rearrange("(c p) t -> p c t", p=128)[:, :, off_:off_ + n_]),
                    f"init_x{t}", wr=[R_h[c][t] for c in range(KC)])
            sc.dma("sp", lambda e: e.dma_start(out=hT[:, :, S:T], in_=ctxT.rearrange("(c p) t -> p c t", p=128)),
                   "init_x4", wr=[R_h[c][4] for c in range(KC)])
            if not record:
                w_issue_upto(LOOKAHEAD)

            sc.op("act", lambda e: e.activation(out=scb[:], in_=ccs[:], func=AF.Silu), rd=[R_small], wr=[R_scb])
            sc.op("dve", lambda e: e.memset(zero_col[:], 0.0), wr=[R_es])
            sc.op("act", lambda e: e.activation(out=es_sb[:], in_=sk_sb[:], func=AF.Exp), rd=[R_small, R_es], wr=[R_es])
            for l in range(DEPTH):
                for i in range(3):
                    a = 1.0 if (l == DEPTH - 1 and i == 2) else ALPHA
                    sc.op("dve", lambda e, l=l, i=i, a=a: e.tensor_scalar(
                        out=lng_sb[:, l, i, :], in0=lng_sb[:, l, i, :], scalar1=a, scalar2=None, op0=ALU.mult),
                        rd=[R_small], wr=[R_ln])
                    sc.op("dve", lambda e, l=l, i=i, a=a: e.tensor_scalar(
                        out=lnb_sb[:, l, i, :], in0=lnb_sb[:, l, i, :], scalar1=a, scalar2=None, op0=ALU.mult),
                        rd=[R_small], wr=[R_ln])
            for t in range(5):
                off_, n_ = TILES[t]
                if t % 2 == 0:
                    sc.op("act", lambda e, off_=off_, n_=n_: e.activation(out=hT[:, :, off_:off_ + n_], in_=hT[:, :, off_:off_ + n_],
                                                                          func=AF.Copy, scale=ALPHA),
                          rd=[], wr=h_res([t]))
                else:
                    sc.op("dve", lambda e, off_=off_, n_=n_: e.tensor_scalar(out=hT[:, :, off_:off_ + n_], in0=hT[:, :, off_:off_ + n_],
                                                                             scalar1=ALPHA, scalar2=None, op0=ALU.mult),
                          rd=[], wr=h_res([t]))

            modq = [(l, i) for l in range(DEPTH) for i in range(NMOD)]

            def mod_step():
                if not modq:
                    return
                l, i = modq.pop(0)
                bk = 6
                si = w_get(("mod", l, i), mod_dmas(l, i))
                wv = ring_view(si, 0, [KC, MODW])
                NJ = MODW // 128

                def f(e):
                    last = None
                    for jj in range(NJ):
                        for kc in range(KC):
                            last = e.matmul(banks[bk][:, 2 * jj:2 * jj + 2], lhsT=wv[:, kc, jj * 128:(jj + 1) * 128],
                                            rhs=scb[:, kc, :], start=(kc == 0), stop=(kc == KC - 1))
                    return last
                sc.op("pe", f, rd=[R_slot[si], R_scb], wr=[R_bank[bk]])
                sc.op("dve", lambda e: e.tensor_tensor(
                    out=raw[:, l, NJ * i:NJ * i + NJ, :], in0=banks[bk][:, 0:2 * NJ].rearrange("p (j s) -> p j s", s=2),
                    in1=bmod_sb[:, l, NJ * i:NJ * i + NJ].unsqueeze(2).broadcast_to([128, NJ, 2]), op=ALU.add),
                    rd=[R_bank[bk], R_small], wr=[R_raw[l]])
                LPS = NMOD // 3
                ii = i // LPS
                n0 = 3 * ii
                if i % LPS == 3:
                    sc.op("dve", lambda e: e.tensor_scalar(
                        out=Acol[:, l, ii, :, :], in0=raw[:, l, (n0 + 1) * 8:(n0 + 2) * 8, :], scalar1=1.0,
                        scalar2=1.0 / ALPHA, op0=ALU.add, op1=ALU.mult), rd=[R_raw[l]], wr=[R_colA[l][ii]])
                if i % LPS == LPS - 1:
                    wres = 1.0 if ii == 1 else 0.5
                    sc.op("dve", lambda e: e.tensor_scalar(
                        out=Gcol[:, l, ii, :, :], in0=raw[:, l, (n0 + 2) * 8:(n0 + 3) * 8, :], scalar1=wres,
                        scalar2=None, op0=ALU.mult), rd=[R_raw[l]], wr=[R_cols[l][ii]])

            def mod_ensure(l, ii):
                while modq and (modq[0][0] < l or (modq[0][0] == l and modq[0][1] < (NMOD // 3) * (ii + 1))):
                    mod_step()

            pend = {}

            def advance(t):
                lst = pend.get(t)
                if lst:
                    lst.pop(0)()
                    if not lst:
                        del pend[t]

            def flush(t):
                while t in pend:
                    advance(t)

            def flush_all():
                for t in sorted(list(pend.keys())):
                    flush(t)

            def modulate_tile(l, i, t, pool_ok=True):
                n0 = 3 * i
                off, n = TILES[t]
                s_ = 1 if t == 4 else 0
                for c in range(KC):
                    if (c < 2 or (not pool_ok and c < 5)) and t != 4:
                        sc.op("dve", lambda e, c=c: e.tensor_scalar(
                            out=uT[:, c, off:off + n], in0=hT[:, c, off:off + n], scalar1=Acol[:, l, i, c, s_:s_ + 1],
                            scalar2=raw[:, l, n0 * 8 + c, s_:s_ + 1], op0=ALU.mult, op1=ALU.add),
                            rd=[R_h[c][t], R_colA[l][i], R_raw[l]], wr=[R_u[t]])
                    elif t == 4 or not pool_ok:
                        sc.op("act", lambda e, c=c: e.activation(
                            out=uT[:, c, off:off + n], in_=hT[:, c, off:off + n], func=AF.Identity,
                            scale=Acol[:, l, i, c, s_:s_ + 1], bias=raw[:, l, n0 * 8 + c, s_:s_ + 1]),
                            rd=[R_h[c][t], R_colA[l][i], R_raw[l]], wr=[R_u[t]])
                    else:
                        sc.op("pool", lambda e, c=c: e.tensor_scalar(
                            out=uT[:, c, off:off + n], in0=hT[:, c, off:off + n], scalar1=Acol[:, l, i, c, s_:s_ + 1],
                            scalar2=raw[:, l, n0 * 8 + c, s_:s_ + 1], op0=ALU.mult, op1=ALU.add),
                            rd=[R_h[c][t], R_colA[l][i], R_raw[l]], wr=[R_u[t]])

            def ln_stats(l, i, t, k):
                off, n = TILES[t]
                mean_sb = mean_sb2[:, k, :]
                tmpA = tmpA2[:, k, :]
                R_mean, R_tmpA = R_mean2[k], R_tmpA2[k]
                for c in range(KC):
                    b = c % 2
                    sc.op("act", lambda e, c=c, b=b: e.activation(out=zb[:, b, 0:n], in_=hT[:, c, off:off + n], func=AF.Copy),
                          rd=[R_h[c][t]], wr=[R_zb[b]])
                    sc.op("act", lambda e, c=c, b=b: e.activation(out=zq[:, b, 0:n], in_=hT[:, c, off:off + n], func=AF.Square),
                          rd=[R_h[c][t]], wr=[R_zq[b]])

                    def f(e, c=c, b=b):
                        e.matmul(banks[6][:, 0:n], lhsT=onesd, rhs=zb[:, b, 0:n], start=(c == 0), stop=(c == KC - 1))
                        return e.matmul(banks[7][:, 0:n], lhsT=onesd, rhs=zq[:, b, 0:n], start=(c == 0), stop=(c == KC - 1))
                    sc.op("pe", f, rd=[R_zb[b], R_zq[b], R_cst], wr=[R_bank[6], R_bank[7]])

            def ln_stats2(l, i, t, k):
                off, n = TILES[t]
                mean_sb = mean_sb2[:, k, :]
                tmpA = tmpA2[:, k, :]
                R_mean, R_tmpA = R_mean2[k], R_tmpA2[k]
                sc.op("act", lambda e: e.activation(out=mean_sb[:, 0:n], in_=banks[6][:, 0:n], func=AF.Copy),
                      rd=[R_bank[6]], wr=[R_mean])
                sc.op("dve", lambda e: e.tensor_tensor(out=tmpA[:, 0:n], in0=mean_sb[:, 0:n], in1=mean_sb[:, 0:n], op=ALU.mult),
                      rd=[R_mean], wr=[R_tmpA])
                sc.op("dve", lambda e: e.tensor_tensor(out=tmpA[:, 0:n], in0=banks[7][:, 0:n], in1=tmpA[:, 0:n], op=ALU.subtract),
                      rd=[R_bank[7], R_tmpA], wr=[R_tmpA])
                sc.op("act", lambda e: e.activation(out=tmpA[:, 0:n], in_=tmpA[:, 0:n], func=AF.Ln, bias=eps_col[:, 0:1]),
                      rd=[R_tmpA, R_es], wr=[R_tmpA])
                sc.op("act", lambda e: e.activation(out=tmpA[:, 0:n], in_=tmpA[:, 0:n], func=AF.Exp, scale=-0.5),
                      rd=[R_tmpA], wr=[R_tmpA])

            def ln_apply(l, i, t, k, nxt, late=False):
                off, n = TILES[t]
                mean_sb = mean_sb2[:, k, :]
                tmpA = tmpA2[:, k, :]
                R_mean, R_tmpA = R_mean2[k], R_tmpA2[k]
                for c in range(KC):
                    hv = hT[:, c, off:off + n]
                    sc.op("dve" if late else "pool",
                          lambda e, hv=hv: e.tensor_tensor(out=hv, in0=hv, in1=mean_sb[:, 0:n], op=ALU.subtract),
                          rd=[R_mean, R_h[c][t]], wr=[R_h[c][t]])
                    sc.op("dve", lambda e, hv=hv: e.tensor_tensor(out=hv, in0=hv, in1=tmpA[:, 0:n], op=ALU.mult),
                          rd=[R_tmpA, R_h[c][t]], wr=[R_h[c][t]])
                    sc.op("act", lambda e, hv=hv, c=c: e.activation(out=hv, in_=hv, func=AF.Identity,
                                                                    scale=lng_sb[:, l, i, c:c + 1], bias=lnb_sb[:, l, i, c:c + 1]),
                          rd=[R_ln, R_h[c][t]], wr=[R_h[c][t]])
                if nxt is not None and t in nxt[2]:
                    assert not (modq and (modq[0][0] < nxt[0] or (modq[0][0] == nxt[0] and modq[0][1] < (NMOD // 3) * (nxt[1] + 1))))
                    modulate_tile(nxt[0], nxt[1], t, pool_ok=not late)
                if nxt is None and stop is None:
                    sc.dma("sp", lambda e: e.dma_start(out=outT.rearrange("(c p) t -> p c t", p=128)[:, :, off:off + n],
                                                       in_=hT[:, :, off:off + n]),
                           "outd", rd=[R_h[c][t] for c in range(KC)])

            cnt = {"sa": 0, "av": 0, "y": 0}

            def z_update(l, i, c, t, bank):
                off, n = TILES[t]
                s_ = 1 if t == 4 else 0
                hv = hT[:, c, off:off + n]
                sc.op("dve", lambda e: e.scalar_tensor_tensor(out=hv, in0=banks[bank][:, 0:n], scalar=Gcol[:, l, i, c, s_:s_ + 1],
                                                              in1=hv, op0=ALU.mult, op1=ALU.add),
                      rd=[R_bank[bank], R_cols[l][i], R_h[c][t]], wr=[R_h[c][t]])

            def tail_pipeline(l, i, tiles, nxt, emit_outproj):
                tl = list(tiles)
                for k, t in enumerate(tl):
                    emit_outproj(t)
                    pend[t] = [(lambda t=t, k=k: ln_stats(l, i, t, k % 2)),
                               (lambda t=t, k=k: ln_stats2(l, i, t, k % 2)),
                               (lambda t=t, k=k: ln_apply(l, i, t, k % 2, nxt, late=(k == len(tl) - 1)))]
                    if k >= 1:
                        advance(tl[k - 1])
                    if k >= 2:
                        advance(tl[k - 2])
                    if k >= 1:
                        advance(tl[k - 1])
                advance(tl[-1])
                if len(tl) >= 2:
                    advance(tl[-2])
                advance(tl[-1])

            def ffn(l, w, tiles, nxt):
                i = 0 if w == 0 else 2
                tiles = list(tiles)
                first = True
                for pi, part in enumerate(PARTS):
                    j0, npart = part
                    for (j, g) in in_groups(part):
                        si = w_get(("win", l, w, j), win_dmas(l, w, j, g))
                        wv = ring_view(si, 0, [KC, 2, g * 128])
                        for jj in range(g):
                            jl = j + jj - j0
                            for t in tiles:
                                flush(t)
                                off, n = TILES[t]
                                k = cnt["av"]; cnt["av"] += 1
                                ba, bv = (k % 2), 2 + (k % 2)

                                def f(e, jj=jj, off=off, n=n, ba=ba, bv=bv, wv=wv):
                                    for kc in range(KC):
                                        e.matmul(banks[ba][:, 0:n], lhsT=wv[:, kc, 0, jj * 128:(jj + 1) * 128],
                                                 rhs=uT[:, kc, off:off + n], start=(kc == 0), stop=(kc == KC - 1))
                                    last = None
                                    for kc in range(KC):
                                        last = e.matmul(banks[bv][:, 0:n], lhsT=wv[:, kc, 1, jj * 128:(jj + 1) * 128],
                                                        rhs=uT[:, kc, off:off + n], start=(kc == 0), stop=(kc == KC - 1))
                                    return last
                                rdx = ffn_arena() if first else []
                                sc.op("pe", f, rd=[R_slot[si], R_u[t]], wr=[R_bank[ba], R_bank[bv]])
                                b = cnt["sa"] % 2; cnt["sa"] += 1
                                sc.op("act", lambda e, b=b, ba=ba, n=n: e.activation(out=sa[:, b, 0:n], in_=banks[ba][:, 0:n], func=AF.Silu),
                                      rd=[R_bank[ba]], wr=[R_sa[b]])
                                sc.op("dve", lambda e, b=b, bv=bv, n=n, jl=jl, off=off: e.tensor_tensor(
                                    out=gT[:, jl, off:off + n], in0=banks[bv][:, 0:n], in1=sa[:, b, 0:n], op=ALU.mult),
                                    rd=[R_bank[bv], R_sa[b]], wr=[R_g[jl][t]])
                                first = False
                        mod_step()
                    mod_ensure(l, i)
                    si = w_get(("wout", l, w, pi), wout_dmas(l, w, j0, npart))
                    wo = ring_view(si, 0, [npart, D])

                    last_part = (pi == len(PARTS) - 1)

                    def outproj(c, t, si=si, wo=wo, npart=npart, last_part=last_part):
                        off, n = TILES[t]
                        bank = (cnt["y"] % 6) if last_part else (4 + cnt["y"] % 4)
                        cnt["y"] += 1

                        def f(e):
                            last = None
                            for jl in range(npart):
                                last = e.matmul(banks[bank][:, 0:n], lhsT=wo[:, jl, c * 128:(c + 1) * 128],
                                                rhs=gT[:, jl, off:off + n], start=(jl == 0), stop=(jl == npart - 1))
                            return last
                        sc.op("pe", f, rd=[R_slot[si]] + [R_g[jl][t] for jl in range(npart)], wr=[R_bank[bank]])
                        z_update(l, i, c, t, bank)
                    if pi < len(PARTS) - 1:
                        for c in range(KC):
                            for t in tiles:
                                outproj(c, t)
                    else:
                        tail_pipeline(l, i, tiles, nxt, lambda t: [outproj(c, t) for c in range(KC)])
                    mod_step()

            def rs_na(r):
                return min(max(r - 4, 0), ROWS - 8)

            def na_items(qb):
                if qb == 4:
                    return [(32 + k, 0, 256, None) for k in range(4)]
                items = [(32 + k, 0, 512, None) for k in range(4)]
                for kr in range(ROWS):
                    rr = [r for r in range(8 * qb, 8 * qb + 8) if rs_na(r) <= kr <= rs_na(r) + 7]
                    if not rr:
                        continue
                    ra, rb = rr[0], rr[-1]
                    assert rr == list(range(ra, rb + 1))
                    e0 = ra - kr + 7
                    assert 0 <= e0 and e0 + (rb - ra) <= 14
                    items.append((kr, (ra - 8 * qb) * 64, (rb - 8 * qb + 1) * 64, ("tb", e0 * 64, (e0 + rb - ra + 1) * 64)))
                return items

            def wa_items(qb):
                items = [(32 + k, 0, 512, None) for k in range(4)]
                for kt in range(32):
                    lo = 64 * kt - 128
                    g0 = max(lo, 512 * qb, 0)
                    g1 = min(lo + 320, 512 * qb + 512, S)
                    if g1 <= g0:
                        continue
                    items.append((kt, g0 - 512 * qb, g1 - 512 * qb, ("mask", g0 - lo, g1 - lo)))
                return items

            def attention_pair(l, pr, pl, si, qbanks):
                exp_scale = 1.0 if l == 0 else 0.125
                flat = []
                for qi, qb in enumerate(qbanks):
                    its = na_items(qb) if l == 0 else wa_items(qb)
                    for ii, it in enumerate(its):
                        flat.append((qi, qb, ii, len(its), it))
                nI = len(flat)
                sbank = [None] * nI

                def emit_qk(x):
                    qi, qb, ii, nits, (kt, q0, q1, bias) = flat[x]
                    bk = x % 4
                    sbank[x] = bk
                    qoff = TILES[qb][0]
                    n = q1 - q0

                    extra = []
                    if bias is not None:
                        if bias[0] == "tb":
                            extra.append((0, n, ring_view(si, 3072, [960])[:, bias[1]:bias[2]]))
                        else:
                            for (lo, hi) in ((0, 63), (257, 320)):
                                a_, b_ = max(bias[1], lo), min(bias[2], hi)
                                if a_ < b_:
                                    extra.append((a_ - bias[1], b_ - bias[1], maskw[:, a_:b_]))

                    def f(e):
                        last = e.matmul(banks[bk][:, 0:n], lhsT=KBD[:, kt, :], rhs=qT[:, qoff + q0:qoff + q1],
                                        start=True, stop=(len(extra) == 0))
                        for xi, (c0, c1, bap) in enumerate(extra):
                            last = e.matmul(banks[bk][:, c0:c1], lhsT=ident, rhs=bap, start=False,
                                            stop=(xi == len(extra) - 1))
                        return last
                    rd = [R_KBD[kt // 8], R_q[qb], R_cst]
                    if bias is not None:
                        rd.append(R_slot[si] if bias[0] == "tb" else R_mask)
                    sc.op("pe", f, rd=rd, wr=[R_bank[bk]])

                LA = 2
                for x in range(min(LA, nI)):
                    emit_qk(x)
                pending = []
                for x in range(nI):
                    qi, qb, ii, nits, (kt, q0, q1, bias) = flat[x]
                    n = q1 - q0
                    bk = sbank[x]
                    pb = x % 3
                    while pending and pending[0][0] <= x:
                        pending.pop(0)[1]()
                    sc.op("act", lambda e, bk=bk, pb=pb, n=n: e.activation(out=PT[:, pb, 0:n], in_=banks[bk][:, 0:n],
                                                                           func=AF.Exp, scale=exp_scale),
                          rd=[R_bank[bk]], wr=[R_PT[pb]])
                    if x + LA < nI:
                        emit_qk(x + LA)
                    ob = 4 + 2 * (qi % 2)

                    def f(e, kt=kt, q0=q0, q1=q1, n=n, pb=pb, ob=ob, ii=ii, nits=nits):
                        e.matmul(banks[ob][:, q0:q1], lhsT=VBD[:, kt, :], rhs=PT[:, pb, 0:n],
                                 start=(ii == 0), stop=(ii == nits - 1))
                        return e.matmul(banks[ob + 1][:, q0:q1], lhsT=onesbd, rhs=PT[:, pb, 0:n],
                                        start=(ii == 0), stop=(ii == nits - 1))
                    sc.op("pe", f, rd=[R_VBD[kt // 4], R_PT[pb], R_cst], wr=[R_bank[ob], R_bank[ob + 1]])
                    if ii == nits - 1:
                        qoff, nq = TILES[qb]
                        escol = es_sb[:, pr:pr + 1] if l == 1 else zero_col[:, 0:1]

                        def norm(ob=ob, nq=nq, escol=escol, qoff=qoff, qb=qb):
                            sc.op("act", lambda e: e.activation(
                                out=rsb[:, 0, 0:nq], in_=banks[ob + 1][:, 0:nq], func=AF.Ln, bias=escol),
                                rd=[R_bank[ob + 1], R_es], wr=[R_rsb[0]])
                            sc.op("act", lambda e: e.activation(out=rsb[:, 0, 0:nq], in_=rsb[:, 0, 0:nq], func=AF.Exp, scale=-1.0),
                                  rd=[R_rsb[0]], wr=[R_rsb[0]])
                            sc.op("dve", lambda e: e.tensor_tensor(
                                out=OT[:, pl, qoff:qoff + nq], in0=banks[ob][:, 0:nq], in1=rsb[:, 0, 0:nq], op=ALU.mult),
                                rd=[R_bank[ob], R_rsb[0]], wr=[R_OT[pl][qb]])
                        pending.append((x + 3, norm))
                for _, fn in pending:
                    fn()

            def qkv_pair(l, pr, si, qtiles, part):
                pc = {"k": 0}

                def nb():
                    b = pc["k"] % 4
                    pc["k"] += 1
                    return b
                if l == 0:
                    wv = ring_view(si, 0, [KC, 3, 128])
                    wq = lambda kc: wv[:, kc, 0, :]
                    wk = lambda kc: wv[:, kc, 1, :]
                    wvv = lambda kc: wv[:, kc, 2, :]
                elif part == "q":
                    wv = ring_view(si, 0, [KC, 256])
                    wq = lambda kc: wv[:, kc, 0:128]
                    wqp = lambda kc: wv[:, kc, 128:256]
                else:
                    wv = ring_view(si, 0, [KC, 384])
                    wk = lambda kc: wv[:, kc, 0:128]
                    wkp = lambda kc: wv[:, kc, 128:256]
                    wvv = lambda kc: wv[:, kc, 256:384]

                def proj(bank, wf, off, n):
                    def f(e):
                        last = None
                        for kc in range(KC):
                            last = e.matmul(banks[bank][:, 0:n], lhsT=wf(kc), rhs=uT[:, kc, off:off + n],
                                            start=(kc == 0), stop=(kc == KC - 1))
                        return last
                    return f

                def rope(bq, bp, off, n, dst_fn):
                    sc.op("dve", lambda e: e.tensor_tensor(out=t1[:, 0, 0:n], in0=banks[bq][:, 0:n], in1=ropeC[:, off:off + n], op=ALU.mult),
                          rd=[R_bank[bq], R_rope], wr=[R_t1[0]])
                    sc.op("dve", lambda e: e.tensor_tensor(out=t2[:, 0, 0:n], in0=banks[bp][:, 0:n], in1=ropeS[:, off:off + n], op=ALU.mult),
                          rd=[R_bank[bp], R_rope2], wr=[R_t2[0]])
                    dst_fn()

                for t in (qtiles if part == "q" else []):
                    flush(t)
                    off, n = TILES[t]
                    b = nb()
                    sc.op("pe", proj(b, wq, off, n), rd=[R_slot[si], R_u[t]], wr=[R_bank[b]])
                    if l == 0:
                        sc.op("act", lambda e, b=b, off=off, n=n: e.activation(out=qT[:, off:off + n], in_=banks[b][:, 0:n],
                                                                               func=AF.Copy, scale=0.125),
                              rd=[R_bank[b]], wr=[R_q[t]])
                    else:
                        b2 = nb()
                        sc.op("pe", proj(b2, wqp, off, n), rd=[R_slot[si], R_u[t]], wr=[R_bank[b2]])

                        def dst(off=off, n=n, t=t):
                            sc.op("dve", lambda e: e.tensor_tensor(out=qT[:, off:off + n], in0=t1[:, 0, 0:n], in1=t2[:, 0, 0:n], op=ALU.add),
                                  rd=[R_t1[0], R_t2[0]], wr=[R_q[t]])
                        rope(b, b2, off, n, dst)
                if part != "kv":
                    return
                for t in range(5):
                    flush(t)
                    off, n = TILES[t]
                    nk = n // 64
                    kt0 = off // 64
                    b = nb()
                    sc.op("pe", proj(b, wk, off, n), rd=[R_slot[si], R_u[t]], wr=[R_bank[b]])
                    dA = KBD[0:64, kt0:kt0 + nk, 0:64]
                    dB = KBD[64:128, kt0:kt0 + nk, 64:128]
                    if l == 0 or t == 4:
                        sA = banks[b][0:64, 0:n].rearrange("p (k c) -> p k c", c=64)
                        sB = banks[b][64:128, 0:n].rearrange("p (k c) -> p k c", c=64)
                        sc.op("act", lambda e, dA=dA, sA=sA: e.activation(out=dA, in_=sA, func=AF.Copy),
                              rd=[R_bank[b]], wr=[R_KBD[t]])
                        sc.op("dve", lambda e, dB=dB, sB=sB: e.tensor_copy(out=dB, in_=sB), rd=[R_bank[b]], wr=[R_KBD[t]])
                    else:
                        b2 = nb()
                        sc.op("pe", proj(b2, wkp, off, n), rd=[R_slot[si], R_u[t]], wr=[R_bank[b2]])

                        def dst(dA=dA, dB=dB, n=n, t=t):
                            sc.op("dve", lambda e: e.tensor_tensor(
                                out=dA, in0=t1[0:64, 0, 0:n].rearrange("p (k c) -> p k c", c=64),
                                in1=t2[0:64, 0, 0:n].rearrange("p (k c) -> p k c", c=64), op=ALU.add),
                                rd=[R_t1[0], R_t2[0]], wr=[R_KBD[t]])
                            sc.op("dve", lambda e: e.tensor_tensor(
                                out=dB, in0=t1[64:128, 0, 0:n].rearrange("p (k c) -> p k c", c=64),
                                in1=t2[64:128, 0, 0:n].rearrange("p (k c) -> p k c", c=64), op=ALU.add),
                                rd=[R_t1[0], R_t2[0]], wr=[R_KBD[t]])
                        rope(b, b2, off, n, dst)
                for gi in range(9):
                    b = nb()

                    def f(e, gi=gi, b=b):
                        last = None
                        for k4 in range(4):
                            m = gi * 4 + k4
                            if m < 35:
                                for kc in range(KC):
                                    last = e.matmul(banks[b][:, k4 * 128:(k4 + 1) * 128], lhsT=uT[:, kc, 64 * m:64 * m + 128],
                                                    rhs=wvv(kc), start=(kc == 0), stop=(kc == KC - 1))
                            else:
                                for kc in range(KC):
                                    e.matmul(banks[b][0:64, k4 * 128:k4 * 128 + 64], lhsT=uT[:, kc, 64 * m:64 * m + 64],
                                             rhs=wvv(kc)[:, 0:64], start=(kc == 0), stop=(kc == KC - 1))
                                for kc in range(KC):
                                    last = e.matmul(banks[b][64:128, k4 * 128 + 64:k4 * 128 + 128], lhsT=uT[:, kc, 0:64],
                                                    rhs=wvv(kc)[:, 64:128], start=(kc == 0), stop=(kc == KC - 1))
                        return last
                    sc.op("pe", f, rd=[R_slot[si]] + R_u, wr=[R_bank[b]])
                    bv = banks[b][:, :].rearrange("p (k c) -> p k c", c=128)
                    sA = bv[0:64, :, 0:64]
                    dA = VBD[0:64, gi * 4:gi * 4 + 4, 0:64]
                    sc.op("act", lambda e, dA=dA, sA=sA: e.activation(out=dA, in_=sA, func=AF.Copy), rd=[R_bank[b]], wr=[R_VBD[gi]])
                    if gi < 8:
                        sB = bv[64:128, :, 64:128]
                        dB = VBD[64:128, gi * 4 + 1:gi * 4 + 5, 64:128]
                        sc.op("dve", lambda e, dB=dB, sB=sB: e.tensor_copy(out=dB, in_=sB), rd=[R_bank[b]],
                              wr=[R_VBD[gi], R_VBD[gi + 1]])
                    else:
                        sB = bv[64:128, 0:3, 64:128]
                        dB = VBD[64:128, 33:36, 64:128]
                        sc.op("dve", lambda e, dB=dB, sB=sB: e.tensor_copy(out=dB, in_=sB), rd=[R_bank[b]], wr=[R_VBD[8]])
                        sB2 = bv[64:128, 3, 64:128]
                        dB2 = VBD[64:128, 0, 64:128]
                        sc.op("dve", lambda e, dB2=dB2, sB2=sB2: e.tensor_copy(out=dB2, in_=sB2), rd=[R_bank[b]], wr=[R_VBD[0]])

            def mixer(l, ctx_out, nxt):
                i = 1
                mod_ensure(l, i)
                qtiles = list(range(5)) if ctx_out else list(range(4))
                if l == 1:
                    sc.op("dve", lambda e: e.memset(rsb[:, 0, 0:2], 0.0), rd=[], wr=[R_rope, R_rope2, R_mask] + R_rsb + ffn_arena())
                    sc.dma("pool", lambda e: e.dma_start(out=ropeC[:], in_=rope_cs[:, 0, :]), "aux", wr=[R_rope])
                    sc.dma("pool", lambda e: e.dma_start(out=ropeS[:], in_=rope_cs[:, 1, :]), "aux", wr=[R_rope2])
                    sc.dma("pool", lambda e: e.dma_start(out=maskw[:], in_=wa_mask), "aux", wr=[R_mask])
                for pr in range(8):
                    if l == 0:
                        si = w_get(("qkv", l, pr), qkv_dmas(l, pr))
                        qkv_pair(l, pr, si, qtiles, "kv")
                        qkv_pair(l, pr, si, qtiles, "q")
                    else:
                        if pr % 2 == 0:
                            skv = w_get(("kv", l, pr), kv_dmas(pr))
                            qkv_pair(l, pr, skv, qtiles, "kv")
                            mod_step()
                        si = w_get(("qkv", l, pr), qkv_dmas(l, pr))
                        qkv_pair(l, pr, si, qtiles, "q")
                    pl = pr % 2
                    attention_pair(l, pr, pl, si, qtiles)
                    mod_step()
                    if pl == 0:
                        continue
                    so = w_get(("wo", l, pr // 2), wo_dmas(l, pr // 2))
                    wo = ring_view(so, 0, [2, D])
                    lastp = (pr == 7)

                    def outproj(c, t, so=so, wo=wo, lastp=lastp):
                        off, n = TILES[t]
                        bank = (cnt["y"] % 6) if lastp else (4 + cnt["y"] % 4)
                        cnt["y"] += 1

                        def f(e):
                            e.matmul(banks[bank][:, 0:n], lhsT=wo[:, 0, c * 128:(c + 1) * 128], rhs=OT[:, 0, off:off + n],
                                     start=True, stop=False)
                            return e.matmul(banks[bank][:, 0:n], lhsT=wo[:, 1, c * 128:(c + 1) * 128], rhs=OT[:, 1, off:off + n],
                                            start=False, stop=True)
                        sc.op("pe", f, rd=[R_slot[so], R_OT[0][t], R_OT[1][t]], wr=[R_bank[bank]])
                        z_update(l, i, c, t, bank)
                    if not lastp:
                        for c in range(KC):
                            for t in qtiles:
                                outproj(c, t)
                    else:
                        tail_pipeline(l, i, qtiles, nxt, lambda t: [outproj(c, t) for c in range(KC)])
                    mod_step()

            def arena_to_ffn():
                sc.op("dve", lambda e: e.memset(sa[:, 0, 0:2], 0.0), rd=[], wr=att_res() + ffn_arena())

            sc.op("dve", lambda e: e.memset(eps_col[:], LN_EPS), wr=[R_es])
            sc.op("dve", lambda e: e.memset(KBD[:, :, :], 0.0), rd=[], wr=R_KBD)
            sc.op("dve", lambda e: e.memset(VBD[:, :, :], 0.0), rd=[], wr=R_VBD)
            for _ in range(4):
                mod_step()
            for t in range(5):
                modulate_tile(0, 0, t)
            seq = [("ffn", 0, 0), ("mix", 0), ("ffn", 0, 1), ("ffn", 1, 0), ("mix", 1), ("ffn", 1, 1)]
            stops = {"ffn00": 1, "mix0": 2, "ffn01": 3, "ffn10": 4, "mix1": 5, None: 6}
            seq = seq[:stops[stop]]
            for k, sub in enumerate(seq):
                nxt = None
                if k + 1 < len(seq):
                    ns = seq[k + 1]
                    if ns[0] == "mix":
                        nxt = (ns[1], 1, list(range(5)))
                    else:
                        ni = 0 if ns[2] == 0 else 2
                        ntl = list(range(4)) if (ns[1] == 1 and ns[2] == 1) else list(range(5))
                        nxt = (ns[1], ni, ntl)
                if sub[0] == "ffn":
                    if k > 0 and seq[k - 1][0] == "mix":
                        arena_to_ffn()
                    tl = range(4) if (sub[1] == 1 and sub[2] == 1) else range(5)
                    ffn(sub[1], sub[2], tl, nxt)
                else:
                    mixer(sub[1], sub[1] == 0, nxt)
            flush_all()

            ntile_out = 5 if stop is not None else 4
            for c in (range(KC) if stop is not None else []):
                sc.dma("sp", lambda e, c=c: e.dma_start(out=outT[c * 128:(c + 1) * 128, :], in_=hT[:, c, 0:n_out_tok]),
                       "outd", rd=[R_h[c][t] for t in range(ntile_out)])

            for coarse in ("init_sp", "aux"):
                tot = sc.cnt[coarse]
                for k in sc.ops:
                    sc.ops[k] = [([(en, (tot if en == coarse else v)) for en, v in waits], fn, sn, c)
                                 for waits, fn, sn, c in sc.ops[k]]
            return rec

        eps_col = sb("eps_col", [128, 1], F32)
        plan = emit(Sched(), None)
        sc = Sched()
        sc.sems = SEMS
        emit(sc, plan)

        block = es.enter_context(nc.Block())

        @block.tensor
        def _(e):
            sc.replay("pe", e)

        @block.scalar
        def _(e):
            sc.replay("act", e)

        @block.vector
        def _(e):
            sc.replay("dve", e)

        @block.gpsimd
        def _(e):
            sc.replay("pool", e)

        @block.sync
        def _(e):
            sc.replay("sp", e, final_waits=[("outd", sc.cnt["outd"])])

    return nc


def _host_tables(na_rpb, wa_sinks):
    kc = np.arange(64)[:, None]
    qc = np.arange(64)[None, :]
    ws = np.clip(qc - 8, 0, 48)
    valid = (kc >= ws) & (kc < ws + 16)
    coff = np.clip(kc - qc, -15, 15) + 15
    rpb = np.asarray(na_rpb[0], dtype=np.float32)
    tb = np.empty((2, 64, 8, 15, 64), np.float32)
    for hh in range(2):
        for e in range(15):
            g = rpb[hh::2][:, 14 - e][:, coff]
            g = np.where(valid[None], g, np.float32(NEG))
            tb[hh, :, :, e, :] = g.transpose(1, 0, 2)
    tb = tb.reshape(128, 8, 15 * 64)
    sk = np.asarray(wa_sinks[0], np.float32)
    wa_sk = np.empty((128, 8), np.float32)
    for p in range(128):
        wa_sk[p] = sk[(p // 64)::2]
    kl = np.arange(64)[:, None]
    j = np.arange(320)[None, :]
    m = np.where((j >= kl) & (j <= kl + 256), np.float32(0), np.float32(NEG)).astype(np.float32)
    wa_mask = np.concatenate([m, m], 0)
    t = np.arange(S)
    rows = (t // GW).astype(np.float32)
    cols = (t % GW).astype(np.float32)
    inv = (np.float32(10000.0) ** (-np.arange(16, dtype=np.float32) / np.float32(16))).astype(np.float32)
    cs = np.empty((128, 2, S), np.float32)
    for p in range(128):
        d = p % 64
        pos = rows if d < 32 else cols
        ang = (pos * inv[d % 16]).astype(np.float32)
        cs[p, 0] = np.cos(ang)
        sn = np.sin(ang)
        cs[p, 1] = -sn if (d % 32) < 16 else sn
    perm = np.empty(64, np.int64)
    for d in range(64):
        perm[d] = d + 16 if (d % 32) < 16 else d - 16
    consts = np.zeros((128, 3, 128), np.float32)
    consts[:, 0, :] = np.eye(128, dtype=np.float32)
    consts[0:64, 1, 0:64] = 1.0
    consts[64:128, 1, 64:128] = 1.0
    consts[:, 2, :] = 1.0 / D
    return tb, wa_sk, wa_mask, cs, perm, consts


_CACHE = {}


def _prep_shared(c_ctx, w_mod, b_mod, ln_g, ln_b, ffn_w_in, ffn_w_out, na_w_qkv, na_w_o, na_rpb,
                 wa_w_qkv, wa_w_o, wa_sinks):
    f = lambda a: np.ascontiguousarray(np.asarray(a, dtype=np.float32))
    tb, wa_sk, wa_mask, cs, perm, consts = _host_tables(np.asarray(na_rpb), np.asarray(wa_sinks))
    wq = f(wa_w_qkv)[0]
    cols = np.concatenate([(h * 64 + perm) for h in range(NH)] + [D + g * 64 + perm for g in range(4)])
    sh = {
        "w_mod": f(w_mod),
        "bmod": f(np.asarray(b_mod).reshape(DEPTH, 72, 128).transpose(2, 0, 1)),
        "lng": f(np.asarray(ln_g).reshape(DEPTH, 3, KC, 128).transpose(3, 0, 1, 2)),
        "lnb": f(np.asarray(ln_b).reshape(DEPTH, 3, KC, 128).transpose(3, 0, 1, 2)),
        "ffn_w_in": f(ffn_w_in),
        "ffn_w_out": f(ffn_w_out),
        "na_w_qkv": f(na_w_qkv)[0],
        "na_w_o": f(na_w_o)[0],
        "na_tb": f(tb),
        "wa_w_qkv": wq,
        "wa_w_perm": f(wq[:, cols]),
        "wa_w_o": f(wa_w_o)[0],
        "wa_sk": f(wa_sk),
        "wa_mask": f(wa_mask),
        "rope_cs": f(cs),
        "consts": f(consts),
    }
    return sh


def run(inputs, stop=None):
    x = np.asarray(inputs["x"], np.float32)
    c = np.asarray(inputs["c"], np.float32)
    ctx = np.asarray(inputs["ctx"], np.float32)
    c_ctx = np.asarray(inputs["c_ctx"], np.float32)
    sh = _prep_shared(c_ctx, *[inputs[k] for k in ("w_mod", "b_mod", "ln_g", "ln_b", "ffn_w_in", "ffn_w_out",
                                                     "na_w_qkv", "na_w_o", "na_rpb", "wa_w_qkv", "wa_w_o", "wa_sinks")])
    key = ("nc", stop)
    if key not in _CACHE:
        _CACHE[key] = build_program(stop)
    nc = _CACHE[key]
    B = x.shape[0]
    in_maps = []
    for b in range(B):
        m = dict(sh)
        m["xT"] = np.ascontiguousarray(x[b].T)
        m["ctxT"] = np.ascontiguousarray(ctx[b].T)
        ccb = np.stack([c[b], c_ctx], -1).reshape(KC, 128, 2).transpose(1, 0, 2)
        m["cc"] = np.ascontiguousarray(ccb)
        in_maps.append(m)
    res = run_bass_kernel_spmd(nc, in_maps, core_ids=list(range(B)))
    outs = [np.asarray(r["outT"]) for r in res.results]
    return np.stack([o.T for o in outs], 0)


def kernel(**inputs):
    out = run(inputs, None)
    return np.ascontiguousarray(out.astype(np.float32))
```

```python
import numpy as np
from contextlib import ExitStack
import concourse.bass as bass
import concourse.mybir as mybir
from concourse.bass_utils import run_bass_kernel_spmd

F32 = mybir.dt.float32
BF16 = mybir.dt.bfloat16
AF = mybir.ActivationFunctionType
ALU = mybir.AluOpType

D = 1024
KC = 8
S = 2048
CT = 256
T = S + CT
FF = 2816
FC = 22
NH = 16
HD = 64
GW = 64
ROWS = 32
DEPTH = 2
ALPHA = (2 * DEPTH) ** 0.25
LN_EPS = 1e-5
NEG = -1e30
SLOT = 4096
NSLOT = 3
LOOKAHEAD = 2
PARTS = [(0, 4), (4, 4), (8, 4), (12, 4), (16, 3), (19, 3)]
MODW = 512
NMOD = 9 * D // MODW
TILES = [(0, 512), (512, 512), (1024, 512), (1536, 512), (2048, 256)]


class Res:
    __slots__ = ("w", "r")

    def __init__(self):
        self.w = None
        self.r = {}


class Sched:
    def __init__(self):
        self.ops = {k: [] for k in ("pe", "act", "dve", "pool", "sp")}
        self.cnt = {k: 0 for k in self.ops}
        self.seen = {k: {} for k in self.ops}
        self.sems = {}
        self.step = {k: 1 for k in self.ops}

    def add_dma_sem(self, name):
        self.cnt[name] = 0
        self.step[name] = 16

    def _deps(self, e, rd, wr):
        deps = {}

        def need(x):
            if x is None:
                return
            en, c = x
            if c > deps.get(en, 0):
                deps[en] = c

        for r in rd:
            need(r.w)
        for w in wr:
            need(w.w)
            for en, c in w.r.items():
                need((en, c))
        waits = []
        seen = self.seen[e]
        for en, c in deps.items():
            if en == e and e == "pe":
                continue
            if seen.get(en, 0) >= c:
                continue
            seen[en] = c
            waits.append((en, c))
        return waits

    def op(self, e, fn, rd=(), wr=()):
        waits = self._deps(e, rd, wr)
        self.cnt[e] += 1
        c = self.cnt[e]
        self.ops[e].append((waits, fn, e, c))
        for r in rd:
            if r.r.get(e, 0) < c:
                r.r[e] = c
        for w in wr:
            w.w = (e, c)
            w.r = {}

    def dma(self, q, fn, dsem, rd=(), wr=()):
        waits = self._deps(q, rd, wr)
        self.cnt[dsem] += 16
        c = self.cnt[dsem]
        self.ops[q].append((waits, fn, dsem, c))
        for r in rd:
            if r.r.get(dsem, 0) < c:
                r.r[dsem] = c
        for w in wr:
            w.w = (dsem, c)
            w.r = {}

    def replay(self, name, eng, final_waits=()):
        for waits, fn, sname, c in self.ops[name]:
            for en, v in waits:
                eng.wait_ge(self.sems[en], v)
            inst = fn(eng)
            inst.then_inc(self.sems[sname], self.step[sname])
        for en, v in final_waits:
            eng.wait_ge(self.sems[en], v)


def build_program(stop=None):
    nc = bass.Bass("TRN2", target_bir_lowering=False)

    def din(name, shape):
        return nc.dram_tensor(name, list(shape), F32, kind="ExternalInput").ap()

    xT = din("xT", [D, S])
    ctxT = din("ctxT", [D, CT])
    cc = din("cc", [128, KC, 2])
    w_mod = din("w_mod", [DEPTH, D, 9 * D])
    bmod = din("bmod", [128, DEPTH, 72])
    lng = din("lng", [128, DEPTH, 3, KC])
    lnb = din("lnb", [128, DEPTH, 3, KC])
    w_in = din("ffn_w_in", [DEPTH, 2, D, 2 * FF])
    w_out = din("ffn_w_out", [DEPTH, 2, FF, D])
    na_qkv = din("na_w_qkv", [D, 3 * D])
    na_o = din("na_w_o", [D, D])
    na_tb = din("na_tb", [128, 8, 15 * 64])
    wa_qkv = din("wa_w_qkv", [D, D + 512])
    wa_perm = din("wa_w_perm", [D, D + 256])
    wa_o = din("wa_w_o", [D, D])
    wa_sk = din("wa_sk", [128, 8])
    wa_mask = din("wa_mask", [128, 320])
    rope_cs = din("rope_cs", [128, 2, S])
    consts = din("consts", [128, 3, 128])
    n_out_tok = T if stop is not None else S
    outT = nc.dram_tensor("outT", [D, n_out_tok], F32, kind="ExternalOutput").ap()

    es = ExitStack()
    with es:
        def sb(name, shape, dt):
            return es.enter_context(nc.sbuf_tensor(name, list(shape), dt))

        hT = sb("hT", [128, KC, T], F32)
        uT = sb("uT", [128, KC, T], BF16)
        ring = sb("ring", [128, NSLOT, SLOT], BF16)
        ARENA = 4 * T + 2 * 1024
        ARENA_N = 25152
        arena = sb("arena", [128, ARENA_N], BF16)
        zb = sb("zb", [128, 2, 512], BF16)
        zq = sb("zq", [128, 2, 512], BF16)
        mean_sb2 = sb("mean_sb", [128, 2, 512], F32)
        tmpA2 = sb("tmpA", [128, 2, 512], F32)
        cst = sb("cst", [128, 3, 128], BF16)
        ccs = sb("ccs", [128, KC, 2], F32)
        scb = sb("scb", [128, KC, 2], BF16)
        bmod_sb = sb("bmod_sb", [128, DEPTH, 72], F32)
        raw = sb("raw", [128, DEPTH, 72, 2], F32)
        Acol = sb("Acol", [128, DEPTH, 3, KC, 2], F32)
        Gcol = sb("Gcol", [128, DEPTH, 3, KC, 2], F32)
        lng_sb = sb("lng_sb", [128, DEPTH, 3, KC], F32)
        lnb_sb = sb("lnb_sb", [128, DEPTH, 3, KC], F32)
        sk_sb = sb("sk_sb", [128, 8], F32)
        es_sb = sb("es_sb", [128, 8], F32)
        zero_col = sb("zero_col", [128, 1], F32)
        banks = [es.enter_context(nc.psum_tensor(f"bank{i}", [128, 512], F32)) for i in range(8)]

        ident = cst[:, 0, :]
        onesbd = cst[:, 1, :]
        onesd = cst[:, 2, :]

        def shaped(flat, shape):
            if len(shape) == 1:
                return flat
            names = " ".join(f"a{i}" for i in range(len(shape)))
            kw = {f"a{i}": int(s_) for i, s_ in enumerate(shape)}
            return flat.rearrange(f"p ({names}) -> p {names}", **kw)

        def aview(off, shape, dt=BF16):
            n = int(np.prod(shape))
            if dt == F32:
                assert off % 2 == 0
                flat = arena[:, off:off + 2 * n].bitcast(F32)
            else:
                flat = arena[:, off:off + n]
            return shaped(flat, shape)

        def ring_view(si, off, shape):
            n = int(np.prod(shape))
            return shaped(ring[:, si, off:off + n], shape)

        gT = aview(0, [4, T])
        sa = aview(4 * T, [2, 512], F32)
        o = 0
        qT = aview(o, [T]); o += T
        OT = aview(o, [2, T]); o += 2 * T
        PT = aview(o, [3, 512]); o += 3 * 512
        rsb = aview(o, [1, 512], F32); o += 1024
        t1 = aview(o, [1, 512], F32); o += 1024
        t2 = aview(o, [1, 512], F32); o += 1024
        ropeC = aview(o, [S]); o += S
        ropeS = aview(o, [S]); o += S
        maskw = aview(o, [320]); o += 320
        assert o >= ARENA
        KBD = aview(o, [36, 128]); o += 36 * 128
        VBD = aview(o, [36, 128]); o += 36 * 128
        assert o <= ARENA_N, o
        assert ARENA <= ARENA_N

        SEMS = {}
        for k in ("pe", "act", "dve", "pool", "sp"):
            SEMS[k] = es.enter_context(nc.semaphore(f"s_{k}"))
        dma_names = [f"slot{i}" for i in range(NSLOT)] + ["init_sp", "init_pool", "aux", "outd"] + [f"init_x{t}" for t in range(5)]
        for k in dma_names:
            SEMS[k] = es.enter_context(nc.semaphore(f"s_{k}"))

        def wrows(ap2d):
            return ap2d.rearrange("(c p) f -> p c f", p=128)

        def in_groups(part):
            j0, n = part
            gs = []
            j = j0
            while j < j0 + n:
                g = min(2, j0 + n - j)
                gs.append((j, g))
                j += g
            return gs

        def emit(sc, plan):
            record = plan is None
            rec = []
            for k in dma_names:
                sc.add_dma_sem(k)

            R_h = [[Res() for _ in TILES] for _ in range(KC)]
            R_u = [Res() for _ in TILES]
            R_g = [[Res() for _ in TILES] for _ in range(4)]
            R_slot = [Res() for _ in range(NSLOT)]
            R_bank = [Res() for _ in range(8)]
            R_sa = [Res(), Res()]
            R_zb = [Res(), Res()]
            R_zq = [Res(), Res()]
            R_mean2 = [Res(), Res()]
            R_tmpA2 = [Res(), Res()]
            R_cst = Res()
            R_small = Res()
            R_scb = Res()
            R_raw = [Res() for _ in range(DEPTH)]
            R_cols = [[Res() for _ in range(3)] for _ in range(DEPTH)]
            R_colA = [[Res() for _ in range(3)] for _ in range(DEPTH)]
            R_ln = Res()
            R_es = Res()
            R_KBD = [Res() for _ in range(5)]
            R_VBD = [Res() for _ in range(9)]
            R_q = [Res() for _ in TILES]
            R_OT = [[Res() for _ in TILES] for _ in range(2)]
            R_PT = [Res() for _ in range(3)]
            R_rsb = [Res()]
            R_t1 = [Res()]
            R_t2 = [Res()]
            R_rope = Res()
            R_rope2 = Res()
            R_mask = Res()

            def h_res(tiles, cs=range(KC)):
                return [R_h[c][t] for c in cs for t in tiles]

            def ffn_arena():
                return [r for row in R_g for r in row] + R_sa

            def att_res():
                return (R_q + R_OT[0] + R_OT[1] + R_PT + R_rsb + R_t1 + R_t2 + [R_rope, R_rope2, R_mask])

            wstate = {"issued": 0, "next": 0}

            def w_issue_upto(n):
                while wstate["issued"] < min(n, len(plan)):
                    i = wstate["issued"]
                    si = i % NSLOT
                    key, dmas_fn = plan[i]
                    for dst, src in dmas_fn(si):
                        sc.dma("pool", (lambda e, dst=dst, src=src: e.dma_start(out=dst, in_=src)),
                               f"slot{si}", wr=[R_slot[si]])
                    wstate["issued"] += 1

            def w_get(key, dmas_fn):
                i = wstate["next"]
                wstate["next"] += 1
                if record:
                    rec.append((key, dmas_fn))
                    return i % NSLOT
                assert plan[i][0] == key, (plan[i][0], key)
                w_issue_upto(i + 1 + LOOKAHEAD)
                return i % NSLOT

            def mod_dmas(l, i):
                src = wrows(w_mod[l][:, i * MODW:(i + 1) * MODW])
                return lambda si: [(ring_view(si, 0, [KC, MODW]), src)]

            def win_dmas(l, w, j, g):
                sa_ = wrows(w_in[l, w][:, j * 128:(j + g) * 128])
                sv_ = wrows(w_in[l, w][:, FF + j * 128:FF + (j + g) * 128])
                return lambda si: [(ring_view(si, 0, [KC, 2, g * 128])[:, :, 0, :], sa_),
                                   (ring_view(si, 0, [KC, 2, g * 128])[:, :, 1, :], sv_)]

            def wout_dmas(l, w, j0, n):
                so = w_out[l, w][j0 * 128:(j0 + n) * 128, :].rearrange("(j p) d -> p j d", p=128)
                return lambda si: [(ring_view(si, 0, [n, D]), so)]

            def qkv_dmas(l, pr):
                if l == 0:
                    def f(si):
                        dm = []
                        for i in range(3):
                            src = wrows(na_qkv[:, i * D + pr * 128:i * D + (pr + 1) * 128])
                            dm.append((ring_view(si, 0, [KC, 3, 128])[:, :, i, :], src))
                        dm.append((ring_view(si, 3072, [960]), na_tb[:, pr, :]))
                        return dm
                    return f
                def f(si):
                    rv = ring_view(si, 0, [KC, 256])
                    return [(rv[:, :, 0:128], wrows(wa_qkv[:, pr * 128:(pr + 1) * 128])),
                            (rv[:, :, 128:256], wrows(wa_perm[:, pr * 128:(pr + 1) * 128]))]
                return f

            def kv_dmas(pr):
                g = pr // 2

                def f(si):
                    rv = ring_view(si, 0, [KC, 384])
                    kq = wrows(wa_qkv[:, D + g * 64:D + (g + 1) * 64])
                    kp = wrows(wa_perm[:, D + g * 64:D + (g + 1) * 64])
                    vq = wrows(wa_qkv[:, D + 256 + g * 64:D + 256 + (g + 1) * 64])
                    return [(rv[:, :, 0:64], kq), (rv[:, :, 64:128], kq),
                            (rv[:, :, 128:192], kp), (rv[:, :, 192:256], kp),
                            (rv[:, :, 256:320], vq), (rv[:, :, 320:384], vq)]
                return f

            def wo_dmas(l, q):
                wo = na_o if l == 0 else wa_o
                src = wo[2 * q * 128:(2 * q + 2) * 128, :].rearrange("(j p) d -> p j d", p=128)
                return lambda si: [(ring_view(si, 0, [2, D]), src)]

            sc.dma("pool", lambda e: e.dma_start(out=cst[:], in_=consts), "init_pool", wr=[R_cst])
            sc.dma("sp", lambda e: e.dma_start(out=ccs[:], in_=cc), "init_sp", wr=[Res()])
            sc.dma("sp", lambda e: e.dma_start(out=bmod_sb[:], in_=bmod), "init_sp", wr=[Res()])
            sc.dma("sp", lambda e: e.dma_start(out=lng_sb[:], in_=lng), "init_sp", wr=[Res()])
            sc.dma("sp", lambda e: e.dma_start(out=lnb_sb[:], in_=lnb), "init_sp", wr=[Res()])
            sc.dma("sp", lambda e: e.dma_start(out=sk_sb[:], in_=wa_sk), "init_sp", wr=[Res()])
            R_small.w = ("init_sp", sc.cnt["init_sp"])
            for t in range(4):
                off_, n_ = TILES[t]
                sc.dma("sp", lambda e, off_=off_, n_=n_: e.dma_start(
                    out=hT[:, :, off_:off_ + n_], in_=xT.rearrange("(c p) t -> p c t", p=128)[:, :, off_:off_ + n_]),
                    f"init_x{t}", wr=[R_h[c][t] for c in range(KC)])
            sc.dma("sp", lambda e: e.dma_start(out=hT[:, :, S:T], in_=ctxT.rearrange("(c p) t -> p c t", p=128)),
                   "init_x4", wr=[R_h[c][4] for c in range(KC)])
            if not record:
                w_issue_upto(LOOKAHEAD)

            sc.op("act", lambda e: e.activation(out=scb[:], in_=ccs[:], func=AF.Silu), rd=[R_small], wr=[R_scb])
            sc.op("dve", lambda e: e.memset(zero_col[:], 0.0), wr=[R_es])
            sc.op("act", lambda e: e.activation(out=es_sb[:], in_=sk_sb[:], func=AF.Exp), rd=[R_small, R_es], wr=[R_es])
            for l in range(DEPTH):
                for i in range(3):
                    a = 1.0 if (l == DEPTH - 1 and i == 2) else ALPHA
                    sc.op("dve", lambda e, l=l, i=i, a=a: e.tensor_scalar(
                        out=lng_sb[:, l, i, :], in0=lng_sb[:, l, i, :], scalar1=a, scalar2=None, op0=ALU.mult),
                        rd=[R_small], wr=[R_ln])
                    sc.op("dve", lambda e, l=l, i=i, a=a: e.tensor_scalar(
                        out=lnb_sb[:, l, i, :], in0=lnb_sb[:, l, i, :], scalar1=a, scalar2=None, op0=ALU.mult),
                        rd=[R_small], wr=[R_ln])
            for t in range(5):
                off_, n_ = TILES[t]
                if t % 2 == 0:
                    sc.op("act", lambda e, off_=off_, n_=n_: e.activation(out=hT[:, :, off_:off_ + n_], in_=hT[:, :, off_:off_ + n_],
                                                                          func=AF.Copy, scale=ALPHA),
                          rd=[], wr=h_res([t]))
                else:
                    sc.op("dve", lambda e, off_=off_, n_=n_: e.tensor_scalar(out=hT[:, :, off_:off_ + n_], in0=hT[:, :, off_:off_ + n_],
                                                                             scalar1=ALPHA, scalar2=None, op0=ALU.mult),
                          rd=[], wr=h_res([t]))

            modq = [(l, i) for l in range(DEPTH) for i in range(NMOD)]

            def mod_step():
                if not modq:
                    return
                l, i = modq.pop(0)
                bk = 6
                si = w_get(("mod", l, i), mod_dmas(l, i))
                wv = ring_view(si, 0, [KC, MODW])
                NJ = MODW // 128

                def f(e):
                    last = None
                    for jj in range(NJ):
                        for kc in range(KC):
                            last = e.matmul(banks[bk][:, 2 * jj:2 * jj + 2], lhsT=wv[:, kc, jj * 128:(jj + 1) * 128],
                                            rhs=scb[:, kc, :], start=(kc == 0), stop=(kc == KC - 1))
                    return last
                sc.op("pe", f, rd=[R_slot[si], R_scb], wr=[R_bank[bk]])
                sc.op("dve", lambda e: e.tensor_tensor(
                    out=raw[:, l, NJ * i:NJ * i + NJ, :], in0=banks[bk][:, 0:2 * NJ].rearrange("p (j s) -> p j s", s=2),
                    in1=bmod_sb[:, l, NJ * i:NJ * i + NJ].unsqueeze(2).broadcast_to([128, NJ, 2]), op=ALU.add),
                    rd=[R_bank[bk], R_small], wr=[R_raw[l]])
                LPS = NMOD // 3
                ii = i // LPS
                n0 = 3 * ii
                if i % LPS == 3:
                    sc.op("dve", lambda e: e.tensor_scalar(
                        out=Acol[:, l, ii, :, :], in0=raw[:, l, (n0 + 1) * 8:(n0 + 2) * 8, :], scalar1=1.0,
                        scalar2=1.0 / ALPHA, op0=ALU.add, op1=ALU.mult), rd=[R_raw[l]], wr=[R_colA[l][ii]])
                if i % LPS == LPS - 1:
                    wres = 1.0 if ii == 1 else 0.5
                    sc.op("dve", lambda e: e.tensor_scalar(
                        out=Gcol[:, l, ii, :, :], in0=raw[:, l, (n0 + 2) * 8:(n0 + 3) * 8, :], scalar1=wres,
                        scalar2=None, op0=ALU.mult), rd=[R_raw[l]], wr=[R_cols[l][ii]])

            def mod_ensure(l, ii):
                while modq and (modq[0][0] < l or (modq[0][0] == l and modq[0][1] < (NMOD // 3) * (ii + 1))):
                    mod_step()

            pend = {}

            def advance(t):
                lst = pend.get(t)
                if lst:
                    lst.pop(0)()
                    if not lst:
                        del pend[t]

            def flush(t):
                while t in pend:
                    advance(t)

            def flush_all():
                for t in sorted(list(pend.keys())):
                    flush(t)

            def modulate_tile(l, i, t, pool_ok=True):
                n0 = 3 * i
                off, n = TILES[t]
                s_ = 1 if t == 4 else 0
                for c in range(KC):
                    if (c < 2 or (not pool_ok and c < 5)) and t != 4:
                        sc.op("dve", lambda e, c=c: e.tensor_scalar(
                            out=uT[:, c, off:off + n], in0=hT[:, c, off:off + n], scalar1=Acol[:, l, i, c, s_:s_ + 1],
                            scalar2=raw[:, l, n0 * 8 + c, s_:s_ + 1], op0=ALU.mult, op1=ALU.add),
                            rd=[R_h[c][t], R_colA[l][i], R_raw[l]], wr=[R_u[t]])
                    elif t == 4 or not pool_ok:
                        sc.op("act", lambda e, c=c: e.activation(
                            out=uT[:, c, off:off + n], in_=hT[:, c, off:off + n], func=AF.Identity,
                            scale=Acol[:, l, i, c, s_:s_ + 1], bias=raw[:, l, n0 * 8 + c, s_:s_ + 1]),
                            rd=[R_h[c][t], R_colA[l][i], R_raw[l]], wr=[R_u[t]])
                    else:
                        sc.op("pool", lambda e, c=c: e.tensor_scalar(
                            out=uT[:, c, off:off + n], in0=hT[:, c, off:off + n], scalar1=Acol[:, l, i, c, s_:s_ + 1],
                            scalar2=raw[:, l, n0 * 8 + c, s_:s_ + 1], op0=ALU.mult, op1=ALU.add),
                            rd=[R_h[c][t], R_colA[l][i], R_raw[l]], wr=[R_u[t]])

            def ln_stats(l, i, t, k):
                off, n = TILES[t]
                mean_sb = mean_sb2[:, k, :]
                tmpA = tmpA2[:, k, :]
                R_mean, R_tmpA = R_mean2[k], R_tmpA2[k]
                for c in range(KC):
                    b = c % 2
                    sc.op("act", lambda e, c=c, b=b: e.activation(out=zb[:, b, 0:n], in_=hT[:, c, off:off + n], func=AF.Copy),
                          rd=[R_h[c][t]], wr=[R_zb[b]])
                    sc.op("act", lambda e, c=c, b=b: e.activation(out=zq[:, b, 0:n], in_=hT[:, c, off:off + n], func=AF.Square),
                          rd=[R_h[c][t]], wr=[R_zq[b]])

                    def f(e, c=c, b=b):
                        e.matmul(banks[6][:, 0:n], lhsT=onesd, rhs=zb[:, b, 0:n], start=(c == 0), stop=(c == KC - 1))
                        return e.matmul(banks[7][:, 0:n], lhsT=onesd, rhs=zq[:, b, 0:n], start=(c == 0), stop=(c == KC - 1))
                    sc.op("pe", f, rd=[R_zb[b], R_zq[b], R_cst], wr=[R_bank[6], R_bank[7]])

            def ln_stats2(l, i, t, k):
                off, n = TILES[t]
                mean_sb = mean_sb2[:, k, :]
                tmpA = tmpA2[:, k, :]
                R_mean, R_tmpA = R_mean2[k], R_tmpA2[k]
                sc.op("act", lambda e: e.activation(out=mean_sb[:, 0:n], in_=banks[6][:, 0:n], func=AF.Copy),
                      rd=[R_bank[6]], wr=[R_mean])
                sc.op("dve", lambda e: e.tensor_tensor(out=tmpA[:, 0:n], in0=mean_sb[:, 0:n], in1=mean_sb[:, 0:n], op=ALU.mult),
                      rd=[R_mean], wr=[R_tmpA])
                sc.op("dve", lambda e: e.tensor_tensor(out=tmpA[:, 0:n], in0=banks[7][:, 0:n], in1=tmpA[:, 0:n], op=ALU.subtract),
                      rd=[R_bank[7], R_tmpA], wr=[R_tmpA])
                sc.op("act", lambda e: e.activation(out=tmpA[:, 0:n], in_=tmpA[:, 0:n], func=AF.Ln, bias=eps_col[:, 0:1]),
                      rd=[R_tmpA, R_es], wr=[R_tmpA])
                sc.op("act", lambda e: e.activation(out=tmpA[:, 0:n], in_=tmpA[:, 0:n], func=AF.Exp, scale=-0.5),
                      rd=[R_tmpA], wr=[R_tmpA])

            def ln_apply(l, i, t, k, nxt, late=False):
                off, n = TILES[t]
                mean_sb = mean_sb2[:, k, :]
                tmpA = tmpA2[:, k, :]
                R_mean, R_tmpA = R_mean2[k], R_tmpA2[k]
                for c in range(KC):
                    hv = hT[:, c, off:off + n]
                    sc.op("dve" if late else "pool",
                          lambda e, hv=hv: e.tensor_tensor(out=hv, in0=hv, in1=mean_sb[:, 0:n], op=ALU.subtract),
                          rd=[R_mean, R_h[c][t]], wr=[R_h[c][t]])
                    sc.op("dve", lambda e, hv=hv: e.tensor_tensor(out=hv, in0=hv, in1=tmpA[:, 0:n], op=ALU.mult),
                          rd=[R_tmpA, R_h[c][t]], wr=[R_h[c][t]])
                    sc.op("act", lambda e, hv=hv, c=c: e.activation(out=hv, in_=hv, func=AF.Identity,
                                                                    scale=lng_sb[:, l, i, c:c + 1], bias=lnb_sb[:, l, i, c:c + 1]),
                          rd=[R_ln, R_h[c][t]], wr=[R_h[c][t]])
                if nxt is not None and t in nxt[2]:
                    assert not (modq and (modq[0][0] < nxt[0] or (modq[0][0] == nxt[0] and modq[0][1] < (NMOD // 3) * (nxt[1] + 1))))
                    modulate_tile(nxt[0], nxt[1], t, pool_ok=not late)
                if nxt is None and stop is None:
                    sc.dma("sp", lambda e: e.dma_start(out=outT.rearrange("(c p) t -> p c t", p=128)[:, :, off:off + n],
                                                       in_=hT[:, :, off:off + n]),
                           "outd", rd=[R_h[c][t] for c in range(KC)])

            cnt = {"sa": 0, "av": 0, "y": 0}

            def z_update(l, i, c, t, bank):
                off, n = TILES[t]
                s_ = 1 if t == 4 else 0
                hv = hT[:, c, off:off + n]
                sc.op("dve", lambda e: e.scalar_tensor_tensor(out=hv, in0=banks[bank][:, 0:n], scalar=Gcol[:, l, i, c, s_:s_ + 1],
                                                              in1=hv, op0=ALU.mult, op1=ALU.add),
                      rd=[R_bank[bank], R_cols[l][i], R_h[c][t]], wr=[R_h[c][t]])

            def tail_pipeline(l, i, tiles, nxt, emit_outproj):
                tl = list(tiles)
                for k, t in enumerate(tl):
                    emit_outproj(t)
                    pend[t] = [(lambda t=t, k=k: ln_stats(l, i, t, k % 2)),
                               (lambda t=t, k=k: ln_stats2(l, i, t, k % 2)),
                               (lambda t=t, k=k: ln_apply(l, i, t, k % 2, nxt, late=(k == len(tl) - 1)))]
                    if k >= 1:
                        advance(tl[k - 1])
                    if k >= 2:
                        advance(tl[k - 2])
                    if k >= 1:
                        advance(tl[k - 1])
                advance(tl[-1])
                if len(tl) >= 2:
                    advance(tl[-2])
                advance(tl[-1])

            def ffn(l, w, tiles, nxt):
                i = 0 if w == 0 else 2
                tiles = list(tiles)
                first = True
                for pi, part in enumerate(PARTS):
                    j0, npart = part
                    for (j, g) in in_groups(part):
                        si = w_get(("win", l, w, j), win_dmas(l, w, j, g))
                        wv = ring_view(si, 0, [KC, 2, g * 128])
                        for jj in range(g):
                            jl = j + jj - j0
                            for t in tiles:
                                flush(t)
                                off, n = TILES[t]
                                k = cnt["av"]; cnt["av"] += 1
                                ba, bv = (k % 2), 2 + (k % 2)

                                def f(e, jj=jj, off=off, n=n, ba=ba, bv=bv, wv=wv):
                                    for kc in range(KC):
                                        e.matmul(banks[ba][:, 0:n], lhsT=wv[:, kc, 0, jj * 128:(jj + 1) * 128],
                                                 rhs=uT[:, kc, off:off + n], start=(kc == 0), stop=(kc == KC - 1))
                                    last = None
                                    for kc in range(KC):
                                        last = e.matmul(banks[bv][:, 0:n], lhsT=wv[:, kc, 1, jj * 128:(jj + 1) * 128],
                                                        rhs=uT[:, kc, off:off + n], start=(kc == 0), stop=(kc == KC - 1))
                                    return last
                                rdx = ffn_arena() if first else []
                                sc.op("pe", f, rd=[R_slot[si], R_u[t]], wr=[R_bank[ba], R_bank[bv]])
                                b = cnt["sa"] % 2; cnt["sa"] += 1
                                sc.op("act", lambda e, b=b, ba=ba, n=n: e.activation(out=sa[:, b, 0:n], in_=banks[ba][:, 0:n], func=AF.Silu),
                                      rd=[R_bank[ba]], wr=[R_sa[b]])
                                sc.op("dve", lambda e, b=b, bv=bv, n=n, jl=jl, off=off: e.tensor_tensor(
                                    out=gT[:, jl, off:off + n], in0=banks[bv][:, 0:n], in1=sa[:, b, 0:n], op=ALU.mult),
                                    rd=[R_bank[bv], R_sa[b]], wr=[R_g[jl][t]])
                                first = False
                        mod_step()
                    mod_ensure(l, i)
                    si = w_get(("wout", l, w, pi), wout_dmas(l, w, j0, npart))
                    wo = ring_view(si, 0, [npart, D])

                    last_part = (pi == len(PARTS) - 1)

                    def outproj(c, t, si=si, wo=wo, npart=npart, last_part=last_part):
                        off, n = TILES[t]
                        bank = (cnt["y"] % 6) if last_part else (4 + cnt["y"] % 4)
                        cnt["y"] += 1

                        def f(e):
                            last = None
                            for jl in range(npart):
                                last = e.matmul(banks[bank][:, 0:n], lhsT=wo[:, jl, c * 128:(c + 1) * 128],
                                                rhs=gT[:, jl, off:off + n], start=(jl == 0), stop=(jl == npart - 1))
                            return last
                        sc.op("pe", f, rd=[R_slot[si]] + [R_g[jl][t] for jl in range(npart)], wr=[R_bank[bank]])
                        z_update(l, i, c, t, bank)
                    if pi < len(PARTS) - 1:
                        for c in range(KC):
                            for t in tiles:
                                outproj(c, t)
                    else:
                        tail_pipeline(l, i, tiles, nxt, lambda t: [outproj(c, t) for c in range(KC)])
                    mod_step()

            def rs_na(r):
                return min(max(r - 4, 0), ROWS - 8)

            def na_items(qb):
                if qb == 4:
                    return [(32 + k, 0, 256, None) for k in range(4)]
                items = [(32 + k, 0, 512, None) for k in range(4)]
                for kr in range(ROWS):
                    rr = [r for r in range(8 * qb, 8 * qb + 8) if rs_na(r) <= kr <= rs_na(r) + 7]
                    if not rr:
                        continue
                    ra, rb = rr[0], rr[-1]
                    assert rr == list(range(ra, rb + 1))
                    e0 = ra - kr + 7
                    assert 0 <= e0 and e0 + (rb - ra) <= 14
                    items.append((kr, (ra - 8 * qb) * 64, (rb - 8 * qb + 1) * 64, ("tb", e0 * 64, (e0 + rb - ra + 1) * 64)))
                return items

            def wa_items(qb):
                items = [(32 + k, 0, 512, None) for k in range(4)]
                for kt in range(32):
                    lo = 64 * kt - 128
                    g0 = max(lo, 512 * qb, 0)
                    g1 = min(lo + 320, 512 * qb + 512, S)
                    if g1 <= g0:
                        continue
                    items.append((kt, g0 - 512 * qb, g1 - 512 * qb, ("mask", g0 - lo, g1 - lo)))
                return items

            def attention_pair(l, pr, pl, si, qbanks):
                exp_scale = 1.0 if l == 0 else 0.125
                flat = []
                for qi, qb in enumerate(qbanks):
                    its = na_items(qb) if l == 0 else wa_items(qb)
                    for ii, it in enumerate(its):
                        flat.append((qi, qb, ii, len(its), it))
                nI = len(flat)
                sbank = [None] * nI

                def emit_qk(x):
                    qi, qb, ii, nits, (kt, q0, q1, bias) = flat[x]
                    bk = x % 4
                    sbank[x] = bk
                    qoff = TILES[qb][0]
                    n = q1 - q0

                    extra = []
                    if bias is not None:
                        if bias[0] == "tb":
                            extra.append((0, n, ring_view(si, 3072, [960])[:, bias[1]:bias[2]]))
                        else:
                            for (lo, hi) in ((0, 63), (257, 320)):
                                a_, b_ = max(bias[1], lo), min(bias[2], hi)
                                if a_ < b_:
                                    extra.append((a_ - bias[1], b_ - bias[1], maskw[:, a_:b_]))

                    def f(e):
                        last = e.matmul(banks[bk][:, 0:n], lhsT=KBD[:, kt, :], rhs=qT[:, qoff + q0:qoff + q1],
                                        start=True, stop=(len(extra) == 0))
                        for xi, (c0, c1, bap) in enumerate(extra):
                            last = e.matmul(banks[bk][:, c0:c1], lhsT=ident, rhs=bap, start=False,
                                            stop=(xi == len(extra) - 1))
                        return last
                    rd = [R_KBD[kt // 8], R_q[qb], R_cst]
                    if bias is not None:
                        rd.append(R_slot[si] if bias[0] == "tb" else R_mask)
                    sc.op("pe", f, rd=rd, wr=[R_bank[bk]])

                LA = 3
                for x in range(min(LA, nI)):
                    emit_qk(x)
                pending = []
                for x in range(nI):
                    qi, qb, ii, nits, (kt, q0, q1, bias) = flat[x]
                    n = q1 - q0
                    bk = sbank[x]
                    pb = x % 3
                    while pending and pending[0][0] <= x:
                        pending.pop(0)[1]()
                    sc.op("act", lambda e, bk=bk, pb=pb, n=n: e.activation(out=PT[:, pb, 0:n], in_=banks[bk][:, 0:n],
                                                                           func=AF.Exp, scale=exp_scale),
                          rd=[R_bank[bk]], wr=[R_PT[pb]])
                    if x + LA < nI:
                        emit_qk(x + LA)
                    ob = 4 + 2 * (qi % 2)

                    def f(e, kt=kt, q0=q0, q1=q1, n=n, pb=pb, ob=ob, ii=ii, nits=nits):
                        e.matmul(banks[ob][:, q0:q1], lhsT=VBD[:, kt, :], rhs=PT[:, pb, 0:n],
                                 start=(ii == 0), stop=(ii == nits - 1))
                        return e.matmul(banks[ob + 1][:, q0:q1], lhsT=onesbd, rhs=PT[:, pb, 0:n],
                                        start=(ii == 0), stop=(ii == nits - 1))
                    sc.op("pe", f, rd=[R_VBD[kt // 4], R_PT[pb], R_cst], wr=[R_bank[ob], R_bank[ob + 1]])
                    if ii == nits - 1:
                        qoff, nq = TILES[qb]
                        escol = es_sb[:, pr:pr + 1] if l == 1 else zero_col[:, 0:1]

                        def norm(ob=ob, nq=nq, escol=escol, qoff=qoff, qb=qb):
                            sc.op("act", lambda e: e.activation(
                                out=rsb[:, 0, 0:nq], in_=banks[ob + 1][:, 0:nq], func=AF.Ln, bias=escol),
                                rd=[R_bank[ob + 1], R_es], wr=[R_rsb[0]])
                            sc.op("act", lambda e: e.activation(out=rsb[:, 0, 0:nq], in_=rsb[:, 0, 0:nq], func=AF.Exp, scale=-1.0),
                                  rd=[R_rsb[0]], wr=[R_rsb[0]])
                            sc.op("dve", lambda e: e.tensor_tensor(
                                out=OT[:, pl, qoff:qoff + nq], in0=banks[ob][:, 0:nq], in1=rsb[:, 0, 0:nq], op=ALU.mult),
                                rd=[R_bank[ob], R_rsb[0]], wr=[R_OT[pl][qb]])
                        pending.append((x + 3, norm))
                for _, fn in pending:
                    fn()

            def qkv_pair(l, pr, si, qtiles, part):
                pc = {"k": 0}

                def nb():
                    b = pc["k"] % 4
                    pc["k"] += 1
                    return b
                if l == 0:
                    wv = ring_view(si, 0, [KC, 3, 128])
                    wq = lambda kc: wv[:, kc, 0, :]
                    wk = lambda kc: wv[:, kc, 1, :]
                    wvv = lambda kc: wv[:, kc, 2, :]
                elif part == "q":
                    wv = ring_view(si, 0, [KC, 256])
                    wq = lambda kc: wv[:, kc, 0:128]
                    wqp = lambda kc: wv[:, kc, 128:256]
                else:
                    wv = ring_view(si, 0, [KC, 384])
                    wk = lambda kc: wv[:, kc, 0:128]
                    wkp = lambda kc: wv[:, kc, 128:256]
                    wvv = lambda kc: wv[:, kc, 256:384]

                def proj(bank, wf, off, n):
                    def f(e):
                        last = None
                        for kc in range(KC):
                            last = e.matmul(banks[bank][:, 0:n], lhsT=wf(kc), rhs=uT[:, kc, off:off + n],
                                            start=(kc == 0), stop=(kc == KC - 1))
                        return last
                    return f

                def rope(bq, bp, off, n, dst_fn):
                    sc.op("dve", lambda e: e.tensor_tensor(out=t1[:, 0, 0:n], in0=banks[bq][:, 0:n], in1=ropeC[:, off:off + n], op=ALU.mult),
                          rd=[R_bank[bq], R_rope], wr=[R_t1[0]])
                    sc.op("dve", lambda e: e.tensor_tensor(out=t2[:, 0, 0:n], in0=banks[bp][:, 0:n], in1=ropeS[:, off:off + n], op=ALU.mult),
                          rd=[R_bank[bp], R_rope2], wr=[R_t2[0]])
                    dst_fn()

                for t in (qtiles if part == "q" else []):
                    flush(t)
                    off, n = TILES[t]
                    b = nb()
                    sc.op("pe", proj(b, wq, off, n), rd=[R_slot[si], R_u[t]], wr=[R_bank[b]])
                    if l == 0:
                        sc.op("act", lambda e, b=b, off=off, n=n: e.activation(out=qT[:, off:off + n], in_=banks[b][:, 0:n],
                                                                               func=AF.Copy, scale=0.125),
                              rd=[R_bank[b]], wr=[R_q[t]])
                    else:
                        b2 = nb()
                        sc.op("pe", proj(b2, wqp, off, n), rd=[R_slot[si], R_u[t]], wr=[R_bank[b2]])

                        def dst(off=off, n=n, t=t):
                            sc.op("dve", lambda e: e.tensor_tensor(out=qT[:, off:off + n], in0=t1[:, 0, 0:n], in1=t2[:, 0, 0:n], op=ALU.add),
                                  rd=[R_t1[0], R_t2[0]], wr=[R_q[t]])
                        rope(b, b2, off, n, dst)
                if part != "kv":
                    return
                for t in range(5):
                    flush(t)
                    off, n = TILES[t]
                    nk = n // 64
                    kt0 = off // 64
                    b = nb()
                    sc.op("pe", proj(b, wk, off, n), rd=[R_slot[si], R_u[t]], wr=[R_bank[b]])
                    dA = KBD[0:64, kt0:kt0 + nk, 0:64]
                    dB = KBD[64:128, kt0:kt0 + nk, 64:128]
                    if l == 0 or t == 4:
                        sA = banks[b][0:64, 0:n].rearrange("p (k c) -> p k c", c=64)
                        sB = banks[b][64:128, 0:n].rearrange("p (k c) -> p k c", c=64)
                        sc.op("act", lambda e, dA=dA, sA=sA: e.activation(out=dA, in_=sA, func=AF.Copy),
                              rd=[R_bank[b]], wr=[R_KBD[t]])
                        sc.op("dve", lambda e, dB=dB, sB=sB: e.tensor_copy(out=dB, in_=sB), rd=[R_bank[b]], wr=[R_KBD[t]])
                    else:
                        b2 = nb()
                        sc.op("pe", proj(b2, wkp, off, n), rd=[R_slot[si], R_u[t]], wr=[R_bank[b2]])

                        def dst(dA=dA, dB=dB, n=n, t=t):
                            sc.op("dve", lambda e: e.tensor_tensor(
                                out=dA, in0=t1[0:64, 0, 0:n].rearrange("p (k c) -> p k c", c=64),
                                in1=t2[0:64, 0, 0:n].rearrange("p (k c) -> p k c", c=64), op=ALU.add),
                                rd=[R_t1[0], R_t2[0]], wr=[R_KBD[t]])
                            sc.op("dve", lambda e: e.tensor_tensor(
                                out=dB, in0=t1[64:128, 0, 0:n].rearrange("p (k c) -> p k c", c=64),
                                in1=t2[64:128, 0, 0:n].rearrange("p (k c) -> p k c", c=64), op=ALU.add),
                                rd=[R_t1[0], R_t2[0]], wr=[R_KBD[t]])
                        rope(b, b2, off, n, dst)
                for gi in range(9):
                    b = nb()

                    def f(e, gi=gi, b=b):
                        last = None
                        for k4 in range(4):
                            m = gi * 4 + k4
                            if m < 35:
                                for kc in range(KC):
                                    last = e.matmul(banks[b][:, k4 * 128:(k4 + 1) * 128], lhsT=uT[:, kc, 64 * m:64 * m + 128],
                                                    rhs=wvv(kc), start=(kc == 0), stop=(kc == KC - 1))
                            else:
                                for kc in range(KC):
                                    e.matmul(banks[b][0:64, k4 * 128:k4 * 128 + 64], lhsT=uT[:, kc, 64 * m:64 * m + 64],
                                             rhs=wvv(kc)[:, 0:64], start=(kc == 0), stop=(kc == KC - 1))
                                for kc in range(KC):
                                    last = e.matmul(banks[b][64:128, k4 * 128 + 64:k4 * 128 + 128], lhsT=uT[:, kc, 0:64],
                                                    rhs=wvv(kc)[:, 64:128], start=(kc == 0), stop=(kc == KC - 1))
                        return last
                    sc.op("pe", f, rd=[R_slot[si]] + R_u, wr=[R_bank[b]])
                    bv = banks[b][:, :].rearrange("p (k c) -> p k c", c=128)
                    sA = bv[0:64, :, 0:64]
                    dA = VBD[0:64, gi * 4:gi * 4 + 4, 0:64]
                    sc.op("act", lambda e, dA=dA, sA=sA: e.activation(out=dA, in_=sA, func=AF.Copy), rd=[R_bank[b]], wr=[R_VBD[gi]])
                    if gi < 8:
                        sB = bv[64:128, :, 64:128]
                        dB = VBD[64:128, gi * 4 + 1:gi * 4 + 5, 64:128]
                        sc.op("dve", lambda e, dB=dB, sB=sB: e.tensor_copy(out=dB, in_=sB), rd=[R_bank[b]],
                              wr=[R_VBD[gi], R_VBD[gi + 1]])
                    else:
                        sB = bv[64:128, 0:3, 64:128]
                        dB = VBD[64:128, 33:36, 64:128]
                        sc.op("dve", lambda e, dB=dB, sB=sB: e.tensor_copy(out=dB, in_=sB), rd=[R_bank[b]], wr=[R_VBD[8]])
                        sB2 = bv[64:128, 3, 64:128]
                        dB2 = VBD[64:128, 0, 64:128]
                        sc.op("dve", lambda e, dB2=dB2, sB2=sB2: e.tensor_copy(out=dB2, in_=sB2), rd=[R_bank[b]], wr=[R_VBD[0]])

            def mixer(l, ctx_out, nxt):
                i = 1
                mod_ensure(l, i)
                qtiles = list(range(5)) if ctx_out else list(range(4))
                if l == 1:
                    sc.op("dve", lambda e: e.memset(rsb[:, 0, 0:2], 0.0), rd=[], wr=[R_rope, R_rope2, R_mask] + R_rsb + ffn_arena())
                    sc.dma("pool", lambda e: e.dma_start(out=ropeC[:], in_=rope_cs[:, 0, :]), "aux", wr=[R_rope])
                    sc.dma("pool", lambda e: e.dma_start(out=ropeS[:], in_=rope_cs[:, 1, :]), "aux", wr=[R_rope2])
                    sc.dma("pool", lambda e: e.dma_start(out=maskw[:], in_=wa_mask), "aux", wr=[R_mask])
                for pr in range(8):
                    if l == 0:
                        si = w_get(("qkv", l, pr), qkv_dmas(l, pr))
                        qkv_pair(l, pr, si, qtiles, "kv")
                        qkv_pair(l, pr, si, qtiles, "q")
                    else:
                        if pr % 2 == 0:
                            skv = w_get(("kv", l, pr), kv_dmas(pr))
                            qkv_pair(l, pr, skv, qtiles, "kv")
                            mod_step()
                        si = w_get(("qkv", l, pr), qkv_dmas(l, pr))
                        qkv_pair(l, pr, si, qtiles, "q")
                    pl = pr % 2
                    attention_pair(l, pr, pl, si, qtiles)
                    mod_step()
                    if pl == 0:
                        continue
                    so = w_get(("wo", l, pr // 2), wo_dmas(l, pr // 2))
                    wo = ring_view(so, 0, [2, D])
                    lastp = (pr == 7)

                    def outproj(c, t, so=so, wo=wo, lastp=lastp):
                        off, n = TILES[t]
                        bank = (cnt["y"] % 6) if lastp else (4 + cnt["y"] % 4)
                        cnt["y"] += 1

                        def f(e):
                            e.matmul(banks[bank][:, 0:n], lhsT=wo[:, 0, c * 128:(c + 1) * 128], rhs=OT[:, 0, off:off + n],
                                     start=True, stop=False)
                            return e.matmul(banks[bank][:, 0:n], lhsT=wo[:, 1, c * 128:(c + 1) * 128], rhs=OT[:, 1, off:off + n],
                                            start=False, stop=True)
                        sc.op("pe", f, rd=[R_slot[so], R_OT[0][t], R_OT[1][t]], wr=[R_bank[bank]])
                        z_update(l, i, c, t, bank)
                    if not lastp:
                        for c in range(KC):
                            for t in qtiles:
                                outproj(c, t)
                    else:
                        tail_pipeline(l, i, qtiles, nxt, lambda t: [outproj(c, t) for c in range(KC)])
                    mod_step()

            def arena_to_ffn():
                sc.op("dve", lambda e: e.memset(sa[:, 0, 0:2], 0.0), rd=[], wr=att_res() + ffn_arena())

            sc.op("dve", lambda e: e.memset(eps_col[:], LN_EPS), wr=[R_es])
            sc.op("dve", lambda e: e.memset(KBD[:, :, :], 0.0), rd=[], wr=R_KBD)
            sc.op("dve", lambda e: e.memset(VBD[:, :, :], 0.0), rd=[], wr=R_VBD)
            for _ in range(4):
                mod_step()
            for t in range(5):
                modulate_tile(0, 0, t)
            seq = [("ffn", 0, 0), ("mix", 0), ("ffn", 0, 1), ("ffn", 1, 0), ("mix", 1), ("ffn", 1, 1)]
            stops = {"ffn00": 1, "mix0": 2, "ffn01": 3, "ffn10": 4, "mix1": 5, None: 6}
            seq = seq[:stops[stop]]
            for k, sub in enumerate(seq):
                nxt = None
                if k + 1 < len(seq):
                    ns = seq[k + 1]
                    if ns[0] == "mix":
                        nxt = (ns[1], 1, list(range(5)))
                    else:
                        ni = 0 if ns[2] == 0 else 2
                        ntl = list(range(4)) if (ns[1] == 1 and ns[2] == 1) else list(range(5))
                        nxt = (ns[1], ni, ntl)
                if sub[0] == "ffn":
                    if k > 0 and seq[k - 1][0] == "mix":
                        arena_to_ffn()
                    tl = range(4) if (sub[1] == 1 and sub[2] == 1) else range(5)
                    ffn(sub[1], sub[2], tl, nxt)
                else:
                    mixer(sub[1], sub[1] == 0, nxt)
            flush_all()

            ntile_out = 5 if stop is not None else 4
            for c in (range(KC) if stop is not None else []):
                sc.dma("sp", lambda e, c=c: e.dma_start(out=outT[c * 128:(c + 1) * 128, :], in_=hT[:, c, 0:n_out_tok]),
                       "outd", rd=[R_h[c][t] for t in range(ntile_out)])

            for coarse in ("init_sp", "aux"):
                tot = sc.cnt[coarse]
                for k in sc.ops:
                    sc.ops[k] = [([(en, (tot if en == coarse else v)) for en, v in waits], fn, sn, c)
                                 for waits, fn, sn, c in sc.ops[k]]
            return rec

        eps_col = sb("eps_col", [128, 1], F32)
        plan = emit(Sched(), None)
        sc = Sched()
        sc.sems = SEMS
        emit(sc, plan)

        block = es.enter_context(nc.Block())

        @block.tensor
        def _(e):
            sc.replay("pe", e)

        @block.scalar
        def _(e):
            sc.replay("act", e)

        @block.vector
        def _(e):
            sc.replay("dve", e)

        @block.gpsimd
        def _(e):
            sc.replay("pool", e)

        @block.sync
        def _(e):
            sc.replay("sp", e, final_waits=[("outd", sc.cnt["outd"])])

    return nc


def _host_tables(na_rpb, wa_sinks):
    kc = np.arange(64)[:, None]
    qc = np.arange(64)[None, :]
    ws = np.clip(qc - 8, 0, 48)
    valid = (kc >= ws) & (kc < ws + 16)
    coff = np.clip(kc - qc, -15, 15) + 15
    rpb = np.asarray(na_rpb[0], dtype=np.float32)
    tb = np.empty((2, 64, 8, 15, 64), np.float32)
    for hh in range(2):
        for e in range(15):
            g = rpb[hh::2][:, 14 - e][:, coff]
            g = np.where(valid[None], g, np.float32(NEG))
            tb[hh, :, :, e, :] = g.transpose(1, 0, 2)
    tb = tb.reshape(128, 8, 15 * 64)
    sk = np.asarray(wa_sinks[0], np.float32)
    wa_sk = np.empty((128, 8), np.float32)
    for p in range(128):
        wa_sk[p] = sk[(p // 64)::2]
    kl = np.arange(64)[:, None]
    j = np.arange(320)[None, :]
    m = np.where((j >= kl) & (j <= kl + 256), np.float32(0), np.float32(NEG)).astype(np.float32)
    wa_mask = np.concatenate([m, m], 0)
    t = np.arange(S)
    rows = (t // GW).astype(np.float32)
    cols = (t % GW).astype(np.float32)
    inv = (np.float32(10000.0) ** (-np.arange(16, dtype=np.float32) / np.float32(16))).astype(np.float32)
    cs = np.empty((128, 2, S), np.float32)
    for p in range(128):
        d = p % 64
        pos = rows if d < 32 else cols
        ang = (pos * inv[d % 16]).astype(np.float32)
        cs[p, 0] = np.cos(ang)
        sn = np.sin(ang)
        cs[p, 1] = -sn if (d % 32) < 16 else sn
    perm = np.empty(64, np.int64)
    for d in range(64):
        perm[d] = d + 16 if (d % 32) < 16 else d - 16
    consts = np.zeros((128, 3, 128), np.float32)
    consts[:, 0, :] = np.eye(128, dtype=np.float32)
    consts[0:64, 1, 0:64] = 1.0
    consts[64:128, 1, 64:128] = 1.0
    consts[:, 2, :] = 1.0 / D
    return tb, wa_sk, wa_mask, cs, perm, consts


_CACHE = {}


def _prep_shared(c_ctx, w_mod, b_mod, ln_g, ln_b, ffn_w_in, ffn_w_out, na_w_qkv, na_w_o, na_rpb,
                 wa_w_qkv, wa_w_o, wa_sinks):
    f = lambda a: np.ascontiguousarray(np.asarray(a, dtype=np.float32))
    tb, wa_sk, wa_mask, cs, perm, consts = _host_tables(np.asarray(na_rpb), np.asarray(wa_sinks))
    wq = f(wa_w_qkv)[0]
    cols = np.concatenate([(h * 64 + perm) for h in range(NH)] + [D + g * 64 + perm for g in range(4)])
    sh = {
        "w_mod": f(w_mod),
        "bmod": f(np.asarray(b_mod).reshape(DEPTH, 72, 128).transpose(2, 0, 1)),
        "lng": f(np.asarray(ln_g).reshape(DEPTH, 3, KC, 128).transpose(3, 0, 1, 2)),
        "lnb": f(np.asarray(ln_b).reshape(DEPTH, 3, KC, 128).transpose(3, 0, 1, 2)),
        "ffn_w_in": f(ffn_w_in),
        "ffn_w_out": f(ffn_w_out),
        "na_w_qkv": f(na_w_qkv)[0],
        "na_w_o": f(na_w_o)[0],
        "na_tb": f(tb),
        "wa_w_qkv": wq,
        "wa_w_perm": f(wq[:, cols]),
        "wa_w_o": f(wa_w_o)[0],
        "wa_sk": f(wa_sk),
        "wa_mask": f(wa_mask),
        "rope_cs": f(cs),
        "consts": f(consts),
    }
    return sh


def run(inputs, stop=None):
    x = np.asarray(inputs["x"], np.float32)
    c = np.asarray(inputs["c"], np.float32)
    ctx = np.asarray(inputs["ctx"], np.float32)
    c_ctx = np.asarray(inputs["c_ctx"], np.float32)
    sh = _prep_shared(c_ctx, *[inputs[k] for k in ("w_mod", "b_mod", "ln_g", "ln_b", "ffn_w_in", "ffn_w_out",
                                                     "na_w_qkv", "na_w_o", "na_rpb", "wa_w_qkv", "wa_w_o", "wa_sinks")])
    key = ("nc", stop)
    if key not in _CACHE:
        _CACHE[key] = build_program(stop)
    nc = _CACHE[key]
    B = x.shape[0]
    in_maps = []
    for b in range(B):
        m = dict(sh)
        m["xT"] = np.ascontiguousarray(x[b].T)
        m["ctxT"] = np.ascontiguousarray(ctx[b].T)
        ccb = np.stack([c[b], c_ctx], -1).reshape(KC, 128, 2).transpose(1, 0, 2)
        m["cc"] = np.ascontiguousarray(ccb)
        in_maps.append(m)
    res = run_bass_kernel_spmd(nc, in_maps, core_ids=list(range(B)))
    outs = [np.asarray(r["outT"]) for r in res.results]
    return np.stack([o.T for o in outs], 0)


def kernel(**inputs):
    out = run(inputs, None)
    return np.ascontiguousarray(out.astype(np.float32))
```
